# Optimizing a Trainium2 kernel written in Bass

```python
import math
import jax
import jax.numpy as jnp
from jax import lax
import numpy as np

D_MODEL = 1024
BATCH = 4
SEQ = 4096
DEPTH = 1
DEC_BATCH = 16
DEC_SEQ = 2048
PAST_LEN = 128

DA_HEADS = 4
DA_HEAD_DIM = 64
DA_V_DIM = 2 * DA_HEAD_DIM
DA_WIDTH = DA_HEADS * DA_V_DIM
Q_BLOCK = 128
T5_BUCKETS = 32
T5_MAX_DIST = 128
GRID_W = 64
NA_HEADS = 8
NA_HEAD_DIM = 64
NA_WIDTH = NA_HEADS * NA_HEAD_DIM
NA_ROWS_MAX = 8
NA_COLS = 16
SPLIT_SIZES = (
    DA_HEADS * 2 * DA_HEAD_DIM,
    DA_HEADS * 2 * DA_HEAD_DIM,
    DA_WIDTH,
    DA_WIDTH,
    NA_WIDTH,
    NA_WIDTH,
    NA_WIDTH,
    NA_WIDTH,
    2 * D_MODEL,
)
IN_WIDTH = 4 * DA_WIDTH + 4 * NA_WIDTH + 2 * D_MODEL
NORM_EPS = 1e-6
SUBLN_EPS = 1e-5

kernel_name = "gated_diffattn_natten_encoder"


def rms_norm(x, w, eps=NORM_EPS):
    xf = x.astype(jnp.float32)
    y = xf * lax.rsqrt(jnp.mean(xf * xf, axis=-1, keepdims=True) + eps)
    return (y * w.astype(jnp.float32)).astype(x.dtype)


def t5_bucket(rel):
    nb = T5_BUCKETS // 2
    ret = jnp.where(rel > 0, nb, 0)
    n = jnp.abs(rel)
    max_exact = nb // 2
    nf = jnp.maximum(n, 1).astype(jnp.float32)
    large = max_exact + (jnp.log(nf / max_exact) / math.log(T5_MAX_DIST / max_exact)
                         * (nb - max_exact)).astype(jnp.int32)
    large = jnp.minimum(large, nb - 1)
    return ret + jnp.where(n < max_exact, n, large)


def diff_attention(q, k, v, t5_rel_bias, lam, subln_w, lam_init):
    B, S = q.shape[0], q.shape[1]
    nblk = S // Q_BLOCK
    qs = q * (DA_HEAD_DIM ** -0.5)
    qb = qs.reshape(B, nblk, Q_BLOCK, DA_HEADS, 2, DA_HEAD_DIM).transpose(1, 0, 2, 3, 4, 5)
    kpos = jnp.arange(S, dtype=jnp.int32)

    def block(args):
        qi, bi = args
        qpos = bi * Q_BLOCK + jnp.arange(Q_BLOCK, dtype=jnp.int32)
        bucket = t5_bucket(kpos[None, :] - qpos[:, None])
        bias = jnp.moveaxis(t5_rel_bias[bucket], -1, 0).astype(jnp.float32)
        s = jnp.einsum('bqhmd,bkhmd->bmhqk', qi, k).astype(jnp.float32) + bias[None, None]
        p = jax.nn.softmax(s, axis=-1)
        attn = p[:, 0] - lam * p[:, 1]
        return jnp.einsum('bhqk,bkhe->bqhe', attn.astype(v.dtype), v)

    out = lax.map(block, (qb, jnp.arange(nblk, dtype=jnp.int32)))
    out = out.transpose(1, 0, 2, 3, 4).reshape(B, S, DA_HEADS, DA_V_DIM)
    out = rms_norm(out, subln_w, SUBLN_EPS) * (1.0 - lam_init)
    return out.reshape(B, S, DA_WIDTH)


def neighbourhood_attention(q, k, v, na_rpb):
    B, S = q.shape[0], q.shape[1]
    rows = S // GRID_W
    kh = min(NA_ROWS_MAX, rows)
    qg = (q * (NA_HEAD_DIM ** -0.5)).reshape(B, rows, GRID_W, NA_HEADS, NA_HEAD_DIM)
    qg = qg.transpose(1, 0, 2, 3, 4)
    kg = k.reshape(B, rows, GRID_W, NA_HEADS, NA_HEAD_DIM)
    vg = v.reshape(B, rows, GRID_W, NA_HEADS, NA_HEAD_DIM)
    cols = jnp.arange(GRID_W, dtype=jnp.int32)
    col_start = jnp.clip(cols - NA_COLS // 2, 0, GRID_W - NA_COLS)
    col_idx = col_start[:, None] + jnp.arange(NA_COLS, dtype=jnp.int32)[None, :]
    dc = col_idx - cols[:, None] + (NA_COLS - 1)

    def row(args):
        qr, r = args
        start = jnp.clip(r - kh // 2, 0, rows - kh)
        kb = lax.dynamic_slice_in_dim(kg, start, kh, axis=1)[:, :, col_idx]
        vb = lax.dynamic_slice_in_dim(vg, start, kh, axis=1)[:, :, col_idx]
        dr = start + jnp.arange(kh, dtype=jnp.int32) - r + (NA_ROWS_MAX - 1)
        bias = na_rpb[:, dr[None, :, None], dc[:, None, :]].astype(jnp.float32)
        s = jnp.einsum('bchd,bkcjhd->bhckj', qr, kb).astype(jnp.float32) + bias[None]
        p = jax.nn.softmax(s.reshape(B, NA_HEADS, GRID_W, kh * NA_COLS), axis=-1).reshape(s.shape)
        return jnp.einsum('bhckj,bkcjhd->bchd', p.astype(vb.dtype), vb)

    out = lax.map(row, (qg, jnp.arange(rows, dtype=jnp.int32)))
    return out.transpose(1, 0, 2, 3, 4).reshape(B, S, NA_WIDTH)


def encoder_layer(x, t5_rel_bias, pre_w, post_w, w_in, lq1, lk1, lq2, lk2, subln_w,
                  na_rpb, w_o_diff, w_o_na, w_out, lam_init):
    B, S, D = x.shape
    h = rms_norm(x, pre_w)
    proj = h @ w_in
    split_pts = np.cumsum(np.array(SPLIT_SIZES))[:-1]
    qa, ka, va, za, qn, kn, vn, zn, g = jnp.split(proj, split_pts, axis=-1)
    lam = (jnp.exp(jnp.sum(lq1.astype(jnp.float32) * lk1.astype(jnp.float32)))
           - jnp.exp(jnp.sum(lq2.astype(jnp.float32) * lk2.astype(jnp.float32))) + lam_init)
    oa = diff_attention(qa.reshape(B, S, DA_HEADS, 2, DA_HEAD_DIM),
                        ka.reshape(B, S, DA_HEADS, 2, DA_HEAD_DIM),
                        va.reshape(B, S, DA_HEADS, DA_V_DIM),
                        t5_rel_bias, lam, subln_w, lam_init)
    ya = (oa * jax.nn.silu(za)) @ w_o_diff
    on = neighbourhood_attention(qn.reshape(B, S, NA_HEADS, NA_HEAD_DIM),
                                 kn.reshape(B, S, NA_HEADS, NA_HEAD_DIM),
                                 vn.reshape(B, S, NA_HEADS, NA_HEAD_DIM), na_rpb)
    yn = (on * jax.nn.silu(zn)) @ w_o_na
    gates = jax.nn.sigmoid(g.astype(jnp.float32)).astype(x.dtype).reshape(B, S, 2, D)
    merged = gates[:, :, 0] * ya + gates[:, :, 1] * yn
    out = merged @ w_out
    return x + rms_norm(out, post_w)


def encoder_trunk(x, t5_rel_bias, pre_norm_w, post_norm_w, w_in, lambda_q1, lambda_k1,
                  lambda_q2, lambda_k2, subln_w, na_rpb, w_o_diff, w_o_na, w_out):
    for l in range(DEPTH):
        lam_init = 0.8 - 0.6 * math.exp(-0.3 * l)
        x = encoder_layer(x, t5_rel_bias, pre_norm_w[l], post_norm_w[l], w_in[l],
                          lambda_q1[l], lambda_k1[l], lambda_q2[l], lambda_k2[l], subln_w[l],
                          na_rpb[l], w_o_diff[l], w_o_na[l], w_out[l], lam_init)
    return x


def setup_inputs(seed: int = 0) -> dict:
    key = jax.random.key(seed)
    ks = jax.random.split(key, 15)

    def nrm(k, shape, scale):
        return jax.random.normal(k, shape, jnp.float32) * scale

    return {
        "x_prompt": nrm(ks[0], (BATCH, SEQ, D_MODEL), 1.0),
        "x_sample": nrm(ks[1], (DEC_BATCH, DEC_SEQ, D_MODEL), 1.0),
        "t5_rel_bias": nrm(ks[2], (T5_BUCKETS, DA_HEADS), 0.5),
        "pre_norm_w": 1.0 + nrm(ks[3], (DEPTH, D_MODEL), 0.05),
        "post_norm_w": 1.0 + nrm(ks[4], (DEPTH, D_MODEL), 0.05),
        "w_in": nrm(ks[5], (DEPTH, D_MODEL, IN_WIDTH), D_MODEL ** -0.5),
        "lambda_q1": nrm(ks[6], (DEPTH, DA_HEAD_DIM), 0.1),
        "lambda_k1": nrm(ks[7], (DEPTH, DA_HEAD_DIM), 0.1),
        "lambda_q2": nrm(ks[8], (DEPTH, DA_HEAD_DIM), 0.1),
        "lambda_k2": nrm(ks[9], (DEPTH, DA_HEAD_DIM), 0.1),
        "subln_w": 1.0 + nrm(ks[10], (DEPTH, DA_V_DIM), 0.05),
        "na_rpb": nrm(ks[11], (DEPTH, NA_HEADS, 2 * NA_ROWS_MAX - 1, 2 * NA_COLS - 1), 0.5),
        "w_o_diff": nrm(ks[12], (DEPTH, DA_WIDTH, D_MODEL), DA_WIDTH ** -0.5),
        "w_o_na": nrm(ks[13], (DEPTH, NA_WIDTH, D_MODEL), NA_WIDTH ** -0.5),
        "w_out": nrm(ks[14], (DEPTH, D_MODEL, D_MODEL), D_MODEL ** -0.5),
    }


def reference(x_prompt, x_sample, t5_rel_bias, pre_norm_w, post_norm_w, w_in, lambda_q1,
              lambda_k1, lambda_q2, lambda_k2, subln_w, na_rpb, w_o_diff, w_o_na, w_out):
    y_prompt = encoder_trunk(x_prompt, t5_rel_bias, pre_norm_w, post_norm_w, w_in, lambda_q1,
                             lambda_k1, lambda_q2, lambda_k2, subln_w, na_rpb, w_o_diff,
                             w_o_na, w_out)
    y_sample = encoder_trunk(x_sample, t5_rel_bias, pre_norm_w, post_norm_w, w_in, lambda_q1,
                             lambda_k1, lambda_q2, lambda_k2, subln_w, na_rpb, w_o_diff,
                             w_o_na, w_out)
    return (y_prompt, y_sample)
```

```python
import math
from contextlib import ExitStack

import numpy as np
import concourse.bass as bass
import concourse.mybir as mybir
from concourse.bass_utils import run_bass_kernel_spmd

F32, BF16 = mybir.dt.float32, mybir.dt.bfloat16
AF = mybir.ActivationFunctionType
ALU = mybir.AluOpType
AX = mybir.AxisListType
NEG = -30000.0
T = 2048
NCORES = 8
STQ = "sp"


class Ev:
    __slots__ = ("s", "v")

    def __init__(self, s, v):
        self.s, self.v = s, v


class Tracker:
    def __init__(self, nc, es):
        self.nc = nc
        self.es = es
        self.eng = {"pe": nc.tensor, "act": nc.scalar, "dve": nc.vector, "pool": nc.gpsimd, "sp": nc.sync}
        self.sem = {e: es.enter_context(nc.semaphore("s_" + e)) for e in ("pe", "act", "dve", "pool")}
        self.cnt = {e: 0 for e in self.sem}
        self.pending = {e: [] for e in self.sem}
        self.dsem, self.dcnt = {}, {}
        self.lastw, self.readers = {}, {}
        self.known = {e: {} for e in self.eng}

    def _semobj(self, s):
        return self.sem[s] if s in self.sem else self.dsem[s]

    def _deps(self, e, r, w):
        deps = {}

        def add(ev):
            if ev is None:
                return
            if ev.s == "pe" and e == "pe":
                return
            assert ev.v is not None, "dependency on unsignalled instruction"
            if deps.get(ev.s, 0) < ev.v:
                deps[ev.s] = ev.v

        for k in r:
            add(self.lastw.get(k))
        for k in w:
            add(self.lastw.get(k))
            for ev in self.readers.get(k, {}).values():
                add(ev)
        for s, v in deps.items():
            if self.known[e].get(s, 0) >= v:
                continue
            self.eng[e].wait_ge(self._semobj(s), v)
            self.known[e][s] = v

    def _record(self, ev, r, w):
        for k in r:
            self.readers.setdefault(k, {})[ev.s] = ev
        for k in w:
            self.lastw[k] = ev
            self.readers[k] = {}

    def op(self, e, fn, r=(), w=(), sig=True):
        self._deps(e, r, w)
        ins = fn()
        ev = Ev(e, None)
        if sig:
            self.cnt[e] += 1
            ins.then_inc(self.sem[e], 1)
            ev.v = self.cnt[e]
            for p in self.pending[e]:
                p.v = ev.v
            self.pending[e] = []
        else:
            self.pending[e].append(ev)
        self._record(ev, r, w)
        return ins

    def dma(self, q, out, in_, r=(), w=(), *, stream):
        if stream not in self.dsem:
            self.dsem[stream] = self.es.enter_context(self.nc.semaphore("d_" + stream))
            self.dcnt[stream] = 0
        self._deps(q, r, w)
        ins = self.eng[q].dma_start(out=out, in_=in_)
        self.dcnt[stream] += 16
        ins.then_inc(self.dsem[stream], 16)
        self._record(Ev(stream, self.dcnt[stream]), r, w)

    def barrier_on(self, stream, engines=("pe", "act", "dve", "pool", "sp")):
        for e in engines:
            self.eng[e].wait_ge(self.dsem[stream], self.dcnt[stream])
            self.known[e][stream] = self.dcnt[stream]

    def finish(self, streams):
        for s in streams:
            self.nc.sync.wait_ge(self.dsem[s], self.dcnt[s])


def na_tiles(halo):
    tl = []
    if halo:
        tl += [(18, -4), (19, -2)]
    tl += [(t, 2 * t) for t in range(16)]
    if halo:
        tl += [(16, 32), (17, 34)]
    return tl


def na_candidates(halo):
    def windows(r):
        ws = []
        if halo:
            s0 = min(max(r - 4, 0), 56)
            ws.append((s0, s0 + 7))
            s1 = min(max(r + 32 - 4, 0), 56) - 32
            ws.append((s1, s1 + 7))
        else:
            s0 = min(max(r - 4, 0), 24)
            ws.append((s0, s0 + 7))
        return ws

    out = {}
    for t, a in na_tiles(halo):
        rs = [r for r in range(32) if any(lo <= a + j <= hi for (lo, hi) in windows(r) for j in (0, 1))]
        r_lo, r_hi = min(rs), max(rs)
        r_lo -= r_lo % 2
        r_hi += 1 - (r_hi % 2)
        assert 0 <= r_lo - a + 7 and r_hi - a + 7 <= 15, (t, a, r_lo, r_hi)
        out[t] = (r_lo, r_hi)
    return out


def t5_bucket_np(rel):
    nb = 16
    ret = np.where(rel > 0, nb, 0)
    n = np.abs(rel)
    me = 8
    nf = np.maximum(n, 1).astype(np.float32)
    large = me + (np.log(nf / np.float32(me)) / np.float32(math.log(128 / me)) * np.float32(nb - me)).astype(np.int32)
    large = np.minimum(large, nb - 1)
    return ret + np.where(n < me, n, large)


def build_program(stage=99, dbg=None):
    nc = bass.Bass("TRN2", target_bir_lowering=False)

    def di(n, s, dt=F32):
        return nc.dram_tensor(n, s, dt, kind="ExternalInput").ap()

    x_all = di("x_all", [8192, 1024])
    w_in = di("w_in", [1024, 6144])
    w_od = di("w_od", [512, 1024])
    w_on = di("w_on", [512, 1024])
    w_out = di("w_out", [1024, 1024])
    prew_d = di("prew", [128, 8])
    postw_d = di("postw", [1, 1024])
    subw_d = di("subw", [128, 1])
    lamv_d = di("lamv", [1, 256])
    ident_d = di("ident", [128, 128])
    e2_d = di("e2", [128, 128])
    t5own_d = di("t5own", [4, 128, 1152])
    t5spec_d = di("t5spec", [2, 4, 128, 512])
    t5c_d = di("t5c", [1, 16])
    natab_d = di("natab", [8, 128, 1024])
    natabi_d = di("natabi", [8, 128, 1024])
    m2p_d = di("m2p", [128, 640])
    m2s_d = di("m2s", [128, 512])
    y_all = nc.dram_tensor("y_all", [6144, 1024], F32, kind="ExternalOutput").ap()
    wkind = "ExternalOutput" if dbg is not None else "Internal"
    ws_in = nc.dram_tensor("ws_in", [48, 128, 8, 128], BF16, kind=wkind).ap()
    ws_od = nc.dram_tensor("ws_od", [8, 128, 4, 128], BF16, kind=wkind).ap()
    ws_on = nc.dram_tensor("ws_on", [8, 128, 4, 128], BF16, kind=wkind).ap()
    ws_out = nc.dram_tensor("ws_out", [8, 128, 8, 128], BF16, kind=wkind).ap()

    with ExitStack() as es:
        def sb(n, s, dt):
            return es.enter_context(nc.sbuf_tensor("sb_" + n, s, dt))

        tr = Tracker(nc, es)
        PS = es.enter_context(nc.psum_tensor("PS", [128, 8, 512], F32))
        hT = sb("hT", [128, 8, T], BF16)
        uT = sb("uT", [128, 8, T], BF16)
        A = sb("A", [128, 20992], BF16)
        wst = [sb("wst%d" % i, [128, 4, 8, 128], BF16) for i in range(2)]
        wz = sb("wz", [128, 2, 8, 128], BF16)
        xs = [sb("xs%d" % i, [128, 1024], F32) for i in range(3)]
        hb = [sb("hb%d" % i, [128, 1024], BF16) for i in range(2)]
        hTo = sb("hTo", [128, 8, 512], BF16)
        PT = [sb("PT%d" % i, [128, 1024], BF16) for i in range(3)]
        accs = sb("accs", [128, 8, 129], F32)
        ob = sb("ob", [128, 4, 128], F32)
        ob2 = sb("ob2", [128, 4, 128], F32)
        onb = sb("onb", [128, 4, 128], BF16)
        thbuf = sb("thbuf", [128, 4, 512], F32)
        th = [thbuf[:, 0, :], thbuf[:, 1, :]]
        thb = [thbuf[:, 2, :], thbuf[:, 3, :]]
        xn = [xs[0][:], xs[1][:], thbuf[:, 0:2, :].rearrange("p a b -> p (a b)"), thbuf[:, 2:4, :].rearrange("p a b -> p (a b)")]
        xnk = [["xs0"], ["xs1"], ["th0", "th1"], ["thb0", "thb1"]]
        sz = sb("sz", [128, 512], BF16)
        untok = sb("untok", [128, 16, 256], BF16)
        t1 = ob[:].rearrange("p a e -> p (a e)")
        t2 = ob2[:].rearrange("p a e -> p (a e)")
        ytmp = accs[:].rearrange("p a e -> p (a e)")[:, 0:1024]
        postw = sb("postw", [128, 1024], F32)
        t5all = sb("t5all", [128, 4352], BF16)
        t5tab = t5all[:, 0:2304].rearrange("p (i c) -> p i c", i=2)
        t5sp = t5all[:, 2304:4352].rearrange("p (s i c) -> p s i c", s=2, i=2)
        natabi = t5all[:, 0:4096].rearrange("p (h c) -> p h c", h=4)
        NIK = ["t5tab", "t5sp"]
        natab = sb("natab", [128, 4, 1024], BF16)
        ident = sb("ident", [128, 128], BF16)
        e2 = sb("e2", [128, 128], BF16)
        m2p = sb("m2p", [128, 640], BF16)
        m2s = sb("m2s", [128, 512], BF16)
        sm = sb("sm", [128, 64], F32)
        lamv = sb("lamv", [128, 256], F32)
        cst = sb("cst", [128, 16], F32)
        prew = sb("prew", [128, 8], F32)
        subw = sb("subw", [128, 1], F32)
        mhalf = sb("mhalf", [128, 8], F32)
        rs = sb("rs", [128, 8], F32)
        ss4 = sb("ss4", [128, 4], F32)
        rs4 = sb("rs4", [128, 4], F32)
        stat = [sb("stat%d" % i, [128, 4], F32) for i in range(4)]
        nstat = sb("nstat", [128, 4, 4], F32)

        pe, act, dve, pool = nc.tensor, nc.scalar, nc.vector, nc.gpsimd
        bankkeys = ["B%d" % i for i in range(8)]
        bg_state = {"done": False}
        state = {"xi": 0, "bank": 0, "wi": 0, "st": 0, "pt": 0, "sb": 0, "xn": 0, "hb": 0, "yo": 0}
        AK = ["A.q0", "A.q1", "A.k0", "A.k1", "A.v"]

        def psflat(b0, nb):
            return PS[:, b0:b0 + nb, :].rearrange("p a b -> p (a b)")

        def psbf(b):
            return PS[:, b, :].bitcast(BF16)

        def ld(dst, src, key):
            tr.dma("sp", dst, src, w=[key], stream=(key if key.startswith("xs") else "c"))

        ld(xs[0][:, 0:128], ident_d[:, :], "xs0")
        tr.op("dve", lambda: dve.tensor_copy(out=ident[:], in_=xs[0][:, 0:128]), r=["xs0"], w=["ident"])
        ld(xs[1][:, 0:128], e2_d[:, :], "xs1")
        tr.op("dve", lambda: dve.tensor_copy(out=e2[:], in_=xs[1][:, 0:128]), r=["xs1"], w=["e2"])
        ld(xs[0][:, 0:640], m2p_d[:, :], "xs0")
        tr.op("dve", lambda: dve.tensor_copy(out=m2p[:], in_=xs[0][:, 0:640]), r=["xs0"], w=["m2p"])
        ld(xs[1][:, 0:512], m2s_d[:, :], "xs1")
        tr.op("dve", lambda: dve.tensor_copy(out=m2s[:], in_=xs[1][:, 0:512]), r=["xs1"], w=["m2s"])
        ld(prew[:], prew_d[:, :], "prew")
        ld(subw[:], subw_d[:, :], "subw")
        ld(postw[:], postw_d[0:1, :].partition_broadcast(128), "postw")
        ld(lamv[:], lamv_d[0:1, :].partition_broadcast(128), "lamv")
        ld(cst[:], t5c_d[0:1, :].partition_broadcast(128), "cst")
        tr.barrier_on("c")
        tr.op("pool", lambda: pool.memset(mhalf[:], -0.5), w=["mhalf"])
        lv = lamv[:].rearrange("p (a b d) -> p a b d", a=2, b=2)
        lp = xs[0][:, 0:128].rearrange("p (a d) -> p a d", a=2)
        tr.op("dve", lambda: dve.tensor_tensor(out=lp, in0=lv[:, :, 0, :], in1=lv[:, :, 1, :], op=ALU.mult),
              r=["lamv"], w=["xs0"])
        tr.op("dve", lambda: dve.tensor_reduce(out=sm[:, 2:4], in_=lp, axis=AX.X, op=ALU.add), r=["xs0"], w=["sm_l"])
        tr.op("act", lambda: act.activation(out=sm[:, 4:6], in_=sm[:, 2:4], func=AF.Exp), r=["sm_l"], w=["sm_e"])
        tr.op("dve", lambda: dve.tensor_tensor(out=sm[:, 6:7], in0=sm[:, 5:6], in1=sm[:, 4:5], op=ALU.subtract),
              r=["sm_e"], w=["sm_d"])
        tr.op("dve", lambda: dve.tensor_scalar(out=sm[:, 0:1], in0=sm[:, 6:7], scalar1=-0.2, scalar2=None, op0=ALU.add),
              r=["sm_d"], w=["nlam"])
        nlam = sm[:, 0:1]

        Fall = uT[:].rearrange("p a b -> p (a b)").bitcast(F32)
        Fbuf = [Fall[:, i * 4096:(i + 1) * 4096] for i in range(2)]
        Obuf = [A[:, j * 4096:(j + 1) * 4096] for j in range(2)]

        def cast_super(src3, nk, nb, dst4, mode, keys=()):
            i = state["xi"] % 2
            state["xi"] += 1
            j = state["wi"] % 2
            state["wi"] += 1
            n = nk * nb * 128
            fk, ok = "F%d" % i, ["O%d" % j]
            fv = Fbuf[i][:, 0:n].rearrange("p (k c) -> p k c", k=nk)
            tr.dma("sp", fv, src3, w=[fk], stream="F%d" % i)
            f4 = Fbuf[i][:, 0:n].rearrange("p (k b c) -> p k b c", k=nk, b=nb)
            o4 = Obuf[j][:, 0:n].rearrange("p (b k c) -> p b k c", b=nb, k=nk)
            o4t = o4.rearrange("p b k c -> p k b c")
            if mode == "pre" and (state["xi"] % 4) < 2:
                for k in range(nk):
                    tr.op("act", lambda k=k: act.activation(out=o4[:, :, k, :], in_=f4[:, k, :, :], func=AF.Copy,
                                                            scale=prew[:, k:k + 1]), r=[fk, "prew"], w=ok)
            elif mode == "pre":
                tr.op("dve", lambda: dve.tensor_tensor(out=o4t, in0=f4,
                                                       in1=prew[:, :].unsqueeze(2).unsqueeze(3).to_broadcast([128, nk, nb, 128]),
                                                       op=ALU.mult), r=[fk, "prew"], w=ok)
            elif mode == "sub":
                tr.op("dve", lambda: dve.tensor_scalar(out=o4t, in0=f4, scalar1=subw[:, 0:1], scalar2=0.8, op0=ALU.mult,
                                                       op1=ALU.mult), r=[fk, "subw"], w=ok)
            elif mode == "half":
                tr.op("act", lambda: act.activation(out=o4t, in_=f4, func=AF.Copy, scale=0.5), r=[fk], w=ok)
            else:
                tr.op("act", lambda: act.activation(out=o4t, in_=f4, func=AF.Copy), r=[fk], w=ok)
            tr.dma(STQ, dst4.rearrange("b p k c -> p b k c"), o4, r=ok, w=list(keys), stream="ws%d" % j)

        w_in3 = w_in.rearrange("(k p) c -> p k c", p=128)
        w_od3 = w_od.rearrange("(k p) c -> p k c", p=128)
        w_on3 = w_on.rearrange("(k p) c -> p k c", p=128)
        w_out3 = w_out.rearrange("(k p) c -> p k c", p=128)
        def next_bank(lo=0, hi=8):
            b = lo + state["bank"] % (hi - lo)
            state["bank"] += 1
            return b

        wcache = {}
        ready = set()
        late = []

        def _load_w_now(blocks):
            assert all(b in ready for b in blocks), ("scratch block loaded before its cast was emitted", blocks)
            j = state["wi"] % 2
            state["wi"] += 1
            for n, b in enumerate(blocks):
                tr.dma("sp", wst[j][:, n], ws_in[b], r=["WSin%d" % b], w=["wst%d" % j], stream="wst%d" % j)
            return j

        def load_w(blocks):
            key = tuple(blocks)
            if key in wcache:
                return wcache.pop(key)
            return _load_w_now(blocks)

        def prefetch_w(*lists):
            if wcache:
                return
            for blocks in lists:
                if any(b not in ready for b in blocks):
                    bg_state["retry"] = lists
                    return
            bg_state["retry"] = None
            for blocks in lists:
                wcache[tuple(blocks)] = _load_w_now(blocks)

        def run_late(n=1):
            for _ in range(n):
                if late:
                    late.pop(0)()

        class NormPipe:
            def __init__(self, ring=None):
                self.t, self.pa, self.pa2, self.pb, self.pc = [], 0, 0, 0, 0
                self.ring = ring

            def add(self, row0, dst, dkey, col0):
                self.t.append(dict(row0=row0, dst=dst, dkey=dkey, col0=col0, idx=len(self.t)))

            def _A1(self, T_):
                i = state["xn"] % 4
                state["xn"] += 1
                si = T_["idx"] % 4
                if self.ring is None:
                    xv, xk, strm = xn[i], xnk[i], "xn%d" % i
                else:
                    xv, xk, strm = self.ring[i]
                T_.update(i=i, si=si, xv=xv, xk=xk)
                sk = "nstat%d" % si
                tr.dma("sp", xv, x_all[T_["row0"]:T_["row0"] + 128, :], w=xk, stream=strm)
                tr.op("act", lambda: act.activation(out=PT[2][:], in_=xv, func=AF.Square, accum_out=nstat[:, si, 0:1]),
                      r=xk, w=["PT2", sk])

            def _A2(self, T_):
                si = T_["si"]
                sk = "nstat%d" % si
                tr.op("dve", lambda: dve.tensor_scalar(out=nstat[:, si, 1:2], in0=nstat[:, si, 0:1], scalar1=1.0 / 1024,
                                                       scalar2=1e-6, op0=ALU.mult, op1=ALU.add), r=[sk], w=[sk])
                tr.op("pool", lambda: pool.tensor_tensor(out=nstat[:, si, 2:3], in0=nstat[:, si, 1:2], in1=mhalf[:, 0:1],
                                                         op=ALU.pow), r=[sk, "mhalf"], w=[sk])

            def _B(self, T_):
                i, si = T_["i"], T_["si"]
                hi = state["hb"] % 2
                state["hb"] += 1
                hk = "hb%d" % hi
                tr.op("dve", lambda: dve.tensor_scalar(out=hb[hi][:], in0=T_["xv"], scalar1=nstat[:, si, 2:3], scalar2=None,
                                                       op0=ALU.mult), r=T_["xk"] + ["nstat%d" % si], w=[hk])
                bnk = next_bank()
                T_["bank"] = bnk
                tp = psbf(bnk)
                for k in range(8):
                    tr.op("pe", lambda k=k: pe.transpose(out=tp[:, k * 128:(k + 1) * 128], in_=hb[hi][:, k * 128:(k + 1) * 128],
                                                         identity=ident[:]), r=[hk, "ident"], w=[bankkeys[bnk]], sig=(k == 7))

            def _C(self, T_):
                bnk = T_["bank"]
                tr.op("dve", lambda: dve.tensor_copy(out=T_["dst"][:, :, T_["col0"]:T_["col0"] + 128],
                                                     in_=psbf(bnk)[:, :].rearrange("p (k c) -> p k c", k=8)),
                      r=[bankkeys[bnk]], w=[T_["dkey"]])

            def prefetch(self, upto):
                while self.pa < min(upto, len(self.t)) and self.pa - self.pb < 4:
                    self._A1(self.t[self.pa])
                    self.pa += 1
                while self.pa2 < self.pa:
                    self._A2(self.t[self.pa2])
                    self.pa2 += 1

            def run(self, upto=None, ahead=3):
                n = len(self.t)
                upto = n if upto is None else upto
                while self.pc < upto:
                    ah = max(ahead, 1) if self.pa <= self.pb and self.pa < upto else ahead
                    while self.pa < n and self.pa - self.pb < ah:
                        self._A1(self.t[self.pa])
                        self.pa += 1
                    if self.pa2 < self.pa and self.pa2 <= self.pb:
                        self._A2(self.t[self.pa2])
                        self.pa2 += 1
                    if self.pb < min(upto, self.pa2):
                        self._B(self.t[self.pb])
                        self.pb += 1
                    if self.pc < self.pb - 1 or (self.pb >= min(upto, self.pa) and self.pc < self.pb):
                        self._C(self.t[self.pc])
                        self.pc += 1
                    if self.pa2 < self.pa and self.pa2 <= self.pb:
                        self._A2(self.t[self.pa2])
                        self.pa2 += 1

        def proj_fm(wj, wn, src, skey, c0, ncols, dst_ap, dkey, scale, wkey=None):
            b = next_bank()
            wsrc = wst[wj] if wkey is None else wz
            for k in range(8):
                tr.op("pe", lambda k=k: pe.matmul(PS[:, b, 0:ncols], lhsT=wsrc[:, wn, k, :], rhs=src[:, k, c0:c0 + ncols],
                                                  start=(k == 0), stop=(k == 7)),
                      r=[("wst%d" % wj) if wkey is None else wkey, skey], w=[bankkeys[b]], sig=(k == 7))
            dk = list(dkey) if isinstance(dkey, (list, tuple)) else [dkey]
            if isinstance(dst_ap, tuple):
                for hf, d_ in enumerate(dst_ap):
                    tr.op("act", lambda hf=hf, d_=d_: act.activation(out=d_, in_=PS[64 * hf:64 * hf + 64, b, 0:ncols],
                                                                      func=AF.Copy, scale=scale), r=[bankkeys[b]], w=dk)
            else:
                tr.op("act", lambda: act.activation(out=dst_ap, in_=PS[:, b, 0:ncols], func=AF.Copy, scale=scale),
                      r=[bankkeys[b]], w=dk)
            state["pj"] = state.get("pj", 0) + 1
            if state["pj"] % 3 == 0:
                run_late()

        def proj_tm(wj, wn0, nblk, src, skey, c0, dst_ap3, dkey, nh, hd):
            b = next_bank()
            for k in range(8):
                tr.op("pe", lambda k=k: pe.matmul(PS[:, b, 0:nblk * 128], lhsT=src[:, k, c0:c0 + 128],
                                                  rhs=wst[wj][:, wn0:wn0 + nblk, k, :], start=(k == 0), stop=(k == 7)),
                      r=["wst%d" % wj, skey], w=[bankkeys[b]], sig=(k == 7))
            tr.op("act", lambda: act.activation(out=dst_ap3, in_=PS[:, b, 0:nblk * 128].rearrange("p (h d) -> p h d", h=nh),
                                                func=AF.Copy), r=[bankkeys[b]], w=[dkey])

        def silu_from_bank(b, dst_bf):
            tr.op("act", lambda: act.activation(out=th[0], in_=PS[:, b, :], func=AF.Tanh, scale=0.5),
                  r=[bankkeys[b]], w=["th0"])
            tr.op("dve", lambda: dve.tensor_scalar(out=th[0], in0=th[0], scalar1=0.5, scalar2=0.5, op0=ALU.mult,
                                                   op1=ALU.add), r=["th0"], w=["th0"])
            tr.op("dve", lambda: dve.tensor_tensor(out=dst_bf, in0=th[0], in1=PS[:, b, :], op=ALU.mult),
                  r=["th0", bankkeys[b]], w=["sz"])

        QTa = A[:, 0:4096].rearrange("p (h t) -> p h t", h=2)
        KTa = A[:, 4096:12288].rearrange("p (h t) -> p h t", h=2)
        Va = A[:, 12288:12288 + 32 * 2 * 129].rearrange("p (t h e) -> p t h e", t=32, h=2)
        QTz = A[:, 0:8192].rearrange("p (h t) -> p h t", h=4)
        KTn = A[:, 8192:8192 + 2 * 2560].rearrange("p (h t) -> p h t", h=2)
        Vn = A[:, 13312:13312 + 20 * 4 * 65].rearrange("p (t h e) -> p t h e", t=20, h=4)
        NQ, NKK = ["A.q0", "A.q1", "A.k0"], ["A.k1", "A.v"]
        Wod_sb = A[:, 0:4096].rearrange("p (b f c) -> p b f c", b=8, f=4)
        Won_sb = A[:, 4096:8192].rearrange("p (b f c) -> p b f c", b=8, f=4)
        Wout_sb = A[:, 8192:16384].rearrange("p (b k c) -> p b k c", b=8, k=8)

        def unit(u, own0, oth0, out0, pre_a=None, next_own0=None):
            halo = oth0 is not None
            nkt = 32 if halo else 16

            if pre_a is None:
                npipe = NormPipe()
                for t in range(16):
                    npipe.add(own0 + t * 128, hT, "hT%d" % (t // 4), t * 128)
                npipe.run()
            else:
                pre_a.run()

            if stage < 2:
                return
            for g in range(2):
                wj = load_w([2 * g, 2 * g + 1, 4 + 2 * g, 5 + 2 * g])
                wj2 = load_w([8 + 2 * g, 9 + 2 * g])
                assert {12 + 2 * g, 13 + 2 * g} <= ready
                tr.dma("sp", wz[:, 0], ws_in[12 + 2 * g], r=["WSin%d" % (12 + 2 * g)], w=["wz"], stream="wz")
                tr.dma("sp", wz[:, 1], ws_in[13 + 2 * g], r=["WSin%d" % (13 + 2 * g)], w=["wz"], stream="wz")
                npipe = None
                if halo:
                    npipe = NormPipe()
                    for c in range(4):
                        for t in range(4):
                            npipe.add(oth0 + (c * 4 + t) * 128, hTo, "hTo", t * 128)
                    npipe.prefetch(4)
                for c in range(4):
                    for i in range(2):
                        proj_fm(wj, i, hT, "hT%d" % c, c * 512, 512, QTa[:, i, c * 512:(c + 1) * 512], "A.q%d" % i, 0.125)
                        proj_fm(wj, 2 + i, hT, "hT%d" % c, c * 512, 512, KTa[:, i, c * 512:(c + 1) * 512], "A.k%d" % i, 1.0)
                    for t in range(4 * c, 4 * c + 4):
                        proj_tm(wj2, 0, 2, hT, "hT%d" % (t // 4), t * 128, Va[:, t, :, 0:128], "A.v", 2, 128)
                    if halo:
                        npipe.run(upto=4 * (c + 1))
                        for i in range(2):
                            proj_fm(wj, 2 + i, hTo, "hTo", 0, 512, KTa[:, i, 2048 + c * 512:2048 + (c + 1) * 512],
                                    "A.k%d" % i, 1.0)
                        for t in range(4):
                            proj_tm(wj2, 0, 2, hTo, "hTo", t * 128, Va[:, 16 + c * 4 + t, :, 0:128], "A.v", 2, 128)

                tr.op("dve", lambda: dve.memset(Va[:, 0:nkt, :, 128:129], 1.0), w=["A.v"])
                for i in range(2):
                    h = 2 * g + i
                    for part in range(2):
                        xi = state["xi"] % 2
                        state["xi"] += 1
                        tr.dma("sp", xs[xi][:, 0:576], t5own_d[h, :, part * 576:(part + 1) * 576], w=["xs%d" % xi],
                               stream="xs%d" % xi)
                        tr.op("dve", lambda xi=xi, i=i, part=part: dve.tensor_copy(
                            out=t5tab[:, i, part * 576:(part + 1) * 576], in_=xs[xi][:, 0:576]), r=["xs%d" % xi], w=["t5tab"])
                    if halo:
                        xi = state["xi"] % 2
                        state["xi"] += 1
                        tr.dma("sp", xs[xi][:, :].rearrange("p (s c) -> p s c", s=2), t5spec_d[:, h].rearrange("s p c -> p s c"),
                               w=["xs%d" % xi], stream="xs%d" % xi)
                        tr.op("dve", lambda xi=xi, i=i: dve.tensor_copy(
                            out=t5sp[:, :, i, :], in_=xs[xi][:, :].rearrange("p (s c) -> p s c", s=2)), r=["xs%d" % xi], w=["t5sp"])
                run_late(99)
                if g == 0:
                    prefetch_w([2, 3, 6, 7], [10, 11])
                else:
                    prefetch_w([16, 17, 20, 21], [24, 25])
                da_group(g, nkt, halo)

            if stage < 3:
                return
            bg.flush()
            cand = na_candidates(halo)
            tiles = na_tiles(halo)
            m2 = m2p if halo else m2s
            m2k = "m2p" if halo else "m2s"
            ntile_m2 = 20 if halo else 16
            m2v = m2[:, :].rearrange("p (t r) -> p t r", t=ntile_m2)
            for G in range(2):
                wj = load_w([16 + 2 * G, 17 + 2 * G, 20 + 2 * G, 21 + 2 * G])
                if G == 0:
                    for hh in range(4):
                        zr = slice(64, 128) if hh % 2 == 0 else slice(0, 64)
                        tr.op("dve", lambda hh=hh, zr=zr: dve.memset(QTz[zr, hh, :].bitcast(F32), 0.0), w=NQ)
                for i in range(2):
                    for c in range(4):
                        proj_fm(wj, 2 + i, hT, "hT%d" % c, c * 512, 512, KTn[:, i, c * 512:(c + 1) * 512], NKK, 1.0)
                for i in range(2):
                    for c in range(4):
                        proj_fm(wj, i, hT, "hT%d" % c, c * 512, 512,
                                (QTz[0:64, 2 * i, c * 512:(c + 1) * 512], QTz[64:128, 2 * i + 1, c * 512:(c + 1) * 512]), NQ, 0.125)
                for hh in range(4):
                    xi = state["xi"] % 2
                    state["xi"] += 1
                    tr.dma("sp", xs[xi][:], natab_d[4 * G + hh], w=["xs%d" % xi], stream="xs%d" % xi)
                    tr.op("dve", lambda xi=xi, hh=hh: dve.tensor_copy(out=natab[:, hh, :], in_=xs[xi][:]),
                          r=["xs%d" % xi], w=["natab"])
                    xi = state["xi"] % 2
                    state["xi"] += 1
                    tr.dma("sp", xs[xi][:], natabi_d[4 * G + hh], w=["xs%d" % xi], stream="xs%d" % xi)
                    tr.op("dve", lambda xi=xi, hh=hh: dve.tensor_copy(out=natabi[:, hh, :], in_=xs[xi][:]),
                          r=["xs%d" % xi], w=NIK)
                wj2 = load_w([24 + 2 * G, 25 + 2 * G])
                assert {28 + 2 * G, 29 + 2 * G} <= ready
                tr.dma("sp", wz[:, 0], ws_in[28 + 2 * G], r=["WSin%d" % (28 + 2 * G)], w=["wz"], stream="wz")
                tr.dma("sp", wz[:, 1], ws_in[29 + 2 * G], r=["WSin%d" % (29 + 2 * G)], w=["wz"], stream="wz")
                tr.op("dve", lambda: dve.memset(Vn[:, :, :, 64:65], 1.0), w=["A.v"])
                for t in range(16):
                    proj_tm(wj2, 0, 2, hT, "hT%d" % (t // 4), t * 128, Vn[:, t, :, 0:64], "A.v", 4, 64)
                if halo:
                    npipe = NormPipe()
                    for t, ot in enumerate([0, 1, 14, 15]):
                        npipe.add(oth0 + ot * 128, hTo, "hTo", t * 128)
                    npipe.run()
                    for i in range(2):
                        proj_fm(wj, 2 + i, hTo, "hTo", 0, 512, KTn[:, i, 2048:2560], NKK, 1.0)
                    for t in range(4):
                        proj_tm(wj2, 0, 2, hTo, "hTo", t * 128, Vn[:, 16 + t, :, 0:64], "A.v", 4, 64)

                first, last = {}, {}
                for (t, a) in tiles:
                    r_lo, r_hi = cand[t]
                    for qt in range(r_lo // 2, r_hi // 2 + 1):
                        first.setdefault(qt, t)
                        last[qt] = t
                nsteps = [(hh, t, a) for hh in range(4) for (t, a) in tiles]

                def na_qk(n):
                    hh, t, a = nsteps[n]
                    i, b0 = hh // 2, (hh % 2) * 64
                    r_lo, r_hi = cand[t]
                    nq = (r_hi - r_lo + 1) * 64
                    s = n % 2
                    s0 = r_lo - a + 7
                    chunks = [(c0, c1) for (c0, c1) in ((0, min(512, nq)), (512, nq)) if c1 > c0]
                    for ci, (c0, c1) in enumerate(chunks):
                        bk = [bankkeys[2 * s + ci]]
                        bnk = 2 * s + ci
                        ops = [("qk", c0, c1)]
                        ra, rb = r_lo + c0 // 64, r_lo + c1 // 64
                        if t < 16:
                            segs = [(ra, min(rb, 4), False), (max(ra, 4), min(rb, 29), True), (max(ra, 29), rb, False)]
                        else:
                            segs = [(ra, rb, False)]
                        for (x0, x1, interior) in segs:
                            if x1 <= x0:
                                continue
                            ops.append(("tabi" if interior else "tab", x0, x1))
                            if not interior:
                                ops.append(("mask", x0, x1))
                        for oi, (kind, x0, x1) in enumerate(ops):
                            lastop = (oi == len(ops) - 1)
                            sg = lastop and (ci == len(chunks) - 1)
                            if kind == "qk":
                                o = PS[:, bnk, 0:c1 - c0]
                                tr.op("pe", lambda o=o: pe.matmul(
                                    o, lhsT=KTn[:, i, t * 128:(t + 1) * 128],
                                    rhs=QTz[:, hh, r_lo * 64 + c0:r_lo * 64 + c1], start=True, stop=False),
                                    r=NKK + NQ, w=bk, sig=False)
                                continue
                            o = PS[:, bnk, (x0 - ra) * 64:(x1 - ra) * 64]
                            sa, sb_ = (x0 - a + 7) * 64, (x1 - a + 7) * 64
                            if kind == "tab":
                                tr.op("pe", lambda o=o, sa=sa, sb_=sb_, lastop=lastop: pe.matmul(
                                    o, lhsT=ident[:], rhs=natab[:, hh, sa:sb_], start=False, stop=lastop, skip_group_check=True),
                                    r=["ident", "natab"], w=bk, sig=sg)
                            elif kind == "tabi":
                                tr.op("pe", lambda o=o, sa=sa, sb_=sb_, lastop=lastop: pe.matmul(
                                    o, lhsT=ident[:], rhs=natabi[:, hh, sa:sb_], start=False, stop=lastop, skip_group_check=True),
                                    r=["ident"] + NIK, w=bk, sig=sg)
                            else:
                                tr.op("pe", lambda o=o, x0=x0, x1=x1, lastop=lastop: pe.matmul(
                                    o, lhsT=e2[:, :], rhs=m2v[:, t, x0:x1].unsqueeze(2).to_broadcast([128, x1 - x0, 64]),
                                    start=False, stop=lastop, skip_group_check=True), r=["e2", m2k], w=bk, sig=sg)

                gfirst, glast = {}, {}
                for (t, a) in tiles:
                    r_lo, r_hi = cand[t]
                    for qt in range(r_lo // 2, r_hi // 2 + 1):
                        gfirst.setdefault(qt // 4, t)
                        glast[qt // 4] = t
                gbank = {}

                def na_exp_pv(n):
                    hh, t, a = nsteps[n]
                    r_lo, r_hi = cand[t]
                    nq = (r_hi - r_lo + 1) * 64
                    s = n % 2
                    sk = [bankkeys[2 * s], bankkeys[2 * s + 1]] if nq > 512 else [bankkeys[2 * s]]
                    pi = state["pt"] % 3
                    state["pt"] += 1
                    pk = "PT%d" % pi
                    tr.op("act", lambda: act.activation(out=PT[pi][:, 0:nq], in_=psflat(2 * s, 2)[:, 0:nq], func=AF.Exp),
                          r=sk, w=[pk])
                    if n + 2 < len(nsteps):
                        na_qk(n + 2)
                    qts = list(range(r_lo // 2, r_hi // 2 + 1))
                    for qn, qt in enumerate(qts):
                        gb = qt // 4
                        fresh = False
                        if (hh, gb) not in gbank:
                            gbank[(hh, gb)] = 4 + state["sb"] % 3
                            state["sb"] += 1
                            fresh = True
                        bnk = gbank[(hh, gb)]
                        slot = qt % 4
                        dst = PS[:, bnk, slot * 65:slot * 65 + 65]
                        q0 = (qt - r_lo // 2) * 128
                        lastmm = (glast[gb] == t) and (qn == len(qts) - 1 or qts[qn + 1] // 4 != gb)
                        tr.op("pe", lambda dst=dst, q0=q0, fresh=fresh: pe.matmul(
                            dst, lhsT=PT[pi][:, q0:q0 + 128], rhs=Vn[:, t, hh, :], start=fresh, stop=(last[qt] == t),
                            skip_group_check=True), r=[pk, "A.v"], w=[bankkeys[bnk]], sig=lastmm)
                        if lastmm:
                            si = state["st"] % 4
                            state["st"] += 1
                            gv = PS[:, bnk, 0:260].rearrange("p (q e) -> p q e", q=4)
                            tr.op("dve", lambda gv=gv, si=si: dve.reciprocal(out=stat[si][:, 0:4], in_=gv[:, :, 64]),
                                  r=[bankkeys[bnk]], w=["stat%d" % si])
                            tr.op("dve", lambda gv=gv, si=si, gb=gb: dve.tensor_tensor(
                                out=untok[:, 4 * gb:4 * gb + 4, hh * 64:(hh + 1) * 64], in0=gv[:, :, 0:64],
                                in1=stat[si][:, 0:4].unsqueeze(2).to_broadcast([128, 4, 64]), op=ALU.mult),
                                r=[bankkeys[bnk], "stat%d" % si], w=["untok"])

                run_late(99)
                if G == 0:
                    prefetch_w([18, 19, 22, 23], [26, 27])
                else:
                    prefetch_w([32, 40], [33, 41])
                na_qk(0)
                if len(nsteps) > 1:
                    na_qk(1)
                for n in range(len(nsteps)):
                    na_exp_pv(n)
                for i in range(2):
                    for qc in range(4):
                        b = (7, 3)[qc % 2]
                        tb_ = (6, 2)[qc % 2]
                        for k in range(8):
                            tr.op("pe", lambda k=k: pe.matmul(PS[:, b, :], lhsT=wz[:, i, k, :], rhs=hT[:, k, qc * 512:(qc + 1) * 512],
                                                              start=(k == 0), stop=(k == 7)), r=["wz", "hT%d" % qc], w=[bankkeys[b]],
                                  sig=(k == 7))
                        silu_from_bank(b, sz[:])
                        tp = psbf(tb_)
                        for j in range(4):
                            tr.op("pe", lambda j=j: pe.transpose(out=tp[:, j * 128:(j + 1) * 128],
                                                                 in_=untok[:, qc * 4 + j, i * 128:(i + 1) * 128], identity=ident[:]),
                                  r=["untok", "ident"], w=[bankkeys[tb_]], sig=(j == 3))
                        tr.op("dve", lambda: dve.tensor_tensor(out=uT[:, 4 + 2 * G + i, qc * 512:(qc + 1) * 512], in0=tp[:, 0:512],
                                                               in1=sz[:], op=ALU.mult), r=[bankkeys[tb_], "sz"], w=["uT"])

            if stage < 4:
                return
            run_late(99)
            tr.dma("sp", Wod_sb, ws_od.rearrange("b p f c -> p b f c"), r=["WSod0", "WSod1"], w=AK, stream="A")
            tr.dma("sp", Won_sb, ws_on.rearrange("b p f c -> p b f c"), r=["WSon0", "WSon1"], w=AK, stream="A")
            for b in range(0, 8, 4):
                tr.dma("sp", Wout_sb[:, b:b + 4], ws_out[b:b + 4].rearrange("b p k c -> p b k c"),
                       r=["WSout%d" % (b // 2), "WSout%d" % (b // 2 + 1)], w=["A.out"], stream="Aout")
            mTb = [hTo, t5all[:, 0:4096].rearrange("p (d t) -> p d t", d=8)]
            mkeys = [["hTo"], NIK]
            pend_tiles = []
            pre = []
            ystore = []
            ypost = []
            if (32, 40) in wcache and (33, 41) in wcache:
                pre = [wcache.pop((32, 40)), wcache.pop((33, 41))]
            nxt = None
            if next_own0 is not None:
                nflat = natab[:].rearrange("p a b -> p (a b)").bitcast(F32)
                uflat2 = untok[:].rearrange("p a b -> p (a b)").bitcast(F32)
                ring = [(nflat[:, 0:1024], ["natabA"], "xe0"), (nflat[:, 1024:2048], ["natabB"], "xe1"),
                        (uflat2[:, 0:1024], ["untokA"], "xe2"), (uflat2[:, 1024:2048], ["untokB"], "xe3")]
                tr.op("dve", lambda: dve.memset(sm[:, 10:11], 0.0), w=["natab", "untok", "natabA", "natabB", "untokA", "untokB"])
                nxt = NormPipe(ring)
                for t in range(16):
                    nxt.add(next_own0 + t * 128, hT, "hT%d" % (t // 4), t * 128)
            for c in range(4):
                cs = slice(c * 512, (c + 1) * 512)
                mT, mk = mTb[c % 2], mkeys[c % 2]
                if nxt is not None:
                    nxt.prefetch(4 * (c + 1))
                for dc in range(8):
                    wj = pre.pop(0) if pre else load_w([32 + dc, 40 + dc])
                    if not pre:
                        if dc < 7:
                            pre.append(load_w([33 + dc, 41 + dc]))
                        elif c < 3:
                            pre.append(load_w([32, 40]))
                    ba, bn, bga, bgb = [4 * (dc % 2) + x_ for x_ in range(4)]
                    ka, kn_, kga, kgb = bankkeys[ba], bankkeys[bn], bankkeys[bga], bankkeys[bgb]
                    for n, bb in ((0, bga), (1, bgb)):
                        for k in range(8):
                            tr.op("pe", lambda k=k, n=n, bb=bb: pe.matmul(PS[:, bb, :], lhsT=wst[wj][:, n, k, :], rhs=hT[:, k, cs],
                                                                          start=(k == 0), stop=(k == 7)),
                                  r=["wst%d" % wj, "hT%d" % c], w=[bankkeys[bb]], sig=(k == 7))
                    for f in range(4):
                        tr.op("pe", lambda f=f: pe.matmul(PS[:, ba, :], lhsT=Wod_sb[:, dc, f, :], rhs=uT[:, f, cs], start=(f == 0),
                                                          stop=(f == 3)), r=AK + ["uT"], w=[ka], sig=(f == 3))
                    for f in range(4):
                        tr.op("pe", lambda f=f: pe.matmul(PS[:, bn, :], lhsT=Won_sb[:, dc, f, :], rhs=uT[:, 4 + f, cs], start=(f == 0),
                                                          stop=(f == 3)), r=AK + ["uT"], w=[kn_], sig=(f == 3))
                    ti = dc % 2
                    tr.op("act", lambda: act.activation(out=th[ti], in_=PS[:, bga, :], func=AF.Tanh, scale=0.5), r=[kga],
                          w=["th%d" % ti])
                    tr.op("act", lambda: act.activation(out=thb[ti], in_=PS[:, bgb, :], func=AF.Tanh, scale=0.5), r=[kgb],
                          w=["thb%d" % ti])
                    tr.op("dve", lambda: dve.scalar_tensor_tensor(out=t1, in0=th[ti], scalar=1.0, in1=PS[:, ba, :], op0=ALU.add,
                                                                  op1=ALU.mult), r=["th%d" % ti, ka], w=["ob"])
                    tr.op("dve", lambda: dve.scalar_tensor_tensor(out=t2, in0=thb[ti], scalar=1.0, in1=PS[:, bn, :], op0=ALU.add,
                                                                  op1=ALU.mult), r=["thb%d" % ti, kn_], w=["ob2"])
                    tr.op("dve", lambda: dve.tensor_tensor(out=mT[:, dc, :], in0=t1, in1=t2, op=ALU.add), r=["ob", "ob2"],
                          w=mk)
                    if pend_tiles and dc % 2 == 0:
                        pend_tiles.pop(0)()
                if c == 3 and next_own0 is not None:
                    prefetch_w([0, 1, 4, 5], [8, 9])
                if nxt is not None:
                    nxt.run(upto=4 * (c + 1), ahead=0)
                def out_tile(c, t, mT, mk, pb, defer):
                    row = c * 512 + t * 128
                    i = state["yo"] % 3
                    state["yo"] += 1
                    xk = "xs%d" % i
                    tr.dma("sp", xs[i][:], x_all[own0 + row:own0 + row + 128, :], w=[xk], stream=xk)
                    for half in range(2):
                        bb = pb + half
                        for dc in range(8):
                            tr.op("pe", lambda dc=dc, half=half, bb=bb: pe.matmul(
                                PS[:, bb, :], lhsT=mT[:, dc, t * 128:(t + 1) * 128], rhs=Wout_sb[:, 4 * half:4 * half + 4, dc, :],
                                start=(dc == 0), stop=(dc == 7)), r=mk + ["A.out"] + AK, w=[bankkeys[bb]], sig=(dc == 7))
                    si = state["st"] % 4
                    state["st"] += 1
                    sk = "stat%d" % si
                    pk2 = [bankkeys[pb], bankkeys[pb + 1]]
                    tr.op("act", lambda si=si, pb=pb: act.activation(out=hb[0][:], in_=psflat(pb, 2), func=AF.Square,
                                                                     accum_out=stat[si][:, 0:1]), r=pk2, w=["hb0", sk])
                    if ystore:
                        ystore.pop(0)()
                    if ypost:
                        ypost.pop(0)()
                    tr.op("dve", lambda si=si: dve.tensor_scalar(out=stat[si][:, 1:2], in0=stat[si][:, 0:1], scalar1=1.0 / 1024,
                                                                 scalar2=1e-6, op0=ALU.mult, op1=ALU.add), r=[sk], w=[sk])
                    tr.op("pool", lambda si=si: pool.tensor_tensor(out=stat[si][:, 2:3], in0=stat[si][:, 1:2], in1=mhalf[:, 0:1],
                                                                   op=ALU.pow), r=[sk, "mhalf"], w=[sk])

                    def post2(si=si, pb=pb, i=i, xk=xk, row=row, sk=sk, pk2=pk2):
                        tr.op("dve", lambda: dve.scalar_tensor_tensor(out=ytmp, in0=psflat(pb, 2), scalar=stat[si][:, 2:3],
                                                                      in1=postw[:], op0=ALU.mult, op1=ALU.mult),
                              r=pk2 + [sk, "postw"], w=["accs"])
                        tr.op("dve", lambda: dve.tensor_tensor(out=xs[i][:], in0=ytmp, in1=xs[i][:], op=ALU.add), r=["accs", xk],
                              w=[xk])
                        ystore.append(lambda: tr.dma("act", y_all[out0 + row:out0 + row + 128, :], xs[i][:],
                                                     r=[xk], stream="st%d" % i))
                    ypost.append(post2)
                    if not defer:
                        while ypost:
                            ypost.pop(0)()

                if c < 3:
                    for t in range(4):
                        pend_tiles.append(lambda c=c, t=t, mT=mT, mk=mk: out_tile(c, t, mT, mk, 4, False))
                else:
                    for t in range(4):
                        out_tile(c, t, mT, mk, 2 * (t % 4), True)
                    while ypost:
                        ypost.pop(0)()
            while ystore:
                ystore.pop(0)()
            if nxt is not None:
                nxt.run()
                tr.op("dve", lambda: dve.memset(sm[:, 11:12], 0.0), w=["natabA", "natabB", "untokA", "untokB", "natab", "untok"])

        def da_group(g, nkt, halo):
            ACC = lambda a: PS[:, 4 + a // 3, (a % 3) * 129:(a % 3) * 129 + 129]
            steps = [(i, qc, kt) for i in range(2) for qc in range(4) for kt in range(nkt)]

            def bias_of(i, qc, kt):
                h = 2 * g + i
                if kt < 16:
                    d = kt * 128 - qc * 512
                    if d < -128:
                        return None, 3 * h + 0
                    if d > 512:
                        return None, 3 * h + 1
                    off = 512 - d
                    return t5tab[:, i, off:off + 512], 12
                if kt == 16 and qc == 3:
                    return t5sp[:, 0, i, :], 12
                if kt == 31 and qc == 0:
                    return t5sp[:, 1, i, :], 12
                return None, 3 * h + 2

            def qk(n):
                i, qc, kt = steps[n]
                s = n % 2
                tab, _ = bias_of(i, qc, kt)
                for m in range(2):
                    o = PS[:, 2 * s + m, :]
                    tr.op("pe", lambda o=o, m=m: pe.matmul(o, lhsT=KTa[64 * m:64 * m + 64, i, kt * 128:(kt + 1) * 128],
                                                           rhs=QTa[64 * m:64 * m + 64, i, qc * 512:(qc + 1) * 512], start=True,
                                                           stop=(tab is None)),
                          r=["A.k%d" % i, "A.q%d" % i], w=[bankkeys[2 * s + m]], sig=(tab is None and m == 1))
                if tab is not None:
                    for m in range(2):
                        o = PS[:, 2 * s + m, :]
                        tr.op("pe", lambda o=o: pe.matmul(o, lhsT=ident[:], rhs=tab, start=False, stop=True),
                              r=["ident", "t5tab", "t5sp"], w=[bankkeys[2 * s + m]], sig=(m == 1))

            EPI_LAG, BG_EVERY = 10, 12

            def zproj_mm(i, qc, k):
                tr.op("pe", lambda: pe.matmul(PS[:, 7, :], lhsT=wz[:, i, k, :], rhs=hT[:, k, qc * 512:(qc + 1) * 512],
                                              start=(k == 0), stop=(k == 7)), r=["wz", "hT%d" % qc], w=["B7"], sig=(k == 7))
                if k == 7:
                    silu_from_bank(7, sz[:])

            zlast = min(EPI_LAG + 8, nkt - 1)
            zsteps = list(range(EPI_LAG, zlast + 1))
            zplan = {kt_: [] for kt_ in zsteps}
            for k in range(8):
                zplan[zsteps[k * len(zsteps) // 8]].append(k)

            def epilogue_dve():
                for bnk, n in ((4, 3), (5, 3), (6, 2)):
                    a0 = (bnk - 4) * 3
                    tr.op("dve", lambda bnk=bnk, n=n, a0=a0: dve.tensor_copy(
                        out=accs[:, a0:a0 + n, :], in_=PS[:, bnk, 0:n * 129].rearrange("p (a e) -> p a e", a=n)),
                        r=[bankkeys[bnk]], w=["accs"])
                tr.op("dve", lambda: dve.reciprocal(out=rs[:], in_=accs[:, :, 128]), r=["accs"], w=["rs"])
                tr.op("dve", lambda: dve.tensor_scalar(out=rs[:, 4:8], in0=rs[:, 4:8], scalar1=nlam, scalar2=None, op0=ALU.mult),
                      r=["rs", "nlam"], w=["rs"])
                tr.op("dve", lambda: dve.tensor_tensor(out=ob[:], in0=accs[:, 0:4, 0:128],
                                                       in1=rs[:, 0:4].unsqueeze(2).to_broadcast([128, 4, 128]), op=ALU.mult),
                      r=["accs", "rs"], w=["ob"])
                tr.op("dve", lambda: dve.tensor_tensor(out=ob2[:], in0=accs[:, 4:8, 0:128],
                                                       in1=rs[:, 4:8].unsqueeze(2).to_broadcast([128, 4, 128]), op=ALU.mult),
                      r=["accs", "rs"], w=["ob2"])
                tr.op("dve", lambda: dve.tensor_tensor(out=ob[:], in0=ob[:], in1=ob2[:], op=ALU.add), r=["ob", "ob2"], w=["ob"])
                tr.op("dve", lambda: dve.tensor_tensor(out=ob2[:], in0=ob[:], in1=ob[:], op=ALU.mult), r=["ob"], w=["ob2"])
                tr.op("dve", lambda: dve.tensor_reduce(out=ss4[:], in_=ob2[:], axis=AX.X, op=ALU.add), r=["ob2"], w=["ss4"])
                tr.op("dve", lambda: dve.tensor_scalar(out=ss4[:], in0=ss4[:], scalar1=1.0 / 128, scalar2=1e-5, op0=ALU.mult,
                                                       op1=ALU.add), r=["ss4"], w=["ss4"])
                tr.op("pool", lambda: pool.tensor_tensor(out=rs4[:], in0=ss4[:], in1=mhalf[:, 0:4], op=ALU.pow), r=["ss4", "mhalf"],
                      w=["rs4"])
                tr.op("dve", lambda: dve.tensor_tensor(out=onb[:], in0=ob[:], in1=rs4[:].unsqueeze(2).to_broadcast([128, 4, 128]),
                                                       op=ALU.mult), r=["ob", "rs4"], w=["onb"])

            def epilogue_pe(i, qc):
                h = 2 * g + i
                tp = psbf(7)
                for j in range(4):
                    tr.op("pe", lambda j=j: pe.transpose(out=tp[:, j * 128:(j + 1) * 128], in_=onb[:, j, :], identity=ident[:]),
                          r=["onb", "ident", "sz"], w=["B7"], sig=(j == 3))
                tr.op("dve", lambda: dve.tensor_tensor(out=uT[:, h, qc * 512:(qc + 1) * 512], in0=tp[:, 0:512], in1=sz[:],
                                                       op=ALU.mult), r=["B7", "sz"], w=["uT"])

            deferred = []
            qk(0)
            if len(steps) > 1:
                qk(1)
            for n, (i, qc, kt) in enumerate(steps):
                s = n % 2
                _, bcol = bias_of(i, qc, kt)
                pi = state["pt"] % 3
                state["pt"] += 1
                pk = "PT%d" % pi
                tr.op("act", lambda s=s, pi=pi, bcol=bcol: act.activation(out=PT[pi][:], in_=psflat(2 * s, 2), func=AF.Exp,
                                                                         bias=cst[:, bcol:bcol + 1], scale=1.0),
                      r=[bankkeys[2 * s], bankkeys[2 * s + 1], "cst"], w=[pk])
                if n + 2 < len(steps):
                    qk(n + 2)
                for m in range(2):
                    for j in range(4):
                        a = m * 4 + j
                        tr.op("pe", lambda a=a, m=m, j=j, pi=pi: pe.matmul(
                            ACC(a), lhsT=PT[pi][:, m * 512 + j * 128:m * 512 + (j + 1) * 128], rhs=Va[:, kt, i, :],
                            start=(kt == 0 and a % 3 == 0), stop=(kt == nkt - 1), skip_group_check=True),
                            r=[pk, "A.v"], w=[bankkeys[4 + a // 3]], sig=(kt == nkt - 1 and a in (2, 5, 7)))
                if deferred and deferred[0][0] <= n:
                    deferred.pop(0)[1]()
                for k in zplan.get(kt, ()):
                    zproj_mm(i, qc, k)
                if n % BG_EVERY == 3 and kt not in (nkt - 1, 0):
                    bg.tick()
                    if bg_state.get("retry"):
                        prefetch_w(*bg_state["retry"])
                if kt == nkt - 1:
                    epilogue_dve()
                    deferred.append((n + EPI_LAG, lambda i=i, qc=qc: epilogue_pe(i, qc)))
            while deferred:
                late.append(deferred.pop(0)[1])
            bg.drain()

        pre_a0 = NormPipe()
        if stage >= 1:
            for t in range(16):
                pre_a0.add(t * 128, hT, "hT%d" % (t // 4), t * 128)
        EARLY = [0, 4, 8, 12]
        pre_a0.prefetch(3)
        for n_, b0 in enumerate(EARLY):
            cast_super(w_in3[:, :, b0 * 128:b0 * 128 + 256], 8, 2, ws_in[b0:b0 + 2], "pre",
                       keys=["WSin%d" % b0, "WSin%d" % (b0 + 1)])
            ready.update((b0, b0 + 1))
            pre_a0.run(upto=min(4 * (n_ + 1), len(pre_a0.t)))
        tr.op("dve", lambda: dve.memset(sm[:, 8:9], 0.0), w=["F0", "F1", "O0", "O1", "uT"] + AK)

        class BgCast:
            def __init__(self):
                self.pieces = []
                for b0 in [2, 6, 10, 14] + list(range(16, 48, 2)):
                    self.pieces.append((w_in3[:, :, b0 * 128:b0 * 128 + 256], 8, 2, ws_in[b0:b0 + 2], "pre",
                                        ["WSin%d" % b0, "WSin%d" % (b0 + 1)]))
                for hf in range(2):
                    self.pieces.append((w_od3[:, :, hf * 512:(hf + 1) * 512], 4, 4, ws_od[4 * hf:4 * hf + 4], "sub", ["WSod%d" % hf]))
                for hf in range(2):
                    self.pieces.append((w_on3[:, :, hf * 512:(hf + 1) * 512], 4, 4, ws_on[4 * hf:4 * hf + 4], "plain", ["WSon%d" % hf]))
                for q4 in range(4):
                    self.pieces.append((w_out3[:, :, q4 * 256:(q4 + 1) * 256], 8, 2, ws_out[2 * q4:2 * q4 + 2], "half", ["WSout%d" % q4]))
                self.F = [(hTo[:].rearrange("p a b -> p (a b)").bitcast(F32), ["hTo"]),
                          (natab[:].rearrange("p a b -> p (a b)").bitcast(F32), ["natab"])]
                uflat = untok[:].rearrange("p a b -> p (a b)")
                self.O = [(uflat[:, 0:2048], ["O0"]), (uflat[:, 2048:4096], ["O1"])]
                self.k = self.kx = self.ks = 0
                self.done = False

            def _L(self, k):
                src3, nk, nb, dst4, mode, keys = self.pieces[k]
                fb, fk = self.F[k % 2]
                n = nk * nb * 128
                tr.dma("sp", fb[:, 0:n].rearrange("p (k c) -> p k c", k=nk), src3, w=fk, stream="Fbg%d" % (k % 2))

            def _X(self, k):
                src3, nk, nb, dst4, mode, keys = self.pieces[k]
                fb, fk = self.F[k % 2]
                ob_, ok = self.O[k % 2]
                n = nk * nb * 128
                f4 = fb[:, 0:n].rearrange("p (k b c) -> p k b c", k=nk, b=nb)
                o4t = ob_[:, 0:n].rearrange("p (b k c) -> p b k c", b=nb, k=nk).rearrange("p b k c -> p k b c")
                if mode == "pre":
                    tr.op("dve", lambda: dve.tensor_tensor(out=o4t, in0=f4,
                                                           in1=prew[:, :].unsqueeze(2).unsqueeze(3).to_broadcast([128, nk, nb, 128]),
                                                           op=ALU.mult), r=fk + ["prew"], w=ok)
                elif mode == "sub":
                    tr.op("dve", lambda: dve.tensor_scalar(out=o4t, in0=f4, scalar1=subw[:, 0:1], scalar2=0.8, op0=ALU.mult,
                                                           op1=ALU.mult), r=fk + ["subw"], w=ok)
                elif mode == "half":
                    tr.op("dve", lambda: dve.tensor_scalar(out=o4t, in0=f4, scalar1=0.5, scalar2=None, op0=ALU.mult), r=fk, w=ok)
                else:
                    tr.op("dve", lambda: dve.tensor_copy(out=o4t, in_=f4), r=fk, w=ok)

            def _S(self, k):
                src3, nk, nb, dst4, mode, keys = self.pieces[k]
                ob_, ok = self.O[k % 2]
                n = nk * nb * 128
                o4 = ob_[:, 0:n].rearrange("p (b k c) -> p b k c", b=nb, k=nk)
                tr.dma("sp", dst4.rearrange("b p k c -> p b k c"), o4, r=ok, w=keys, stream="ws%d" % (k % 2))
                for kk in keys:
                    if kk.startswith("WSin"):
                        ready.add(int(kk[4:]))

            def tick(self, load=True):
                if self.done:
                    return
                n = len(self.pieces)
                if self.ks < self.kx:
                    self._S(self.ks)
                    self.ks += 1
                if self.kx < self.k:
                    self._X(self.kx)
                    self.kx += 1
                if load and self.k < n:
                    self._L(self.k)
                    self.k += 1
                if self.ks >= n:
                    self.done = True
                    bg_state["done"] = True
                    tr.op("dve", lambda: dve.memset(sm[:, 9:10], 0.0), w=["O0", "O1", "untok"])

            def drain(self):
                while not self.done and self.ks < self.k:
                    self.tick(load=False)

            def flush(self):
                while not self.done:
                    self.tick()

        bg = BgCast()

        tr.op("dve", lambda: dve.memset(cst[:, 12:13], 0.0), r=["cst"], w=["cst"])
        if stage >= 1:
            unit(0, 0, 6144, 0, pre_a=pre_a0, next_own0=(2048 if stage >= 5 else None))
        if stage >= 5:
            unit(1, 2048, None, 2048, pre_a=NormPipe(), next_own0=(4096 if stage >= 6 else None))
        if stage >= 6:
            unit(2, 4096, None, 4096, pre_a=NormPipe())
        if dbg is not None:
            dbg(nc, tr, locals())
        tr.finish([k for k in ("st0", "st1", "st2", "dbg") if k in tr.dsem])
    return nc


def _tables(t5_rel_bias, na_rpb, parity):
    tb = np.asarray(t5_rel_bias, np.float32)
    i = np.arange(128)[:, None]
    c = np.arange(1152)[None, :]
    t5own = np.ascontiguousarray(np.moveaxis(tb[t5_bucket_np(i - c + 512)], -1, 0))
    def pos_q(qp):
        return qp + 2048 * parity
    def pos_k_other(kp):
        return kp if parity == 0 else kp - 2048
    spec = []
    for (kt, qc) in ((16, 3), (31, 0)):
        kp = kt * 128 + np.arange(128)[:, None]
        qp = qc * 512 + np.arange(512)[None, :]
        rel = pos_k_other(kp) - pos_q(qp)
        spec.append(np.moveaxis(tb[t5_bucket_np(rel)], -1, 0))
    t5spec = np.ascontiguousarray(np.stack(spec, 0))
    far_neg = tb[t5_bucket_np(np.array(-1000))]
    far_pos = tb[t5_bucket_np(np.array(1000))]
    oth = far_pos if parity == 0 else far_neg
    t5c = np.zeros((1, 16), np.float32)
    for h in range(4):
        t5c[0, 3 * h + 0] = far_neg[h]
        t5c[0, 3 * h + 1] = far_pos[h]
        t5c[0, 3 * h + 2] = oth[h]
    rpb = np.asarray(na_rpb, np.float32)
    j = np.arange(2)[:, None, None, None]
    kc = np.arange(64)[None, :, None, None]
    s = np.arange(16)[None, None, :, None]
    cc = np.arange(64)[None, None, None, :]
    dr = j + 14 - s + 0 * kc + 0 * cc
    cs0 = np.clip(cc - 8, 0, 48)
    dcv = kc - cc + 15 + 0 * j + 0 * s
    valid = (dr >= 0) & (dr <= 14) & (kc >= cs0) & (kc < cs0 + 16) & (dcv >= 0) & (dcv <= 30)
    drc, dcc = np.clip(dr, 0, 14), np.clip(dcv, 0, 30)
    natab = np.empty((8, 128, 1024), np.float32)
    natabi = np.empty((8, 128, 1024), np.float32)
    valid_i = valid & (dr >= 3) & (dr <= 10)
    for h in range(8):
        g = rpb[h][drc, dcc]
        natab[h] = np.where(valid, g, np.float32(NEG)).reshape(128, 1024)
        natabi[h] = np.where(valid_i, g, np.float32(NEG)).reshape(128, 1024)
    return t5own, t5spec, t5c, natab, natabi


def _mask(halo, parity):
    ntile = 20 if halo else 16
    m = np.full((2, ntile, 32), NEG, np.float32)
    if halo:
        rows_abs, base = 64, 32 * parity
        oth_base = 32 * (1 - parity)
    else:
        rows_abs, base, oth_base = 32, 0, 0
    for t in range(ntile):
        for j in range(2):
            if t < 16:
                ka = base + 2 * t + j
            elif t < 18:
                ka = oth_base + 2 * (t - 16) + j
            else:
                ka = oth_base + 28 + 2 * (t - 18) + j
            for r in range(32):
                ra = base + r
                st = min(max(ra - 4, 0), rows_abs - 8)
                if st <= ka < st + 8:
                    m[j, t, r] = 0.0
    mp = np.zeros((128, ntile * 32), np.float32)
    mp[0:2] = m.reshape(2, ntile * 32)
    return mp


_PROGRAM = None
_HOOK = None


def kernel(x_prompt, x_sample, t5_rel_bias, pre_norm_w, post_norm_w, w_in, lambda_q1, lambda_k1, lambda_q2,
           lambda_k2, subln_w, na_rpb, w_o_diff, w_o_na, w_out):
    global _PROGRAM
    f = lambda a: np.ascontiguousarray(np.asarray(a, np.float32))
    x_prompt, x_sample = f(x_prompt), f(x_sample)
    w_in0, w_od0, w_on0, w_out0 = f(w_in)[0], f(w_o_diff)[0], f(w_o_na)[0], f(w_out)[0]
    prew = np.ascontiguousarray(f(pre_norm_w)[0].reshape(8, 128).T)
    postw = f(post_norm_w)[0].reshape(1, 1024)
    subw = f(subln_w)[0].reshape(128, 1)
    lamv = np.concatenate([f(lambda_q1)[0], f(lambda_k1)[0], f(lambda_q2)[0], f(lambda_k2)[0]]).reshape(1, 256)
    ident = np.eye(128, dtype=np.float32)
    e2 = np.zeros((128, 128), np.float32)
    e2[0, 0:64] = 1.0
    e2[1, 64:128] = 1.0
    m2s = _mask(False, 0)
    in_maps = []
    for c in range(NCORES):
        p, par = c // 2, c % 2
        own = x_prompt[p, par * 2048:(par + 1) * 2048]
        oth = x_prompt[p, (1 - par) * 2048:(2 - par) * 2048]
        x_all = np.ascontiguousarray(np.concatenate([own, x_sample[2 * c], x_sample[2 * c + 1], oth], 0))
        t5own, t5spec, t5c, natab, natabi = _tables(f(t5_rel_bias), f(na_rpb)[0], par)
        in_maps.append({
            "x_all": x_all, "w_in": w_in0, "w_od": w_od0, "w_on": w_on0, "w_out": w_out0, "prew": prew, "postw": postw,
            "subw": subw, "lamv": lamv, "ident": ident, "e2": e2, "t5own": t5own, "t5spec": t5spec, "t5c": t5c,
            "natab": natab, "natabi": natabi, "m2p": _mask(True, par), "m2s": m2s,
        })
    if _PROGRAM is None:
        _PROGRAM = build_program()
    if _HOOK is not None:
        return _HOOK(in_maps)
    res = run_bass_kernel_spmd(_PROGRAM, in_maps, core_ids=list(range(NCORES)))
    y_prompt = np.empty((4, 4096, 1024), np.float32)
    y_sample = np.empty((16, 2048, 1024), np.float32)
    for c in range(NCORES):
        y = res.results[c]["y_all"]
        p, par = c // 2, c % 2
        y_prompt[p, par * 2048:(par + 1) * 2048] = y[0:2048]
        y_sample[2 * c] = y[2048:4096]
        y_sample[2 * c + 1] = y[4096:6144]
    return (y_prompt, y_sample)
```

```python
import math
from contextlib import ExitStack

import numpy as np
import concourse.bass as bass
import concourse.mybir as mybir
from concourse.bass_utils import run_bass_kernel_spmd

F32, BF16 = mybir.dt.float32, mybir.dt.bfloat16
AF = mybir.ActivationFunctionType
ALU = mybir.AluOpType
AX = mybir.AxisListType
NEG = -30000.0
T = 2048
NCORES = 8
STQ = "sp"


class Ev:
    __slots__ = ("s", "v")

    def __init__(self, s, v):
        self.s, self.v = s, v


class Tracker:
    def __init__(self, nc, es):
        self.nc = nc
        self.es = es
        self.eng = {"pe": nc.tensor, "act": nc.scalar, "dve": nc.vector, "pool": nc.gpsimd, "sp": nc.sync}
        self.sem = {e: es.enter_context(nc.semaphore("s_" + e)) for e in ("pe", "act", "dve", "pool")}
        self.cnt = {e: 0 for e in self.sem}
        self.pending = {e: [] for e in self.sem}
        self.dsem, self.dcnt = {}, {}
        self.lastw, self.readers = {}, {}
        self.known = {e: {} for e in self.eng}

    def _semobj(self, s):
        return self.sem[s] if s in self.sem else self.dsem[s]

    def _deps(self, e, r, w):
        deps = {}

        def add(ev):
            if ev is None:
                return
            if ev.s == "pe" and e == "pe":
                return
            assert ev.v is not None, "dependency on unsignalled instruction"
            if deps.get(ev.s, 0) < ev.v:
                deps[ev.s] = ev.v

        for k in r:
            add(self.lastw.get(k))
        for k in w:
            add(self.lastw.get(k))
            for ev in self.readers.get(k, {}).values():
                add(ev)
        for s, v in deps.items():
            if self.known[e].get(s, 0) >= v:
                continue
            self.eng[e].wait_ge(self._semobj(s), v)
            self.known[e][s] = v

    def _record(self, ev, r, w):
        for k in r:
            self.readers.setdefault(k, {})[ev.s] = ev
        for k in w:
            self.lastw[k] = ev
            self.readers[k] = {}

    def op(self, e, fn, r=(), w=(), sig=True):
        self._deps(e, r, w)
        ins = fn()
        ev = Ev(e, None)
        if sig:
            self.cnt[e] += 1
            ins.then_inc(self.sem[e], 1)
            ev.v = self.cnt[e]
            for p in self.pending[e]:
                p.v = ev.v
            self.pending[e] = []
        else:
            self.pending[e].append(ev)
        self._record(ev, r, w)
        return ins

    def dma(self, q, out, in_, r=(), w=(), *, stream):
        if stream not in self.dsem:
            self.dsem[stream] = self.es.enter_context(self.nc.semaphore("d_" + stream))
            self.dcnt[stream] = 0
        self._deps(q, r, w)
        ins = self.eng[q].dma_start(out=out, in_=in_)
        self.dcnt[stream] += 16
        ins.then_inc(self.dsem[stream], 16)
        self._record(Ev(stream, self.dcnt[stream]), r, w)

    def barrier_on(self, stream, engines=("pe", "act", "dve", "pool", "sp")):
        for e in engines:
            self.eng[e].wait_ge(self.dsem[stream], self.dcnt[stream])
            self.known[e][stream] = self.dcnt[stream]

    def finish(self, streams):
        for s in streams:
            self.nc.sync.wait_ge(self.dsem[s], self.dcnt[s])


def na_tiles(halo):
    tl = []
    if halo:
        tl += [(18, -4), (19, -2)]
    tl += [(t, 2 * t) for t in range(16)]
    if halo:
        tl += [(16, 32), (17, 34)]
    return tl


def na_candidates(halo):
    def windows(r):
        ws = []
        if halo:
            s0 = min(max(r - 4, 0), 56)
            ws.append((s0, s0 + 7))
            s1 = min(max(r + 32 - 4, 0), 56) - 32
            ws.append((s1, s1 + 7))
        else:
            s0 = min(max(r - 4, 0), 24)
            ws.append((s0, s0 + 7))
        return ws

    out = {}
    for t, a in na_tiles(halo):
        rs = [r for r in range(32) if any(lo <= a + j <= hi for (lo, hi) in windows(r) for j in (0, 1))]
        r_lo, r_hi = min(rs), max(rs)
        r_lo -= r_lo % 2
        r_hi += 1 - (r_hi % 2)
        assert 0 <= r_lo - a + 7 and r_hi - a + 7 <= 15, (t, a, r_lo, r_hi)
        out[t] = (r_lo, r_hi)
    return out


def t5_bucket_np(rel):
    nb = 16
    ret = np.where(rel > 0, nb, 0)
    n = np.abs(rel)
    me = 8
    nf = np.maximum(n, 1).astype(np.float32)
    large = me + (np.log(nf / np.float32(me)) / np.float32(math.log(128 / me)) * np.float32(nb - me)).astype(np.int32)
    large = np.minimum(large, nb - 1)
    return ret + np.where(n < me, n, large)


def build_program(stage=99, dbg=None):
    nc = bass.Bass("TRN2", target_bir_lowering=False)

    def di(n, s, dt=F32):
        return nc.dram_tensor(n, s, dt, kind="ExternalInput").ap()

    x_all = di("x_all", [8192, 1024])
    w_in = di("w_in", [1024, 6144])
    w_od = di("w_od", [512, 1024])
    w_on = di("w_on", [512, 1024])
    w_out = di("w_out", [1024, 1024])
    prew_d = di("prew", [128, 8])
    postw_d = di("postw", [1, 1024])
    subw_d = di("subw", [128, 1])
    lamv_d = di("lamv", [1, 256])
    ident_d = di("ident", [128, 128])
    e2_d = di("e2", [128, 128])
    t5own_d = di("t5own", [4, 128, 1152])
    t5spec_d = di("t5spec", [2, 4, 128, 512])
    t5c_d = di("t5c", [1, 16])
    natab_d = di("natab", [8, 128, 1024])
    natabi_d = di("natabi", [8, 128, 1024])
    m2p_d = di("m2p", [128, 640])
    m2s_d = di("m2s", [128, 512])
    y_all = nc.dram_tensor("y_all", [6144, 1024], F32, kind="ExternalOutput").ap()
    wkind = "ExternalOutput" if dbg is not None else "Internal"
    ws_in = nc.dram_tensor("ws_in", [48, 128, 8, 128], BF16, kind=wkind).ap()
    ws_od = nc.dram_tensor("ws_od", [8, 128, 4, 128], BF16, kind=wkind).ap()
    ws_on = nc.dram_tensor("ws_on", [8, 128, 4, 128], BF16, kind=wkind).ap()
    ws_out = nc.dram_tensor("ws_out", [8, 128, 8, 128], BF16, kind=wkind).ap()

    with ExitStack() as es:
        def sb(n, s, dt):
            return es.enter_context(nc.sbuf_tensor("sb_" + n, s, dt))

        tr = Tracker(nc, es)
        PS = es.enter_context(nc.psum_tensor("PS", [128, 8, 512], F32))
        hT = sb("hT", [128, 8, T], BF16)
        uT = sb("uT", [128, 8, T], BF16)
        A = sb("A", [128, 20992], BF16)
        wst = [sb("wst%d" % i, [128, 4, 8, 128], BF16) for i in range(2)]
        wz = sb("wz", [128, 2, 8, 128], BF16)
        xs = [sb("xs%d" % i, [128, 1024], F32) for i in range(3)]
        hb = [sb("hb%d" % i, [128, 1024], BF16) for i in range(2)]
        hTo = sb("hTo", [128, 8, 512], BF16)
        PT = [sb("PT%d" % i, [128, 1024], BF16) for i in range(3)]
        accs = sb("accs", [128, 8, 129], F32)
        ob = sb("ob", [128, 4, 128], F32)
        ob2 = sb("ob2", [128, 4, 128], F32)
        onb = sb("onb", [128, 4, 128], BF16)
        thbuf = sb("thbuf", [128, 4, 512], F32)
        th = [thbuf[:, 0, :], thbuf[:, 1, :]]
        thb = [thbuf[:, 2, :], thbuf[:, 3, :]]
        xn = [xs[0][:], xs[1][:], thbuf[:, 0:2, :].rearrange("p a b -> p (a b)"), thbuf[:, 2:4, :].rearrange("p a b -> p (a b)")]
        xnk = [["xs0"], ["xs1"], ["th0", "th1"], ["thb0", "thb1"]]
        sz = sb("sz", [128, 512], BF16)
        untok = sb("untok", [128, 16, 256], BF16)
        t1 = ob[:].rearrange("p a e -> p (a e)")
        t2 = ob2[:].rearrange("p a e -> p (a e)")
        ytmp = accs[:].rearrange("p a e -> p (a e)")[:, 0:1024]
        postw = sb("postw", [128, 1024], F32)
        t5all = sb("t5all", [128, 4352], BF16)
        t5tab = t5all[:, 0:2304].rearrange("p (i c) -> p i c", i=2)
        t5sp = t5all[:, 2304:4352].rearrange("p (s i c) -> p s i c", s=2, i=2)
        natabi = t5all[:, 0:4096].rearrange("p (h c) -> p h c", h=4)
        NIK = ["t5tab", "t5sp"]
        natab = sb("natab", [128, 4, 1024], BF16)
        ident = sb("ident", [128, 128], BF16)
        e2 = sb("e2", [128, 128], BF16)
        m2p = sb("m2p", [128, 640], BF16)
        m2s = sb("m2s", [128, 512], BF16)
        sm = sb("sm", [128, 64], F32)
        lamv = sb("lamv", [128, 256], F32)
        cst = sb("cst", [128, 16], F32)
        prew = sb("prew", [128, 8], F32)
        subw = sb("subw", [128, 1], F32)
        mhalf = sb("mhalf", [128, 8], F32)
        rs = sb("rs", [128, 8], F32)
        ss4 = sb("ss4", [128, 4], F32)
        rs4 = sb("rs4", [128, 4], F32)
        stat = [sb("stat%d" % i, [128, 4], F32) for i in range(4)]
        nstat = sb("nstat", [128, 4, 4], F32)

        pe, act, dve, pool = nc.tensor, nc.scalar, nc.vector, nc.gpsimd
        bankkeys = ["B%d" % i for i in range(8)]
        bg_state = {"done": False}
        state = {"xi": 0, "bank": 0, "wi": 0, "st": 0, "pt": 0, "sb": 0, "xn": 0, "hb": 0, "yo": 0}
        AK = ["A.q0", "A.q1", "A.k0", "A.k1", "A.v"]

        def psflat(b0, nb):
            return PS[:, b0:b0 + nb, :].rearrange("p a b -> p (a b)")

        def psbf(b):
            return PS[:, b, :].bitcast(BF16)

        def ld(dst, src, key):
            tr.dma("sp", dst, src, w=[key], stream=(key if key.startswith("xs") else "c"))

        ld(xs[0][:, 0:128], ident_d[:, :], "xs0")
        tr.op("dve", lambda: dve.tensor_copy(out=ident[:], in_=xs[0][:, 0:128]), r=["xs0"], w=["ident"])
        ld(xs[1][:, 0:128], e2_d[:, :], "xs1")
        tr.op("dve", lambda: dve.tensor_copy(out=e2[:], in_=xs[1][:, 0:128]), r=["xs1"], w=["e2"])
        ld(xs[0][:, 0:640], m2p_d[:, :], "xs0")
        tr.op("dve", lambda: dve.tensor_copy(out=m2p[:], in_=xs[0][:, 0:640]), r=["xs0"], w=["m2p"])
        ld(xs[1][:, 0:512], m2s_d[:, :], "xs1")
        tr.op("dve", lambda: dve.tensor_copy(out=m2s[:], in_=xs[1][:, 0:512]), r=["xs1"], w=["m2s"])
        ld(prew[:], prew_d[:, :], "prew")
        ld(subw[:], subw_d[:, :], "subw")
        ld(postw[:], postw_d[0:1, :].partition_broadcast(128), "postw")
        ld(lamv[:], lamv_d[0:1, :].partition_broadcast(128), "lamv")
        ld(cst[:], t5c_d[0:1, :].partition_broadcast(128), "cst")
        tr.barrier_on("c")
        tr.op("pool", lambda: pool.memset(mhalf[:], -0.5), w=["mhalf"])
        lv = lamv[:].rearrange("p (a b d) -> p a b d", a=2, b=2)
        lp = xs[0][:, 0:128].rearrange("p (a d) -> p a d", a=2)
        tr.op("dve", lambda: dve.tensor_tensor(out=lp, in0=lv[:, :, 0, :], in1=lv[:, :, 1, :], op=ALU.mult),
              r=["lamv"], w=["xs0"])
        tr.op("dve", lambda: dve.tensor_reduce(out=sm[:, 2:4], in_=lp, axis=AX.X, op=ALU.add), r=["xs0"], w=["sm_l"])
        tr.op("act", lambda: act.activation(out=sm[:, 4:6], in_=sm[:, 2:4], func=AF.Exp), r=["sm_l"], w=["sm_e"])
        tr.op("dve", lambda: dve.tensor_tensor(out=sm[:, 6:7], in0=sm[:, 5:6], in1=sm[:, 4:5], op=ALU.subtract),
              r=["sm_e"], w=["sm_d"])
        tr.op("dve", lambda: dve.tensor_scalar(out=sm[:, 0:1], in0=sm[:, 6:7], scalar1=-0.2, scalar2=None, op0=ALU.add),
              r=["sm_d"], w=["nlam"])
        nlam = sm[:, 0:1]

        Fall = uT[:].rearrange("p a b -> p (a b)").bitcast(F32)
        Fbuf = [Fall[:, i * 4096:(i + 1) * 4096] for i in range(2)]
        Obuf = [A[:, j * 4096:(j + 1) * 4096] for j in range(2)]

        def cast_super(src3, nk, nb, dst4, mode, keys=()):
            i = state["xi"] % 2
            state["xi"] += 1
            j = state["wi"] % 2
            state["wi"] += 1
            n = nk * nb * 128
            fk, ok = "F%d" % i, ["O%d" % j]
            fv = Fbuf[i][:, 0:n].rearrange("p (k c) -> p k c", k=nk)
            tr.dma("sp", fv, src3, w=[fk], stream="F%d" % i)
            f4 = Fbuf[i][:, 0:n].rearrange("p (k b c) -> p k b c", k=nk, b=nb)
            o4 = Obuf[j][:, 0:n].rearrange("p (b k c) -> p b k c", b=nb, k=nk)
            o4t = o4.rearrange("p b k c -> p k b c")
            if mode == "pre" and (state["xi"] % 4) < 2:
                for k in range(nk):
                    tr.op("act", lambda k=k: act.activation(out=o4[:, :, k, :], in_=f4[:, k, :, :], func=AF.Copy,
                                                            scale=prew[:, k:k + 1]), r=[fk, "prew"], w=ok)
            elif mode == "pre":
                tr.op("dve", lambda: dve.tensor_tensor(out=o4t, in0=f4,
                                                       in1=prew[:, :].unsqueeze(2).unsqueeze(3).to_broadcast([128, nk, nb, 128]),
                                                       op=ALU.mult), r=[fk, "prew"], w=ok)
            elif mode == "sub":
                tr.op("dve", lambda: dve.tensor_scalar(out=o4t, in0=f4, scalar1=subw[:, 0:1], scalar2=0.8, op0=ALU.mult,
                                                       op1=ALU.mult), r=[fk, "subw"], w=ok)
            elif mode == "half":
                tr.op("act", lambda: act.activation(out=o4t, in_=f4, func=AF.Copy, scale=0.5), r=[fk], w=ok)
            else:
                tr.op("act", lambda: act.activation(out=o4t, in_=f4, func=AF.Copy), r=[fk], w=ok)
            tr.dma(STQ, dst4.rearrange("b p k c -> p b k c"), o4, r=ok, w=list(keys), stream="ws%d" % j)

        w_in3 = w_in.rearrange("(k p) c -> p k c", p=128)
        w_od3 = w_od.rearrange("(k p) c -> p k c", p=128)
        w_on3 = w_on.rearrange("(k p) c -> p k c", p=128)
        w_out3 = w_out.rearrange("(k p) c -> p k c", p=128)
        def next_bank(lo=0, hi=8):
            b = lo + state["bank"] % (hi - lo)
            state["bank"] += 1
            return b

        wcache = {}
        ready = set()
        late = []

        def _load_w_now(blocks):
            assert all(b in ready for b in blocks), ("scratch block loaded before its cast was emitted", blocks)
            j = state["wi"] % 2
            state["wi"] += 1
            for n, b in enumerate(blocks):
                tr.dma("sp", wst[j][:, n], ws_in[b], r=["WSin%d" % b], w=["wst%d" % j], stream="wst%d" % j)
            return j

        def load_w(blocks):
            key = tuple(blocks)
            if key in wcache:
                return wcache.pop(key)
            return _load_w_now(blocks)

        def prefetch_w(*lists):
            if wcache:
                return
            for blocks in lists:
                if any(b not in ready for b in blocks):
                    bg_state["retry"] = lists
                    return
            bg_state["retry"] = None
            for blocks in lists:
                wcache[tuple(blocks)] = _load_w_now(blocks)

        def run_late(n=1):
            for _ in range(n):
                if late:
                    late.pop(0)()

        class NormPipe:
            def __init__(self, ring=None):
                self.t, self.pa, self.pa2, self.pb, self.pc = [], 0, 0, 0, 0
                self.ring = ring

            def add(self, row0, dst, dkey, col0):
                self.t.append(dict(row0=row0, dst=dst, dkey=dkey, col0=col0, idx=len(self.t)))

            def _A1(self, T_):
                i = state["xn"] % 4
                state["xn"] += 1
                si = T_["idx"] % 4
                if self.ring is None:
                    xv, xk, strm = xn[i], xnk[i], "xn%d" % i
                else:
                    xv, xk, strm = self.ring[i]
                T_.update(i=i, si=si, xv=xv, xk=xk)
                sk = "nstat%d" % si
                tr.dma("sp", xv, x_all[T_["row0"]:T_["row0"] + 128, :], w=xk, stream=strm)
                tr.op("act", lambda: act.activation(out=PT[2][:], in_=xv, func=AF.Square, accum_out=nstat[:, si, 0:1]),
                      r=xk, w=["PT2", sk])

            def _A2(self, T_):
                si = T_["si"]
                sk = "nstat%d" % si
                tr.op("dve", lambda: dve.tensor_scalar(out=nstat[:, si, 1:2], in0=nstat[:, si, 0:1], scalar1=1.0 / 1024,
                                                       scalar2=1e-6, op0=ALU.mult, op1=ALU.add), r=[sk], w=[sk])
                tr.op("pool", lambda: pool.tensor_tensor(out=nstat[:, si, 2:3], in0=nstat[:, si, 1:2], in1=mhalf[:, 0:1],
                                                         op=ALU.pow), r=[sk, "mhalf"], w=[sk])

            def _B(self, T_):
                i, si = T_["i"], T_["si"]
                hi = state["hb"] % 2
                state["hb"] += 1
                hk = "hb%d" % hi
                tr.op("dve", lambda: dve.tensor_scalar(out=hb[hi][:], in0=T_["xv"], scalar1=nstat[:, si, 2:3], scalar2=None,
                                                       op0=ALU.mult), r=T_["xk"] + ["nstat%d" % si], w=[hk])
                bnk = next_bank()
                T_["bank"] = bnk
                tp = psbf(bnk)
                for k in range(8):
                    tr.op("pe", lambda k=k: pe.transpose(out=tp[:, k * 128:(k + 1) * 128], in_=hb[hi][:, k * 128:(k + 1) * 128],
                                                         identity=ident[:]), r=[hk, "ident"], w=[bankkeys[bnk]], sig=(k == 7))

            def _C(self, T_):
                bnk = T_["bank"]
                tr.op("dve", lambda: dve.tensor_copy(out=T_["dst"][:, :, T_["col0"]:T_["col0"] + 128],
                                                     in_=psbf(bnk)[:, :].rearrange("p (k c) -> p k c", k=8)),
                      r=[bankkeys[bnk]], w=[T_["dkey"]])

            def prefetch(self, upto):
                while self.pa < min(upto, len(self.t)) and self.pa - self.pb < 4:
                    self._A1(self.t[self.pa])
                    self.pa += 1
                while self.pa2 < self.pa:
                    self._A2(self.t[self.pa2])
                    self.pa2 += 1

            def run(self, upto=None, ahead=3):
                n = len(self.t)
                upto = n if upto is None else upto
                while self.pc < upto:
                    ah = max(ahead, 1) if self.pa <= self.pb and self.pa < upto else ahead
                    while self.pa < n and self.pa - self.pb < ah:
                        self._A1(self.t[self.pa])
                        self.pa += 1
                    if self.pa2 < self.pa and self.pa2 <= self.pb:
                        self._A2(self.t[self.pa2])
                        self.pa2 += 1
                    if self.pb < min(upto, self.pa2):
                        self._B(self.t[self.pb])
                        self.pb += 1
                    if self.pc < self.pb - 1 or (self.pb >= min(upto, self.pa) and self.pc < self.pb):
                        self._C(self.t[self.pc])
                        self.pc += 1
                    if self.pa2 < self.pa and self.pa2 <= self.pb:
                        self._A2(self.t[self.pa2])
                        self.pa2 += 1

        def proj_fm(wj, wn, src, skey, c0, ncols, dst_ap, dkey, scale, wkey=None):
            b = next_bank()
            wsrc = wst[wj] if wkey is None else wz
            for k in range(8):
                tr.op("pe", lambda k=k: pe.matmul(PS[:, b, 0:ncols], lhsT=wsrc[:, wn, k, :], rhs=src[:, k, c0:c0 + ncols],
                                                  start=(k == 0), stop=(k == 7)),
                      r=[("wst%d" % wj) if wkey is None else wkey, skey], w=[bankkeys[b]], sig=(k == 7))
            dk = list(dkey) if isinstance(dkey, (list, tuple)) else [dkey]
            if isinstance(dst_ap, tuple):
                for hf, d_ in enumerate(dst_ap):
                    tr.op("act", lambda hf=hf, d_=d_: act.activation(out=d_, in_=PS[64 * hf:64 * hf + 64, b, 0:ncols],
                                                                      func=AF.Copy, scale=scale), r=[bankkeys[b]], w=dk)
            else:
                tr.op("act", lambda: act.activation(out=dst_ap, in_=PS[:, b, 0:ncols], func=AF.Copy, scale=scale),
                      r=[bankkeys[b]], w=dk)
            state["pj"] = state.get("pj", 0) + 1
            if state["pj"] % 3 == 0:
                run_late()

        def proj_tm(wj, wn0, nblk, src, skey, c0, dst_ap3, dkey, nh, hd):
            b = next_bank()
            for k in range(8):
                tr.op("pe", lambda k=k: pe.matmul(PS[:, b, 0:nblk * 128], lhsT=src[:, k, c0:c0 + 128],
                                                  rhs=wst[wj][:, wn0:wn0 + nblk, k, :], start=(k == 0), stop=(k == 7)),
                      r=["wst%d" % wj, skey], w=[bankkeys[b]], sig=(k == 7))
            tr.op("act", lambda: act.activation(out=dst_ap3, in_=PS[:, b, 0:nblk * 128].rearrange("p (h d) -> p h d", h=nh),
                                                func=AF.Copy), r=[bankkeys[b]], w=[dkey])

        def silu_from_bank(b, dst_bf):
            tr.op("act", lambda: act.activation(out=th[0], in_=PS[:, b, :], func=AF.Tanh, scale=0.5),
                  r=[bankkeys[b]], w=["th0"])
            tr.op("dve", lambda: dve.tensor_scalar(out=th[0], in0=th[0], scalar1=0.5, scalar2=0.5, op0=ALU.mult,
                                                   op1=ALU.add), r=["th0"], w=["th0"])
            tr.op("dve", lambda: dve.tensor_tensor(out=dst_bf, in0=th[0], in1=PS[:, b, :], op=ALU.mult),
                  r=["th0", bankkeys[b]], w=["sz"])

        QTa = A[:, 0:4096].rearrange("p (h t) -> p h t", h=2)
        KTa = A[:, 4096:12288].rearrange("p (h t) -> p h t", h=2)
        Va = A[:, 12288:12288 + 32 * 2 * 129].rearrange("p (t h e) -> p t h e", t=32, h=2)
        QTz = A[:, 0:8192].rearrange("p (h t) -> p h t", h=4)
        KTn = A[:, 8192:8192 + 2 * 2560].rearrange("p (h t) -> p h t", h=2)
        Vn = A[:, 13312:13312 + 20 * 4 * 65].rearrange("p (t h e) -> p t h e", t=20, h=4)
        NQ, NKK = ["A.q0", "A.q1", "A.k0"], ["A.k1", "A.v"]
        Wod_sb = A[:, 0:4096].rearrange("p (b f c) -> p b f c", b=8, f=4)
        Won_sb = A[:, 4096:8192].rearrange("p (b f c) -> p b f c", b=8, f=4)
        Wout_sb = A[:, 8192:16384].rearrange("p (b k c) -> p b k c", b=8, k=8)

        def unit(u, own0, oth0, out0, pre_a=None, next_own0=None):
            halo = oth0 is not None
            nkt = 32 if halo else 16

            if pre_a is None:
                npipe = NormPipe()
                for t in range(16):
                    npipe.add(own0 + t * 128, hT, "hT%d" % (t // 4), t * 128)
                npipe.run()
            else:
                pre_a.run()

            if stage < 2:
                return
            for g in range(2):
                wj = load_w([2 * g, 2 * g + 1, 4 + 2 * g, 5 + 2 * g])
                wj2 = load_w([8 + 2 * g, 9 + 2 * g])
                assert {12 + 2 * g, 13 + 2 * g} <= ready
                tr.dma("sp", wz[:, 0], ws_in[12 + 2 * g], r=["WSin%d" % (12 + 2 * g)], w=["wz"], stream="wz")
                tr.dma("sp", wz[:, 1], ws_in[13 + 2 * g], r=["WSin%d" % (13 + 2 * g)], w=["wz"], stream="wz")
                npipe = None
                if halo:
                    npipe = NormPipe()
                    for c in range(4):
                        for t in range(4):
                            npipe.add(oth0 + (c * 4 + t) * 128, hTo, "hTo", t * 128)
                    npipe.prefetch(4)
                for c in range(4):
                    for i in range(2):
                        proj_fm(wj, i, hT, "hT%d" % c, c * 512, 512, QTa[:, i, c * 512:(c + 1) * 512], "A.q%d" % i, 0.125)
                        proj_fm(wj, 2 + i, hT, "hT%d" % c, c * 512, 512, KTa[:, i, c * 512:(c + 1) * 512], "A.k%d" % i, 1.0)
                    for t in range(4 * c, 4 * c + 4):
                        proj_tm(wj2, 0, 2, hT, "hT%d" % (t // 4), t * 128, Va[:, t, :, 0:128], "A.v", 2, 128)
                    if halo:
                        npipe.run(upto=4 * (c + 1))
                        for i in range(2):
                            proj_fm(wj, 2 + i, hTo, "hTo", 0, 512, KTa[:, i, 2048 + c * 512:2048 + (c + 1) * 512],
                                    "A.k%d" % i, 1.0)
                        for t in range(4):
                            proj_tm(wj2, 0, 2, hTo, "hTo", t * 128, Va[:, 16 + c * 4 + t, :, 0:128], "A.v", 2, 128)

                tr.op("dve", lambda: dve.memset(Va[:, 0:nkt, :, 128:129], 1.0), w=["A.v"])
                for i in range(2):
                    h = 2 * g + i
                    for part in range(2):
                        xi = state["xi"] % 2
                        state["xi"] += 1
                        tr.dma("sp", xs[xi][:, 0:576], t5own_d[h, :, part * 576:(part + 1) * 576], w=["xs%d" % xi],
                               stream="xs%d" % xi)
                        tr.op("dve", lambda xi=xi, i=i, part=part: dve.tensor_copy(
                            out=t5tab[:, i, part * 576:(part + 1) * 576], in_=xs[xi][:, 0:576]), r=["xs%d" % xi], w=["t5tab"])
                    if halo:
                        xi = state["xi"] % 2
                        state["xi"] += 1
                        tr.dma("sp", xs[xi][:, :].rearrange("p (s c) -> p s c", s=2), t5spec_d[:, h].rearrange("s p c -> p s c"),
                               w=["xs%d" % xi], stream="xs%d" % xi)
                        tr.op("dve", lambda xi=xi, i=i: dve.tensor_copy(
                            out=t5sp[:, :, i, :], in_=xs[xi][:, :].rearrange("p (s c) -> p s c", s=2)), r=["xs%d" % xi], w=["t5sp"])
                run_late(99)
                if g == 0:
                    prefetch_w([2, 3, 6, 7], [10, 11])
                else:
                    prefetch_w([16, 17, 20, 21], [24, 25])
                da_group(g, nkt, halo)

            if stage < 3:
                return
            bg.flush()
            cand = na_candidates(halo)
            tiles = na_tiles(halo)
            m2 = m2p if halo else m2s
            m2k = "m2p" if halo else "m2s"
            ntile_m2 = 20 if halo else 16
            m2v = m2[:, :].rearrange("p (t r) -> p t r", t=ntile_m2)
            for G in range(2):
                wj = load_w([16 + 2 * G, 17 + 2 * G, 20 + 2 * G, 21 + 2 * G])
                if G == 0:
                    for hh in range(4):
                        zr = slice(64, 128) if hh % 2 == 0 else slice(0, 64)
                        tr.op("dve", lambda hh=hh, zr=zr: dve.memset(QTz[zr, hh, :].bitcast(F32), 0.0), w=NQ)
                for i in range(2):
                    for c in range(4):
                        proj_fm(wj, 2 + i, hT, "hT%d" % c, c * 512, 512, KTn[:, i, c * 512:(c + 1) * 512], NKK, 1.0)
                for i in range(2):
                    for c in range(4):
                        proj_fm(wj, i, hT, "hT%d" % c, c * 512, 512,
                                (QTz[0:64, 2 * i, c * 512:(c + 1) * 512], QTz[64:128, 2 * i + 1, c * 512:(c + 1) * 512]), NQ, 0.125)
                for hh in range(4):
                    xi = state["xi"] % 2
                    state["xi"] += 1
                    tr.dma("sp", xs[xi][:], natab_d[4 * G + hh], w=["xs%d" % xi], stream="xs%d" % xi)
                    tr.op("dve", lambda xi=xi, hh=hh: dve.tensor_copy(out=natab[:, hh, :], in_=xs[xi][:]),
                          r=["xs%d" % xi], w=["natab"])
                    xi = state["xi"] % 2
                    state["xi"] += 1
                    tr.dma("sp", xs[xi][:], natabi_d[4 * G + hh], w=["xs%d" % xi], stream="xs%d" % xi)
                    tr.op("dve", lambda xi=xi, hh=hh: dve.tensor_copy(out=natabi[:, hh, :], in_=xs[xi][:]),
                          r=["xs%d" % xi], w=NIK)
                wj2 = load_w([24 + 2 * G, 25 + 2 * G])
                assert {28 + 2 * G, 29 + 2 * G} <= ready
                tr.dma("sp", wz[:, 0], ws_in[28 + 2 * G], r=["WSin%d" % (28 + 2 * G)], w=["wz"], stream="wz")
                tr.dma("sp", wz[:, 1], ws_in[29 + 2 * G], r=["WSin%d" % (29 + 2 * G)], w=["wz"], stream="wz")
                tr.op("dve", lambda: dve.memset(Vn[:, :, :, 64:65], 1.0), w=["A.v"])
                for t in range(16):
                    proj_tm(wj2, 0, 2, hT, "hT%d" % (t // 4), t * 128, Vn[:, t, :, 0:64], "A.v", 4, 64)
                if halo:
                    npipe = NormPipe()
                    for t, ot in enumerate([0, 1, 14, 15]):
                        npipe.add(oth0 + ot * 128, hTo, "hTo", t * 128)
                    npipe.run()
                    for i in range(2):
                        proj_fm(wj, 2 + i, hTo, "hTo", 0, 512, KTn[:, i, 2048:2560], NKK, 1.0)
                    for t in range(4):
                        proj_tm(wj2, 0, 2, hTo, "hTo", t * 128, Vn[:, 16 + t, :, 0:64], "A.v", 4, 64)

                first, last = {}, {}
                for (t, a) in tiles:
                    r_lo, r_hi = cand[t]
                    for qt in range(r_lo // 2, r_hi // 2 + 1):
                        first.setdefault(qt, t)
                        last[qt] = t
                nsteps = [(hh, t, a) for hh in range(4) for (t, a) in tiles]

                def na_qk(n):
                    hh, t, a = nsteps[n]
                    i, b0 = hh // 2, (hh % 2) * 64
                    r_lo, r_hi = cand[t]
                    nq = (r_hi - r_lo + 1) * 64
                    s = n % 2
                    s0 = r_lo - a + 7
                    chunks = [(c0, c1) for (c0, c1) in ((0, min(512, nq)), (512, nq)) if c1 > c0]
                    for ci, (c0, c1) in enumerate(chunks):
                        bk = [bankkeys[2 * s + ci]]
                        bnk = 2 * s + ci
                        ops = [("qk", c0, c1)]
                        ra, rb = r_lo + c0 // 64, r_lo + c1 // 64
                        if t < 16:
                            segs = [(ra, min(rb, 4), False), (max(ra, 4), min(rb, 29), True), (max(ra, 29), rb, False)]
                        else:
                            segs = [(ra, rb, False)]
                        for (x0, x1, interior) in segs:
                            if x1 <= x0:
                                continue
                            ops.append(("tabi" if interior else "tab", x0, x1))
                            if not interior:
                                ops.append(("mask", x0, x1))
                        for oi, (kind, x0, x1) in enumerate(ops):
                            lastop = (oi == len(ops) - 1)
                            sg = lastop and (ci == len(chunks) - 1)
                            if kind == "qk":
                                o = PS[:, bnk, 0:c1 - c0]
                                tr.op("pe", lambda o=o: pe.matmul(
                                    o, lhsT=KTn[:, i, t * 128:(t + 1) * 128],
                                    rhs=QTz[:, hh, r_lo * 64 + c0:r_lo * 64 + c1], start=True, stop=False),
                                    r=NKK + NQ, w=bk, sig=False)
                                continue
                            o = PS[:, bnk, (x0 - ra) * 64:(x1 - ra) * 64]
                            sa, sb_ = (x0 - a + 7) * 64, (x1 - a + 7) * 64
                            if kind == "tab":
                                tr.op("pe", lambda o=o, sa=sa, sb_=sb_, lastop=lastop: pe.matmul(
                                    o, lhsT=ident[:], rhs=natab[:, hh, sa:sb_], start=False, stop=lastop, skip_group_check=True),
                                    r=["ident", "natab"], w=bk, sig=sg)
                            elif kind == "tabi":
                                tr.op("pe", lambda o=o, sa=sa, sb_=sb_, lastop=lastop: pe.matmul(
                                    o, lhsT=ident[:], rhs=natabi[:, hh, sa:sb_], start=False, stop=lastop, skip_group_check=True),
                                    r=["ident"] + NIK, w=bk, sig=sg)
                            else:
                                tr.op("pe", lambda o=o, x0=x0, x1=x1, lastop=lastop: pe.matmul(
                                    o, lhsT=e2[:, :], rhs=m2v[:, t, x0:x1].unsqueeze(2).to_broadcast([128, x1 - x0, 64]),
                                    start=False, stop=lastop, skip_group_check=True), r=["e2", m2k], w=bk, sig=sg)

                gfirst, glast = {}, {}
                for (t, a) in tiles:
                    r_lo, r_hi = cand[t]
                    for qt in range(r_lo // 2, r_hi // 2 + 1):
                        gfirst.setdefault(qt // 4, t)
                        glast[qt // 4] = t
                gbank = {}

                def na_exp_pv(n):
                    hh, t, a = nsteps[n]
                    r_lo, r_hi = cand[t]
                    nq = (r_hi - r_lo + 1) * 64
                    s = n % 2
                    sk = [bankkeys[2 * s], bankkeys[2 * s + 1]] if nq > 512 else [bankkeys[2 * s]]
                    pi = state["pt"] % 3
                    state["pt"] += 1
                    pk = "PT%d" % pi
                    tr.op("act", lambda: act.activation(out=PT[pi][:, 0:nq], in_=psflat(2 * s, 2)[:, 0:nq], func=AF.Exp),
                          r=sk, w=[pk])
                    if n + 2 < len(nsteps):
                        na_qk(n + 2)
                    qts = list(range(r_lo // 2, r_hi // 2 + 1))
                    for qn, qt in enumerate(qts):
                        gb = qt // 4
                        fresh = False
                        if (hh, gb) not in gbank:
                            gbank[(hh, gb)] = 4 + state["sb"] % 3
                            state["sb"] += 1
                            fresh = True
                        bnk = gbank[(hh, gb)]
                        slot = qt % 4
                        dst = PS[:, bnk, slot * 65:slot * 65 + 65]
                        q0 = (qt - r_lo // 2) * 128
                        lastmm = (glast[gb] == t) and (qn == len(qts) - 1 or qts[qn + 1] // 4 != gb)
                        tr.op("pe", lambda dst=dst, q0=q0, fresh=fresh: pe.matmul(
                            dst, lhsT=PT[pi][:, q0:q0 + 128], rhs=Vn[:, t, hh, :], start=fresh, stop=(last[qt] == t),
                            skip_group_check=True), r=[pk, "A.v"], w=[bankkeys[bnk]], sig=lastmm)
                        if lastmm:
                            si = state["st"] % 4
                            state["st"] += 1
                            gv = PS[:, bnk, 0:260].rearrange("p (q e) -> p q e", q=4)
                            tr.op("dve", lambda gv=gv, si=si: dve.reciprocal(out=stat[si][:, 0:4], in_=gv[:, :, 64]),
                                  r=[bankkeys[bnk]], w=["stat%d" % si])
                            tr.op("dve", lambda gv=gv, si=si, gb=gb: dve.tensor_tensor(
                                out=untok[:, 4 * gb:4 * gb + 4, hh * 64:(hh + 1) * 64], in0=gv[:, :, 0:64],
                                in1=stat[si][:, 0:4].unsqueeze(2).to_broadcast([128, 4, 64]), op=ALU.mult),
                                r=[bankkeys[bnk], "stat%d" % si], w=["untok"])

                run_late(99)
                if G == 0:
                    prefetch_w([18, 19, 22, 23], [26, 27])
                else:
                    prefetch_w([32, 40], [33, 41])
                na_qk(0)
                if len(nsteps) > 1:
                    na_qk(1)
                for n in range(len(nsteps)):
                    na_exp_pv(n)
                for i in range(2):
                    for qc in range(4):
                        b = (7, 3)[qc % 2]
                        tb_ = (6, 2)[qc % 2]
                        for k in range(8):
                            tr.op("pe", lambda k=k: pe.matmul(PS[:, b, :], lhsT=wz[:, i, k, :], rhs=hT[:, k, qc * 512:(qc + 1) * 512],
                                                              start=(k == 0), stop=(k == 7)), r=["wz", "hT%d" % qc], w=[bankkeys[b]],
                                  sig=(k == 7))
                        silu_from_bank(b, sz[:])
                        tp = psbf(tb_)
                        for j in range(4):
                            tr.op("pe", lambda j=j: pe.transpose(out=tp[:, j * 128:(j + 1) * 128],
                                                                 in_=untok[:, qc * 4 + j, i * 128:(i + 1) * 128], identity=ident[:]),
                                  r=["untok", "ident"], w=[bankkeys[tb_]], sig=(j == 3))
                        tr.op("dve", lambda: dve.tensor_tensor(out=uT[:, 4 + 2 * G + i, qc * 512:(qc + 1) * 512], in0=tp[:, 0:512],
                                                               in1=sz[:], op=ALU.mult), r=[bankkeys[tb_], "sz"], w=["uT"])

            if stage < 4:
                return
            run_late(99)
            tr.dma("sp", Wod_sb, ws_od.rearrange("b p f c -> p b f c"), r=["WSod0", "WSod1"], w=AK, stream="A")
            tr.dma("sp", Won_sb, ws_on.rearrange("b p f c -> p b f c"), r=["WSon0", "WSon1"], w=AK, stream="A")
            for b in range(0, 8, 4):
                tr.dma("sp", Wout_sb[:, b:b + 4], ws_out[b:b + 4].rearrange("b p k c -> p b k c"),
                       r=["WSout%d" % (b // 2), "WSout%d" % (b // 2 + 1)], w=["A.out"], stream="Aout")
            mTb = [hTo, t5all[:, 0:4096].rearrange("p (d t) -> p d t", d=8)]
            mkeys = [["hTo"], NIK]
            pend_tiles = []
            pre = []
            ystore = []
            ypost = []
            if (32, 40) in wcache and (33, 41) in wcache:
                pre = [wcache.pop((32, 40)), wcache.pop((33, 41))]
            nxt = None
            if next_own0 is not None:
                nflat = natab[:].rearrange("p a b -> p (a b)").bitcast(F32)
                uflat2 = untok[:].rearrange("p a b -> p (a b)").bitcast(F32)
                ring = [(nflat[:, 0:1024], ["natabA"], "xe0"), (nflat[:, 1024:2048], ["natabB"], "xe1"),
                        (uflat2[:, 0:1024], ["untokA"], "xe2"), (uflat2[:, 1024:2048], ["untokB"], "xe3")]
                tr.op("dve", lambda: dve.memset(sm[:, 10:11], 0.0), w=["natab", "untok", "natabA", "natabB", "untokA", "untokB"])
                nxt = NormPipe(ring)
                for t in range(16):
                    nxt.add(next_own0 + t * 128, hT, "hT%d" % (t // 4), t * 128)
            for c in range(4):
                cs = slice(c * 512, (c + 1) * 512)
                mT, mk = mTb[c % 2], mkeys[c % 2]
                if nxt is not None:
                    nxt.prefetch(4 * (c + 1))
                for dc in range(8):
                    wj = pre.pop(0) if pre else load_w([32 + dc, 40 + dc])
                    if not pre:
                        if dc < 7:
                            pre.append(load_w([33 + dc, 41 + dc]))
                        elif c < 3:
                            pre.append(load_w([32, 40]))
                    ba, bn, bga, bgb = [4 * (dc % 2) + x_ for x_ in range(4)]
                    ka, kn_, kga, kgb = bankkeys[ba], bankkeys[bn], bankkeys[bga], bankkeys[bgb]
                    for n, bb in ((0, bga), (1, bgb)):
                        for k in range(8):
                            tr.op("pe", lambda k=k, n=n, bb=bb: pe.matmul(PS[:, bb, :], lhsT=wst[wj][:, n, k, :], rhs=hT[:, k, cs],
                                                                          start=(k == 0), stop=(k == 7)),
                                  r=["wst%d" % wj, "hT%d" % c], w=[bankkeys[bb]], sig=(k == 7))
                    for f in range(4):
                        tr.op("pe", lambda f=f: pe.matmul(PS[:, ba, :], lhsT=Wod_sb[:, dc, f, :], rhs=uT[:, f, cs], start=(f == 0),
                                                          stop=(f == 3)), r=AK + ["uT"], w=[ka], sig=(f == 3))
                    for f in range(4):
                        tr.op("pe", lambda f=f: pe.matmul(PS[:, bn, :], lhsT=Won_sb[:, dc, f, :], rhs=uT[:, 4 + f, cs], start=(f == 0),
                                                          stop=(f == 3)), r=AK + ["uT"], w=[kn_], sig=(f == 3))
                    ti = dc % 2
                    tr.op("act", lambda: act.activation(out=th[ti], in_=PS[:, bga, :], func=AF.Tanh, scale=0.5), r=[kga],
                          w=["th%d" % ti])
                    tr.op("act", lambda: act.activation(out=thb[ti], in_=PS[:, bgb, :], func=AF.Tanh, scale=0.5), r=[kgb],
                          w=["thb%d" % ti])
                    tr.op("dve", lambda: dve.scalar_tensor_tensor(out=t1, in0=th[ti], scalar=1.0, in1=PS[:, ba, :], op0=ALU.add,
                                                                  op1=ALU.mult), r=["th%d" % ti, ka], w=["ob"])
                    tr.op("dve", lambda: dve.scalar_tensor_tensor(out=t2, in0=thb[ti], scalar=1.0, in1=PS[:, bn, :], op0=ALU.add,
                                                                  op1=ALU.mult), r=["thb%d" % ti, kn_], w=["ob2"])
                    tr.op("dve", lambda: dve.tensor_tensor(out=mT[:, dc, :], in0=t1, in1=t2, op=ALU.add), r=["ob", "ob2"],
                          w=mk)
                    if pend_tiles and dc % 2 == 0:
                        pend_tiles.pop(0)()
                if c == 3 and next_own0 is not None:
                    prefetch_w([0, 1, 4, 5], [8, 9])
                if nxt is not None:
                    nxt.run(upto=4 * (c + 1), ahead=0)
                def out_tile(c, t, mT, mk, pb, defer):
                    row = c * 512 + t * 128
                    i = state["yo"] % 3
                    state["yo"] += 1
                    xk = "xs%d" % i
                    tr.dma("sp", xs[i][:], x_all[own0 + row:own0 + row + 128, :], w=[xk], stream=xk)
                    for half in range(2):
                        bb = pb + half
                        for dc in range(8):
                            tr.op("pe", lambda dc=dc, half=half, bb=bb: pe.matmul(
                                PS[:, bb, :], lhsT=mT[:, dc, t * 128:(t + 1) * 128], rhs=Wout_sb[:, 4 * half:4 * half + 4, dc, :],
                                start=(dc == 0), stop=(dc == 7)), r=mk + ["A.out"] + AK, w=[bankkeys[bb]], sig=(dc == 7))
                    si = state["st"] % 4
                    state["st"] += 1
                    sk = "stat%d" % si
                    pk2 = [bankkeys[pb], bankkeys[pb + 1]]
                    tr.op("act", lambda si=si, pb=pb: act.activation(out=hb[0][:], in_=psflat(pb, 2), func=AF.Square,
                                                                     accum_out=stat[si][:, 0:1]), r=pk2, w=["hb0", sk])
                    if ystore:
                        ystore.pop(0)()
                    if ypost:
                        ypost.pop(0)()
                    tr.op("dve", lambda si=si: dve.tensor_scalar(out=stat[si][:, 1:2], in0=stat[si][:, 0:1], scalar1=1.0 / 1024,
                                                                 scalar2=1e-6, op0=ALU.mult, op1=ALU.add), r=[sk], w=[sk])
                    tr.op("pool", lambda si=si: pool.tensor_tensor(out=stat[si][:, 2:3], in0=stat[si][:, 1:2], in1=mhalf[:, 0:1],
                                                                   op=ALU.pow), r=[sk, "mhalf"], w=[sk])

                    def post2(si=si, pb=pb, i=i, xk=xk, row=row, sk=sk, pk2=pk2):
                        tr.op("dve", lambda: dve.scalar_tensor_tensor(out=ytmp, in0=psflat(pb, 2), scalar=stat[si][:, 2:3],
                                                                      in1=postw[:], op0=ALU.mult, op1=ALU.mult),
                              r=pk2 + [sk, "postw"], w=["accs"])
                        tr.op("dve", lambda: dve.tensor_tensor(out=xs[i][:], in0=ytmp, in1=xs[i][:], op=ALU.add), r=["accs", xk],
                              w=[xk])
                        ystore.append(lambda: tr.dma("act", y_all[out0 + row:out0 + row + 128, :], xs[i][:],
                                                     r=[xk], stream="st%d" % i))
                    ypost.append(post2)
                    if not defer:
                        while ypost:
                            ypost.pop(0)()

                if c < 3:
                    for t in range(4):
                        pend_tiles.append(lambda c=c, t=t, mT=mT, mk=mk: out_tile(c, t, mT, mk, 4, False))
                else:
                    for t in range(4):
                        out_tile(c, t, mT, mk, 2 * (t % 4), True)
                    while ypost:
                        ypost.pop(0)()
            while ystore:
                ystore.pop(0)()
            if nxt is not None:
                nxt.run()
                tr.op("dve", lambda: dve.memset(sm[:, 11:12], 0.0), w=["natabA", "natabB", "untokA", "untokB", "natab", "untok"])

        def da_group(g, nkt, halo):
            ACC = lambda a: PS[:, 4 + a // 3, (a % 3) * 129:(a % 3) * 129 + 129]
            steps = [(i, qc, kt) for i in range(2) for qc in range(4) for kt in range(nkt)]

            def bias_of(i, qc, kt):
                h = 2 * g + i
                if kt < 16:
                    d = kt * 128 - qc * 512
                    if d < -128:
                        return None, 3 * h + 0
                    if d > 512:
                        return None, 3 * h + 1
                    off = 512 - d
                    return t5tab[:, i, off:off + 512], 12
                if kt == 16 and qc == 3:
                    return t5sp[:, 0, i, :], 12
                if kt == 31 and qc == 0:
                    return t5sp[:, 1, i, :], 12
                return None, 3 * h + 2

            def qk(n):
                i, qc, kt = steps[n]
                s = n % 2
                tab, _ = bias_of(i, qc, kt)
                for m in range(2):
                    o = PS[:, 2 * s + m, :]
                    tr.op("pe", lambda o=o, m=m: pe.matmul(o, lhsT=KTa[64 * m:64 * m + 64, i, kt * 128:(kt + 1) * 128],
                                                           rhs=QTa[64 * m:64 * m + 64, i, qc * 512:(qc + 1) * 512], start=True,
                                                           stop=(tab is None)),
                          r=["A.k%d" % i, "A.q%d" % i], w=[bankkeys[2 * s + m]], sig=(tab is None and m == 1))
                if tab is not None:
                    for m in range(2):
                        o = PS[:, 2 * s + m, :]
                        tr.op("pe", lambda o=o: pe.matmul(o, lhsT=ident[:], rhs=tab, start=False, stop=True),
                              r=["ident", "t5tab", "t5sp"], w=[bankkeys[2 * s + m]], sig=(m == 1))

            EPI_LAG, BG_EVERY = 10, 12

            def zproj_mm(i, qc, k):
                tr.op("pe", lambda: pe.matmul(PS[:, 7, :], lhsT=wz[:, i, k, :], rhs=hT[:, k, qc * 512:(qc + 1) * 512],
                                              start=(k == 0), stop=(k == 7)), r=["wz", "hT%d" % qc], w=["B7"], sig=(k == 7))
                if k == 7:
                    silu_from_bank(7, sz[:])

            zlast = min(EPI_LAG + 8, nkt - 1)
            zsteps = list(range(EPI_LAG, zlast + 1))
            zplan = {kt_: [] for kt_ in zsteps}
            for k in range(8):
                zplan[zsteps[k * len(zsteps) // 8]].append(k)

            def epilogue_dve():
                for bnk, n in ((4, 3), (5, 3), (6, 2)):
                    a0 = (bnk - 4) * 3
                    tr.op("dve", lambda bnk=bnk, n=n, a0=a0: dve.tensor_copy(
                        out=accs[:, a0:a0 + n, :], in_=PS[:, bnk, 0:n * 129].rearrange("p (a e) -> p a e", a=n)),
                        r=[bankkeys[bnk]], w=["accs"])
                tr.op("dve", lambda: dve.reciprocal(out=rs[:], in_=accs[:, :, 128]), r=["accs"], w=["rs"])
                tr.op("dve", lambda: dve.tensor_scalar(out=rs[:, 4:8], in0=rs[:, 4:8], scalar1=nlam, scalar2=None, op0=ALU.mult),
                      r=["rs", "nlam"], w=["rs"])
                tr.op("dve", lambda: dve.tensor_tensor(out=ob[:], in0=accs[:, 0:4, 0:128],
                                                       in1=rs[:, 0:4].unsqueeze(2).to_broadcast([128, 4, 128]), op=ALU.mult),
                      r=["accs", "rs"], w=["ob"])
                tr.op("dve", lambda: dve.tensor_tensor(out=ob2[:], in0=accs[:, 4:8, 0:128],
                                                       in1=rs[:, 4:8].unsqueeze(2).to_broadcast([128, 4, 128]), op=ALU.mult),
                      r=["accs", "rs"], w=["ob2"])
                tr.op("dve", lambda: dve.tensor_tensor(out=ob[:], in0=ob[:], in1=ob2[:], op=ALU.add), r=["ob", "ob2"], w=["ob"])
                tr.op("dve", lambda: dve.tensor_tensor(out=ob2[:], in0=ob[:], in1=ob[:], op=ALU.mult), r=["ob"], w=["ob2"])
                tr.op("dve", lambda: dve.tensor_reduce(out=ss4[:], in_=ob2[:], axis=AX.X, op=ALU.add), r=["ob2"], w=["ss4"])
                tr.op("dve", lambda: dve.tensor_scalar(out=ss4[:], in0=ss4[:], scalar1=1.0 / 128, scalar2=1e-5, op0=ALU.mult,
                                                       op1=ALU.add), r=["ss4"], w=["ss4"])
                tr.op("pool", lambda: pool.tensor_tensor(out=rs4[:], in0=ss4[:], in1=mhalf[:, 0:4], op=ALU.pow), r=["ss4", "mhalf"],
                      w=["rs4"])
                tr.op("dve", lambda: dve.tensor_tensor(out=onb[:], in0=ob[:], in1=rs4[:].unsqueeze(2).to_broadcast([128, 4, 128]),
                                                       op=ALU.mult), r=["ob", "rs4"], w=["onb"])

            def epilogue_pe(i, qc):
                h = 2 * g + i
                tp = psbf(7)
                for j in range(4):
                    tr.op("pe", lambda j=j: pe.transpose(out=tp[:, j * 128:(j + 1) * 128], in_=onb[:, j, :], identity=ident[:]),
                          r=["onb", "ident", "sz"], w=["B7"], sig=(j == 3))
                tr.op("dve", lambda: dve.tensor_tensor(out=uT[:, h, qc * 512:(qc + 1) * 512], in0=tp[:, 0:512], in1=sz[:],
                                                       op=ALU.mult), r=["B7", "sz"], w=["uT"])

            deferred = []
            qk(0)
            if len(steps) > 1:
                qk(1)
            for n, (i, qc, kt) in enumerate(steps):
                s = n % 2
                _, bcol = bias_of(i, qc, kt)
                pi = state["pt"] % 3
                state["pt"] += 1
                pk = "PT%d" % pi
                tr.op("act", lambda s=s, pi=pi, bcol=bcol: act.activation(out=PT[pi][:], in_=psflat(2 * s, 2), func=AF.Exp,
                                                                         bias=cst[:, bcol:bcol + 1], scale=1.0),
                      r=[bankkeys[2 * s], bankkeys[2 * s + 1], "cst"], w=[pk])
                if n + 2 < len(steps):
                    qk(n + 2)
                for m in range(2):
                    for j in range(4):
                        a = m * 4 + j
                        tr.op("pe", lambda a=a, m=m, j=j, pi=pi: pe.matmul(
                            ACC(a), lhsT=PT[pi][:, m * 512 + j * 128:m * 512 + (j + 1) * 128], rhs=Va[:, kt, i, :],
                            start=(kt == 0 and a % 3 == 0), stop=(kt == nkt - 1), skip_group_check=True),
                            r=[pk, "A.v"], w=[bankkeys[4 + a // 3]], sig=(kt == nkt - 1 and a in (2, 5, 7)))
                if deferred and deferred[0][0] <= n:
                    deferred.pop(0)[1]()
                for k in zplan.get(kt, ()):
                    zproj_mm(i, qc, k)
                if n % BG_EVERY == 3 and kt not in (nkt - 1, 0):
                    bg.tick()
                    if bg_state.get("retry"):
                        prefetch_w(*bg_state["retry"])
                if kt == nkt - 1:
                    epilogue_dve()
                    deferred.append((n + EPI_LAG, lambda i=i, qc=qc: epilogue_pe(i, qc)))
            while deferred:
                late.append(deferred.pop(0)[1])
            bg.drain()

        pre_a0 = NormPipe()
        if stage >= 1:
            for t in range(16):
                pre_a0.add(t * 128, hT, "hT%d" % (t // 4), t * 128)
        EARLY = [0, 4, 8, 12]
        for n_, b0 in enumerate(EARLY):
            cast_super(w_in3[:, :, b0 * 128:b0 * 128 + 256], 8, 2, ws_in[b0:b0 + 2], "pre",
                       keys=["WSin%d" % b0, "WSin%d" % (b0 + 1)])
            ready.update((b0, b0 + 1))
            pre_a0.run(upto=min(4 * (n_ + 1), len(pre_a0.t)))
        tr.op("dve", lambda: dve.memset(sm[:, 8:9], 0.0), w=["F0", "F1", "O0", "O1", "uT"] + AK)

        class BgCast:
            def __init__(self):
                self.pieces = []
                for b0 in [2, 6, 10, 14] + list(range(16, 48, 2)):
                    self.pieces.append((w_in3[:, :, b0 * 128:b0 * 128 + 256], 8, 2, ws_in[b0:b0 + 2], "pre",
                                        ["WSin%d" % b0, "WSin%d" % (b0 + 1)]))
                for hf in range(2):
                    self.pieces.append((w_od3[:, :, hf * 512:(hf + 1) * 512], 4, 4, ws_od[4 * hf:4 * hf + 4], "sub", ["WSod%d" % hf]))
                for hf in range(2):
                    self.pieces.append((w_on3[:, :, hf * 512:(hf + 1) * 512], 4, 4, ws_on[4 * hf:4 * hf + 4], "plain", ["WSon%d" % hf]))
                for q4 in range(4):
                    self.pieces.append((w_out3[:, :, q4 * 256:(q4 + 1) * 256], 8, 2, ws_out[2 * q4:2 * q4 + 2], "half", ["WSout%d" % q4]))
                self.F = [(hTo[:].rearrange("p a b -> p (a b)").bitcast(F32), ["hTo"]),
                          (natab[:].rearrange("p a b -> p (a b)").bitcast(F32), ["natab"])]
                uflat = untok[:].rearrange("p a b -> p (a b)")
                self.O = [(uflat[:, 0:2048], ["O0"]), (uflat[:, 2048:4096], ["O1"])]
                self.k = self.kx = self.ks = 0
                self.done = False

            def _L(self, k):
                src3, nk, nb, dst4, mode, keys = self.pieces[k]
                fb, fk = self.F[k % 2]
                n = nk * nb * 128
                tr.dma("sp", fb[:, 0:n].rearrange("p (k c) -> p k c", k=nk), src3, w=fk, stream="Fbg%d" % (k % 2))

            def _X(self, k):
                src3, nk, nb, dst4, mode, keys = self.pieces[k]
                fb, fk = self.F[k % 2]
                ob_, ok = self.O[k % 2]
                n = nk * nb * 128
                f4 = fb[:, 0:n].rearrange("p (k b c) -> p k b c", k=nk, b=nb)
                o4t = ob_[:, 0:n].rearrange("p (b k c) -> p b k c", b=nb, k=nk).rearrange("p b k c -> p k b c")
                if mode == "pre":
                    tr.op("dve", lambda: dve.tensor_tensor(out=o4t, in0=f4,
                                                           in1=prew[:, :].unsqueeze(2).unsqueeze(3).to_broadcast([128, nk, nb, 128]),
                                                           op=ALU.mult), r=fk + ["prew"], w=ok)
                elif mode == "sub":
                    tr.op("dve", lambda: dve.tensor_scalar(out=o4t, in0=f4, scalar1=subw[:, 0:1], scalar2=0.8, op0=ALU.mult,
                                                           op1=ALU.mult), r=fk + ["subw"], w=ok)
                elif mode == "half":
                    tr.op("dve", lambda: dve.tensor_scalar(out=o4t, in0=f4, scalar1=0.5, scalar2=None, op0=ALU.mult), r=fk, w=ok)
                else:
                    tr.op("dve", lambda: dve.tensor_copy(out=o4t, in_=f4), r=fk, w=ok)

            def _S(self, k):
                src3, nk, nb, dst4, mode, keys = self.pieces[k]
                ob_, ok = self.O[k % 2]
                n = nk * nb * 128
                o4 = ob_[:, 0:n].rearrange("p (b k c) -> p b k c", b=nb, k=nk)
                tr.dma("sp", dst4.rearrange("b p k c -> p b k c"), o4, r=ok, w=keys, stream="ws%d" % (k % 2))
                for kk in keys:
                    if kk.startswith("WSin"):
                        ready.add(int(kk[4:]))

            def tick(self, load=True):
                if self.done:
                    return
                n = len(self.pieces)
                if self.ks < self.kx:
                    self._S(self.ks)
                    self.ks += 1
                if self.kx < self.k:
                    self._X(self.kx)
                    self.kx += 1
                if load and self.k < n:
                    self._L(self.k)
                    self.k += 1
                if self.ks >= n:
                    self.done = True
                    bg_state["done"] = True
                    tr.op("dve", lambda: dve.memset(sm[:, 9:10], 0.0), w=["O0", "O1", "untok"])

            def drain(self):
                while not self.done and self.ks < self.k:
                    self.tick(load=False)

            def flush(self):
                while not self.done:
                    self.tick()

        bg = BgCast()

        tr.op("dve", lambda: dve.memset(cst[:, 12:13], 0.0), r=["cst"], w=["cst"])
        if stage >= 1:
            unit(0, 0, 6144, 0, pre_a=pre_a0, next_own0=(2048 if stage >= 5 else None))
        if stage >= 5:
            unit(1, 2048, None, 2048, pre_a=NormPipe(), next_own0=(4096 if stage >= 6 else None))
        if stage >= 6:
            unit(2, 4096, None, 4096, pre_a=NormPipe())
        if dbg is not None:
            dbg(nc, tr, locals())
        tr.finish([k for k in ("st0", "st1", "st2", "dbg") if k in tr.dsem])
    return nc


def _tables(t5_rel_bias, na_rpb, parity):
    tb = np.asarray(t5_rel_bias, np.float32)
    i = np.arange(128)[:, None]
    c = np.arange(1152)[None, :]
    t5own = np.ascontiguousarray(np.moveaxis(tb[t5_bucket_np(i - c + 512)], -1, 0))
    def pos_q(qp):
        return qp + 2048 * parity
    def pos_k_other(kp):
        return kp if parity == 0 else kp - 2048
    spec = []
    for (kt, qc) in ((16, 3), (31, 0)):
        kp = kt * 128 + np.arange(128)[:, None]
        qp = qc * 512 + np.arange(512)[None, :]
        rel = pos_k_other(kp) - pos_q(qp)
        spec.append(np.moveaxis(tb[t5_bucket_np(rel)], -1, 0))
    t5spec = np.ascontiguousarray(np.stack(spec, 0))
    far_neg = tb[t5_bucket_np(np.array(-1000))]
    far_pos = tb[t5_bucket_np(np.array(1000))]
    oth = far_pos if parity == 0 else far_neg
    t5c = np.zeros((1, 16), np.float32)
    for h in range(4):
        t5c[0, 3 * h + 0] = far_neg[h]
        t5c[0, 3 * h + 1] = far_pos[h]
        t5c[0, 3 * h + 2] = oth[h]
    rpb = np.asarray(na_rpb, np.float32)
    j = np.arange(2)[:, None, None, None]
    kc = np.arange(64)[None, :, None, None]
    s = np.arange(16)[None, None, :, None]
    cc = np.arange(64)[None, None, None, :]
    dr = j + 14 - s + 0 * kc + 0 * cc
    cs0 = np.clip(cc - 8, 0, 48)
    dcv = kc - cc + 15 + 0 * j + 0 * s
    valid = (dr >= 0) & (dr <= 14) & (kc >= cs0) & (kc < cs0 + 16) & (dcv >= 0) & (dcv <= 30)
    drc, dcc = np.clip(dr, 0, 14), np.clip(dcv, 0, 30)
    natab = np.empty((8, 128, 1024), np.float32)
    natabi = np.empty((8, 128, 1024), np.float32)
    valid_i = valid & (dr >= 3) & (dr <= 10)
    for h in range(8):
        g = rpb[h][drc, dcc]
        natab[h] = np.where(valid, g, np.float32(NEG)).reshape(128, 1024)
        natabi[h] = np.where(valid_i, g, np.float32(NEG)).reshape(128, 1024)
    return t5own, t5spec, t5c, natab, natabi


def _mask(halo, parity):
    ntile = 20 if halo else 16
    m = np.full((2, ntile, 32), NEG, np.float32)
    if halo:
        rows_abs, base = 64, 32 * parity
        oth_base = 32 * (1 - parity)
    else:
        rows_abs, base, oth_base = 32, 0, 0
    for t in range(ntile):
        for j in range(2):
            if t < 16:
                ka = base + 2 * t + j
            elif t < 18:
                ka = oth_base + 2 * (t - 16) + j
            else:
                ka = oth_base + 28 + 2 * (t - 18) + j
            for r in range(32):
                ra = base + r
                st = min(max(ra - 4, 0), rows_abs - 8)
                if st <= ka < st + 8:
                    m[j, t, r] = 0.0
    mp = np.zeros((128, ntile * 32), np.float32)
    mp[0:2] = m.reshape(2, ntile * 32)
    return mp


_PROGRAM = None
_HOOK = None


def kernel(x_prompt, x_sample, t5_rel_bias, pre_norm_w, post_norm_w, w_in, lambda_q1, lambda_k1, lambda_q2,
           lambda_k2, subln_w, na_rpb, w_o_diff, w_o_na, w_out):
    global _PROGRAM
    f = lambda a: np.ascontiguousarray(np.asarray(a, np.float32))
    x_prompt, x_sample = f(x_prompt), f(x_sample)
    w_in0, w_od0, w_on0, w_out0 = f(w_in)[0], f(w_o_diff)[0], f(w_o_na)[0], f(w_out)[0]
    prew = np.ascontiguousarray(f(pre_norm_w)[0].reshape(8, 128).T)
    postw = f(post_norm_w)[0].reshape(1, 1024)
    subw = f(subln_w)[0].reshape(128, 1)
    lamv = np.concatenate([f(lambda_q1)[0], f(lambda_k1)[0], f(lambda_q2)[0], f(lambda_k2)[0]]).reshape(1, 256)
    ident = np.eye(128, dtype=np.float32)
    e2 = np.zeros((128, 128), np.float32)
    e2[0, 0:64] = 1.0
    e2[1, 64:128] = 1.0
    m2s = _mask(False, 0)
    in_maps = []
    for c in range(NCORES):
        p, par = c // 2, c % 2
        own = x_prompt[p, par * 2048:(par + 1) * 2048]
        oth = x_prompt[p, (1 - par) * 2048:(2 - par) * 2048]
        x_all = np.ascontiguousarray(np.concatenate([own, x_sample[2 * c], x_sample[2 * c + 1], oth], 0))
        t5own, t5spec, t5c, natab, natabi = _tables(f(t5_rel_bias), f(na_rpb)[0], par)
        in_maps.append({
            "x_all": x_all, "w_in": w_in0, "w_od": w_od0, "w_on": w_on0, "w_out": w_out0, "prew": prew, "postw": postw,
            "subw": subw, "lamv": lamv, "ident": ident, "e2": e2, "t5own": t5own, "t5spec": t5spec, "t5c": t5c,
            "natab": natab, "natabi": natabi, "m2p": _mask(True, par), "m2s": m2s,
        })
    if _PROGRAM is None:
        _PROGRAM = build_program()
    if _HOOK is not None:
        return _HOOK(in_maps)
    res = run_bass_kernel_spmd(_PROGRAM, in_maps, core_ids=list(range(NCORES)))
    y_prompt = np.empty((4, 4096, 1024), np.float32)
    y_sample = np.empty((16, 2048, 1024), np.float32)
    for c in range(NCORES):
        y = res.results[c]["y_all"]
        p, par = c // 2, c % 2
        y_prompt[p, par * 2048:(par + 1) * 2048] = y[0:2048]
        y_sample[2 * c] = y[2048:4096]
        y_sample[2 * c + 1] = y[4096:6144]
    return (y_prompt, y_sample)
```

```python
import math
from contextlib import ExitStack

import numpy as np
import concourse.bass as bass
import concourse.mybir as mybir
from concourse.bass_utils import run_bass_kernel_spmd

F32, BF16 = mybir.dt.float32, mybir.dt.bfloat16
AF = mybir.ActivationFunctionType
ALU = mybir.AluOpType
AX = mybir.AxisListType
NEG = -30000.0
T = 2048
NCORES = 8
STQ = "sp"


class Ev:
    __slots__ = ("s", "v")

    def __init__(self, s, v):
        self.s, self.v = s, v


class Tracker:
    def __init__(self, nc, es):
        self.nc = nc
        self.es = es
        self.eng = {"pe": nc.tensor, "act": nc.scalar, "dve": nc.vector, "pool": nc.gpsimd, "sp": nc.sync}
        self.sem = {e: es.enter_context(nc.semaphore("s_" + e)) for e in ("pe", "act", "dve", "pool")}
        self.cnt = {e: 0 for e in self.sem}
        self.pending = {e: [] for e in self.sem}
        self.dsem, self.dcnt = {}, {}
        self.lastw, self.readers = {}, {}
        self.known = {e: {} for e in self.eng}

    def _semobj(self, s):
        return self.sem[s] if s in self.sem else self.dsem[s]

    def _deps(self, e, r, w):
        deps = {}

        def add(ev):
            if ev is None:
                return
            if ev.s == "pe" and e == "pe":
                return
            assert ev.v is not None, "dependency on unsignalled instruction"
            if deps.get(ev.s, 0) < ev.v:
                deps[ev.s] = ev.v

        for k in r:
            add(self.lastw.get(k))
        for k in w:
            add(self.lastw.get(k))
            for ev in self.readers.get(k, {}).values():
                add(ev)
        for s, v in deps.items():
            if self.known[e].get(s, 0) >= v:
                continue
            self.eng[e].wait_ge(self._semobj(s), v)
            self.known[e][s] = v

    def _record(self, ev, r, w):
        for k in r:
            self.readers.setdefault(k, {})[ev.s] = ev
        for k in w:
            self.lastw[k] = ev
            self.readers[k] = {}

    def op(self, e, fn, r=(), w=(), sig=True):
        self._deps(e, r, w)
        ins = fn()
        ev = Ev(e, None)
        if sig:
            self.cnt[e] += 1
            ins.then_inc(self.sem[e], 1)
            ev.v = self.cnt[e]
            for p in self.pending[e]:
                p.v = ev.v
            self.pending[e] = []
        else:
            self.pending[e].append(ev)
        self._record(ev, r, w)
        return ins

    def dma(self, q, out, in_, r=(), w=(), *, stream):
        if stream not in self.dsem:
            self.dsem[stream] = self.es.enter_context(self.nc.semaphore("d_" + stream))
            self.dcnt[stream] = 0
        self._deps(q, r, w)
        ins = self.eng[q].dma_start(out=out, in_=in_)
        self.dcnt[stream] += 16
        ins.then_inc(self.dsem[stream], 16)
        self._record(Ev(stream, self.dcnt[stream]), r, w)

    def barrier_on(self, stream, engines=("pe", "act", "dve", "pool", "sp")):
        for e in engines:
            self.eng[e].wait_ge(self.dsem[stream], self.dcnt[stream])
            self.known[e][stream] = self.dcnt[stream]

    def finish(self, streams):
        for s in streams:
            self.nc.sync.wait_ge(self.dsem[s], self.dcnt[s])


def na_tiles(halo):
    tl = []
    if halo:
        tl += [(18, -4), (19, -2)]
    tl += [(t, 2 * t) for t in range(16)]
    if halo:
        tl += [(16, 32), (17, 34)]
    return tl


def na_candidates(halo):
    def windows(r):
        ws = []
        if halo:
            s0 = min(max(r - 4, 0), 56)
            ws.append((s0, s0 + 7))
            s1 = min(max(r + 32 - 4, 0), 56) - 32
            ws.append((s1, s1 + 7))
        else:
            s0 = min(max(r - 4, 0), 24)
            ws.append((s0, s0 + 7))
        return ws

    out = {}
    for t, a in na_tiles(halo):
        rs = [r for r in range(32) if any(lo <= a + j <= hi for (lo, hi) in windows(r) for j in (0, 1))]
        r_lo, r_hi = min(rs), max(rs)
        r_lo -= r_lo % 2
        r_hi += 1 - (r_hi % 2)
        assert 0 <= r_lo - a + 7 and r_hi - a + 7 <= 15, (t, a, r_lo, r_hi)
        out[t] = (r_lo, r_hi)
    return out


def t5_bucket_np(rel):
    nb = 16
    ret = np.where(rel > 0, nb, 0)
    n = np.abs(rel)
    me = 8
    nf = np.maximum(n, 1).astype(np.float32)
    large = me + (np.log(nf / np.float32(me)) / np.float32(math.log(128 / me)) * np.float32(nb - me)).astype(np.int32)
    large = np.minimum(large, nb - 1)
    return ret + np.where(n < me, n, large)


def build_program(stage=99, dbg=None):
    nc = bass.Bass("TRN2", target_bir_lowering=False)

    def di(n, s, dt=F32):
        return nc.dram_tensor(n, s, dt, kind="ExternalInput").ap()

    x_all = di("x_all", [8192, 1024])
    w_in = di("w_in", [1024, 6144])
    w_od = di("w_od", [512, 1024])
    w_on = di("w_on", [512, 1024])
    w_out = di("w_out", [1024, 1024])
    prew_d = di("prew", [128, 8])
    postw_d = di("postw", [1, 1024])
    subw_d = di("subw", [128, 1])
    lamv_d = di("lamv", [1, 256])
    ident_d = di("ident", [128, 128])
    e2_d = di("e2", [128, 128])
    t5own_d = di("t5own", [4, 128, 1152])
    t5spec_d = di("t5spec", [2, 4, 128, 512])
    t5c_d = di("t5c", [1, 16])
    natab_d = di("natab", [8, 128, 1024])
    natabi_d = di("natabi", [8, 128, 1024])
    m2p_d = di("m2p", [128, 640])
    m2s_d = di("m2s", [128, 512])
    y_all = nc.dram_tensor("y_all", [6144, 1024], F32, kind="ExternalOutput").ap()
    wkind = "ExternalOutput" if dbg is not None else "Internal"
    ws_in = nc.dram_tensor("ws_in", [48, 128, 8, 128], BF16, kind=wkind).ap()
    ws_od = nc.dram_tensor("ws_od", [8, 128, 4, 128], BF16, kind=wkind).ap()
    ws_on = nc.dram_tensor("ws_on", [8, 128, 4, 128], BF16, kind=wkind).ap()
    ws_out = nc.dram_tensor("ws_out", [8, 128, 8, 128], BF16, kind=wkind).ap()

    with ExitStack() as es:
        def sb(n, s, dt):
            return es.enter_context(nc.sbuf_tensor("sb_" + n, s, dt))

        tr = Tracker(nc, es)
        PS = es.enter_context(nc.psum_tensor("PS", [128, 8, 512], F32))
        hT = sb("hT", [128, 8, T], BF16)
        uT = sb("uT", [128, 8, T], BF16)
        A = sb("A", [128, 20992], BF16)
        wst = [sb("wst%d" % i, [128, 4, 8, 128], BF16) for i in range(2)]
        wz = sb("wz", [128, 2, 8, 128], BF16)
        xs = [sb("xs%d" % i, [128, 1024], F32) for i in range(3)]
        hb = [sb("hb%d" % i, [128, 1024], BF16) for i in range(2)]
        hTo = sb("hTo", [128, 8, 512], BF16)
        PT = [sb("PT%d" % i, [128, 1024], BF16) for i in range(3)]
        accs = sb("accs", [128, 8, 129], F32)
        ob = sb("ob", [128, 4, 128], F32)
        ob2 = sb("ob2", [128, 4, 128], F32)
        onb = sb("onb", [128, 4, 128], BF16)
        thbuf = sb("thbuf", [128, 4, 512], F32)
        th = [thbuf[:, 0, :], thbuf[:, 1, :]]
        thb = [thbuf[:, 2, :], thbuf[:, 3, :]]
        xn = [xs[0][:], xs[1][:], thbuf[:, 0:2, :].rearrange("p a b -> p (a b)"), thbuf[:, 2:4, :].rearrange("p a b -> p (a b)")]
        xnk = [["xs0"], ["xs1"], ["th0", "th1"], ["thb0", "thb1"]]
        sz = sb("sz", [128, 512], BF16)
        untok = sb("untok", [128, 16, 256], BF16)
        t1 = ob[:].rearrange("p a e -> p (a e)")
        t2 = ob2[:].rearrange("p a e -> p (a e)")
        ytmp = accs[:].rearrange("p a e -> p (a e)")[:, 0:1024]
        postw = sb("postw", [128, 1024], F32)
        t5all = sb("t5all", [128, 4352], BF16)
        t5tab = t5all[:, 0:2304].rearrange("p (i c) -> p i c", i=2)
        t5sp = t5all[:, 2304:4352].rearrange("p (s i c) -> p s i c", s=2, i=2)
        natabi = t5all[:, 0:4096].rearrange("p (h c) -> p h c", h=4)
        NIK = ["t5tab", "t5sp"]
        natab = sb("natab", [128, 4, 1024], BF16)
        ident = sb("ident", [128, 128], BF16)
        e2 = sb("e2", [128, 128], BF16)
        m2p = sb("m2p", [128, 640], BF16)
        m2s = sb("m2s", [128, 512], BF16)
        sm = sb("sm", [128, 64], F32)
        lamv = sb("lamv", [128, 256], F32)
        cst = sb("cst", [128, 16], F32)
        prew = sb("prew", [128, 8], F32)
        subw = sb("subw", [128, 1], F32)
        mhalf = sb("mhalf", [128, 8], F32)
        rs = sb("rs", [128, 8], F32)
        ss4 = sb("ss4", [128, 4], F32)
        rs4 = sb("rs4", [128, 4], F32)
        stat = [sb("stat%d" % i, [128, 4], F32) for i in range(4)]
        nstat = sb("nstat", [128, 4, 4], F32)

        pe, act, dve, pool = nc.tensor, nc.scalar, nc.vector, nc.gpsimd
        bankkeys = ["B%d" % i for i in range(8)]
        bg_state = {"done": False}
        state = {"xi": 0, "bank": 0, "wi": 0, "st": 0, "pt": 0, "sb": 0, "xn": 0, "hb": 0, "yo": 0}
        AK = ["A.q0", "A.q1", "A.k0", "A.k1", "A.v"]

        def psflat(b0, nb):
            return PS[:, b0:b0 + nb, :].rearrange("p a b -> p (a b)")

        def psbf(b):
            return PS[:, b, :].bitcast(BF16)

        def ld(dst, src, key):
            tr.dma("sp", dst, src, w=[key], stream=(key if key.startswith("xs") else "c"))

        ld(xs[0][:, 0:128], ident_d[:, :], "xs0")
        tr.op("dve", lambda: dve.tensor_copy(out=ident[:], in_=xs[0][:, 0:128]), r=["xs0"], w=["ident"])
        ld(xs[1][:, 0:128], e2_d[:, :], "xs1")
        tr.op("dve", lambda: dve.tensor_copy(out=e2[:], in_=xs[1][:, 0:128]), r=["xs1"], w=["e2"])
        ld(xs[0][:, 0:640], m2p_d[:, :], "xs0")
        tr.op("dve", lambda: dve.tensor_copy(out=m2p[:], in_=xs[0][:, 0:640]), r=["xs0"], w=["m2p"])
        ld(xs[1][:, 0:512], m2s_d[:, :], "xs1")
        tr.op("dve", lambda: dve.tensor_copy(out=m2s[:], in_=xs[1][:, 0:512]), r=["xs1"], w=["m2s"])
        ld(prew[:], prew_d[:, :], "prew")
        ld(subw[:], subw_d[:, :], "subw")
        ld(postw[:], postw_d[0:1, :].partition_broadcast(128), "postw")
        ld(lamv[:], lamv_d[0:1, :].partition_broadcast(128), "lamv")
        ld(cst[:], t5c_d[0:1, :].partition_broadcast(128), "cst")
        tr.barrier_on("c")
        tr.op("pool", lambda: pool.memset(mhalf[:], -0.5), w=["mhalf"])
        lv = lamv[:].rearrange("p (a b d) -> p a b d", a=2, b=2)
        lp = xs[0][:, 0:128].rearrange("p (a d) -> p a d", a=2)
        tr.op("dve", lambda: dve.tensor_tensor(out=lp, in0=lv[:, :, 0, :], in1=lv[:, :, 1, :], op=ALU.mult),
              r=["lamv"], w=["xs0"])
        tr.op("dve", lambda: dve.tensor_reduce(out=sm[:, 2:4], in_=lp, axis=AX.X, op=ALU.add), r=["xs0"], w=["sm_l"])
        tr.op("act", lambda: act.activation(out=sm[:, 4:6], in_=sm[:, 2:4], func=AF.Exp), r=["sm_l"], w=["sm_e"])
        tr.op("dve", lambda: dve.tensor_tensor(out=sm[:, 6:7], in0=sm[:, 5:6], in1=sm[:, 4:5], op=ALU.subtract),
              r=["sm_e"], w=["sm_d"])
        tr.op("dve", lambda: dve.tensor_scalar(out=sm[:, 0:1], in0=sm[:, 6:7], scalar1=-0.2, scalar2=None, op0=ALU.add),
              r=["sm_d"], w=["nlam"])
        nlam = sm[:, 0:1]

        Fall = uT[:].rearrange("p a b -> p (a b)").bitcast(F32)
        Fbuf = [Fall[:, i * 4096:(i + 1) * 4096] for i in range(2)]
        Obuf = [A[:, j * 4096:(j + 1) * 4096] for j in range(2)]

        def cast_super(src3, nk, nb, dst4, mode, keys=()):
            i = state["xi"] % 2
            state["xi"] += 1
            j = state["wi"] % 2
            state["wi"] += 1
            n = nk * nb * 128
            fk, ok = "F%d" % i, ["O%d" % j]
            fv = Fbuf[i][:, 0:n].rearrange("p (k c) -> p k c", k=nk)
            tr.dma("sp", fv, src3, w=[fk], stream="F%d" % i)
            f4 = Fbuf[i][:, 0:n].rearrange("p (k b c) -> p k b c", k=nk, b=nb)
            o4 = Obuf[j][:, 0:n].rearrange("p (b k c) -> p b k c", b=nb, k=nk)
            o4t = o4.rearrange("p b k c -> p k b c")
            if mode == "pre" and (state["xi"] % 4) < 2:
                for k in range(nk):
                    tr.op("act", lambda k=k: act.activation(out=o4[:, :, k, :], in_=f4[:, k, :, :], func=AF.Copy,
                                                            scale=prew[:, k:k + 1]), r=[fk, "prew"], w=ok)
            elif mode == "pre":
                tr.op("dve", lambda: dve.tensor_tensor(out=o4t, in0=f4,
                                                       in1=prew[:, :].unsqueeze(2).unsqueeze(3).to_broadcast([128, nk, nb, 128]),
                                                       op=ALU.mult), r=[fk, "prew"], w=ok)
            elif mode == "sub":
                tr.op("dve", lambda: dve.tensor_scalar(out=o4t, in0=f4, scalar1=subw[:, 0:1], scalar2=0.8, op0=ALU.mult,
                                                       op1=ALU.mult), r=[fk, "subw"], w=ok)
            elif mode == "half":
                tr.op("act", lambda: act.activation(out=o4t, in_=f4, func=AF.Copy, scale=0.5), r=[fk], w=ok)
            else:
                tr.op("act", lambda: act.activation(out=o4t, in_=f4, func=AF.Copy), r=[fk], w=ok)
            tr.dma(STQ, dst4.rearrange("b p k c -> p b k c"), o4, r=ok, w=list(keys), stream="ws%d" % j)

        w_in3 = w_in.rearrange("(k p) c -> p k c", p=128)
        w_od3 = w_od.rearrange("(k p) c -> p k c", p=128)
        w_on3 = w_on.rearrange("(k p) c -> p k c", p=128)
        w_out3 = w_out.rearrange("(k p) c -> p k c", p=128)
        def next_bank(lo=0, hi=8):
            b = lo + state["bank"] % (hi - lo)
            state["bank"] += 1
            return b

        wcache = {}
        ready = set()
        late = []

        def _load_w_now(blocks):
            assert all(b in ready for b in blocks), ("scratch block loaded before its cast was emitted", blocks)
            j = state["wi"] % 2
            state["wi"] += 1
            for n, b in enumerate(blocks):
                tr.dma("sp", wst[j][:, n], ws_in[b], r=["WSin%d" % b], w=["wst%d" % j], stream="wst%d" % j)
            return j

        def load_w(blocks):
            key = tuple(blocks)
            if key in wcache:
                return wcache.pop(key)
            return _load_w_now(blocks)

        def prefetch_w(*lists):
            if wcache:
                return
            for blocks in lists:
                if any(b not in ready for b in blocks):
                    bg_state["retry"] = lists
                    return
            bg_state["retry"] = None
            for blocks in lists:
                wcache[tuple(blocks)] = _load_w_now(blocks)

        def run_late(n=1):
            for _ in range(n):
                if late:
                    late.pop(0)()

        class NormPipe:
            def __init__(self, ring=None):
                self.t, self.pa, self.pa2, self.pb, self.pc = [], 0, 0, 0, 0
                self.ring = ring

            def add(self, row0, dst, dkey, col0):
                self.t.append(dict(row0=row0, dst=dst, dkey=dkey, col0=col0, idx=len(self.t)))

            def _A1(self, T_):
                i = state["xn"] % 4
                state["xn"] += 1
                si = T_["idx"] % 4
                if self.ring is None:
                    xv, xk, strm = xn[i], xnk[i], "xn%d" % i
                else:
                    xv, xk, strm = self.ring[i]
                T_.update(i=i, si=si, xv=xv, xk=xk)
                sk = "nstat%d" % si
                tr.dma("sp", xv, x_all[T_["row0"]:T_["row0"] + 128, :], w=xk, stream=strm)
                tr.op("act", lambda: act.activation(out=PT[2][:], in_=xv, func=AF.Square, accum_out=nstat[:, si, 0:1]),
                      r=xk, w=["PT2", sk])

            def _A2(self, T_):
                si = T_["si"]
                sk = "nstat%d" % si
                tr.op("dve", lambda: dve.tensor_scalar(out=nstat[:, si, 1:2], in0=nstat[:, si, 0:1], scalar1=1.0 / 1024,
                                                       scalar2=1e-6, op0=ALU.mult, op1=ALU.add), r=[sk], w=[sk])
                tr.op("pool", lambda: pool.tensor_tensor(out=nstat[:, si, 2:3], in0=nstat[:, si, 1:2], in1=mhalf[:, 0:1],
                                                         op=ALU.pow), r=[sk, "mhalf"], w=[sk])

            def _B(self, T_):
                i, si = T_["i"], T_["si"]
                hi = state["hb"] % 2
                state["hb"] += 1
                hk = "hb%d" % hi
                tr.op("dve", lambda: dve.tensor_scalar(out=hb[hi][:], in0=T_["xv"], scalar1=nstat[:, si, 2:3], scalar2=None,
                                                       op0=ALU.mult), r=T_["xk"] + ["nstat%d" % si], w=[hk])
                bnk = next_bank()
                T_["bank"] = bnk
                tp = psbf(bnk)
                for k in range(8):
                    tr.op("pe", lambda k=k: pe.transpose(out=tp[:, k * 128:(k + 1) * 128], in_=hb[hi][:, k * 128:(k + 1) * 128],
                                                         identity=ident[:]), r=[hk, "ident"], w=[bankkeys[bnk]], sig=(k == 7))

            def _C(self, T_):
                bnk = T_["bank"]
                tr.op("dve", lambda: dve.tensor_copy(out=T_["dst"][:, :, T_["col0"]:T_["col0"] + 128],
                                                     in_=psbf(bnk)[:, :].rearrange("p (k c) -> p k c", k=8)),
                      r=[bankkeys[bnk]], w=[T_["dkey"]])

            def prefetch(self, upto):
                while self.pa < min(upto, len(self.t)) and self.pa - self.pb < 4:
                    self._A1(self.t[self.pa])
                    self.pa += 1
                while self.pa2 < self.pa:
                    self._A2(self.t[self.pa2])
                    self.pa2 += 1

            def run(self, upto=None, ahead=3):
                n = len(self.t)
                upto = n if upto is None else upto
                while self.pc < upto:
                    ah = max(ahead, 1) if self.pa <= self.pb and self.pa < upto else ahead
                    while self.pa < n and self.pa - self.pb < ah:
                        self._A1(self.t[self.pa])
                        self.pa += 1
                    if self.pa2 < self.pa and self.pa2 <= self.pb:
                        self._A2(self.t[self.pa2])
                        self.pa2 += 1
                    if self.pb < min(upto, self.pa2):
                        self._B(self.t[self.pb])
                        self.pb += 1
                    if self.pc < self.pb - 1 or (self.pb >= min(upto, self.pa) and self.pc < self.pb):
                        self._C(self.t[self.pc])
                        self.pc += 1
                    if self.pa2 < self.pa and self.pa2 <= self.pb:
                        self._A2(self.t[self.pa2])
                        self.pa2 += 1

        def proj_fm(wj, wn, src, skey, c0, ncols, dst_ap, dkey, scale, wkey=None):
            b = next_bank()
            wsrc = wst[wj] if wkey is None else wz
            for k in range(8):
                tr.op("pe", lambda k=k: pe.matmul(PS[:, b, 0:ncols], lhsT=wsrc[:, wn, k, :], rhs=src[:, k, c0:c0 + ncols],
                                                  start=(k == 0), stop=(k == 7)),
                      r=[("wst%d" % wj) if wkey is None else wkey, skey], w=[bankkeys[b]], sig=(k == 7))
            dk = list(dkey) if isinstance(dkey, (list, tuple)) else [dkey]
            if isinstance(dst_ap, tuple):
                for hf, d_ in enumerate(dst_ap):
                    tr.op("act", lambda hf=hf, d_=d_: act.activation(out=d_, in_=PS[64 * hf:64 * hf + 64, b, 0:ncols],
                                                                      func=AF.Copy, scale=scale), r=[bankkeys[b]], w=dk)
            else:
                tr.op("act", lambda: act.activation(out=dst_ap, in_=PS[:, b, 0:ncols], func=AF.Copy, scale=scale),
                      r=[bankkeys[b]], w=dk)
            state["pj"] = state.get("pj", 0) + 1
            if state["pj"] % 3 == 0:
                run_late()

        def proj_tm(wj, wn0, nblk, src, skey, c0, dst_ap3, dkey, nh, hd):
            b = next_bank()
            for k in range(8):
                tr.op("pe", lambda k=k: pe.matmul(PS[:, b, 0:nblk * 128], lhsT=src[:, k, c0:c0 + 128],
                                                  rhs=wst[wj][:, wn0:wn0 + nblk, k, :], start=(k == 0), stop=(k == 7)),
                      r=["wst%d" % wj, skey], w=[bankkeys[b]], sig=(k == 7))
            tr.op("act", lambda: act.activation(out=dst_ap3, in_=PS[:, b, 0:nblk * 128].rearrange("p (h d) -> p h d", h=nh),
                                                func=AF.Copy), r=[bankkeys[b]], w=[dkey])

        def silu_from_bank(b, dst_bf):
            tr.op("act", lambda: act.activation(out=th[0], in_=PS[:, b, :], func=AF.Tanh, scale=0.5),
                  r=[bankkeys[b]], w=["th0"])
            tr.op("dve", lambda: dve.tensor_scalar(out=th[0], in0=th[0], scalar1=0.5, scalar2=0.5, op0=ALU.mult,
                                                   op1=ALU.add), r=["th0"], w=["th0"])
            tr.op("dve", lambda: dve.tensor_tensor(out=dst_bf, in0=th[0], in1=PS[:, b, :], op=ALU.mult),
                  r=["th0", bankkeys[b]], w=["sz"])

        QTa = A[:, 0:4096].rearrange("p (h t) -> p h t", h=2)
        KTa = A[:, 4096:12288].rearrange("p (h t) -> p h t", h=2)
        Va = A[:, 12288:12288 + 32 * 2 * 129].rearrange("p (t h e) -> p t h e", t=32, h=2)
        QTz = A[:, 0:8192].rearrange("p (h t) -> p h t", h=4)
        KTn = A[:, 8192:8192 + 2 * 2560].rearrange("p (h t) -> p h t", h=2)
        Vn = A[:, 13312:13312 + 20 * 4 * 65].rearrange("p (t h e) -> p t h e", t=20, h=4)
        NQ, NKK = ["A.q0", "A.q1", "A.k0"], ["A.k1", "A.v"]
        Wod_sb = A[:, 0:4096].rearrange("p (b f c) -> p b f c", b=8, f=4)
        Won_sb = A[:, 4096:8192].rearrange("p (b f c) -> p b f c", b=8, f=4)
        Wout_sb = A[:, 8192:16384].rearrange("p (b k c) -> p b k c", b=8, k=8)

        def unit(u, own0, oth0, out0, pre_a=None, next_own0=None):
            halo = oth0 is not None
            nkt = 32 if halo else 16

            if pre_a is None:
                npipe = NormPipe()
                for t in range(16):
                    npipe.add(own0 + t * 128, hT, "hT%d" % (t // 4), t * 128)
                npipe.run()
            else:
                pre_a.run()

            if stage < 2:
                return
            for g in range(2):
                wj = load_w([2 * g, 2 * g + 1, 4 + 2 * g, 5 + 2 * g])
                wj2 = load_w([8 + 2 * g, 9 + 2 * g])
                assert {12 + 2 * g, 13 + 2 * g} <= ready
                tr.dma("sp", wz[:, 0], ws_in[12 + 2 * g], r=["WSin%d" % (12 + 2 * g)], w=["wz"], stream="wz")
                tr.dma("sp", wz[:, 1], ws_in[13 + 2 * g], r=["WSin%d" % (13 + 2 * g)], w=["wz"], stream="wz")
                tr.op("dve", lambda: dve.memset(Va[:, 0:nkt, :, 128:129], 1.0), w=["A.v"])
                npipe = None
                if halo:
                    npipe = NormPipe()
                    for c in range(4):
                        for t in range(4):
                            npipe.add(oth0 + (c * 4 + t) * 128, hTo, "hTo", t * 128)
                    npipe.prefetch(4)
                for c in range(4):
                    for i in range(2):
                        proj_fm(wj, i, hT, "hT%d" % c, c * 512, 512, QTa[:, i, c * 512:(c + 1) * 512], "A.q%d" % i, 0.125)
                        proj_fm(wj, 2 + i, hT, "hT%d" % c, c * 512, 512, KTa[:, i, c * 512:(c + 1) * 512], "A.k%d" % i, 1.0)
                    for t in range(4 * c, 4 * c + 4):
                        proj_tm(wj2, 0, 2, hT, "hT%d" % (t // 4), t * 128, Va[:, t, :, 0:128], "A.v", 2, 128)
                    if halo:
                        npipe.run(upto=4 * (c + 1))
                        for i in range(2):
                            proj_fm(wj, 2 + i, hTo, "hTo", 0, 512, KTa[:, i, 2048 + c * 512:2048 + (c + 1) * 512],
                                    "A.k%d" % i, 1.0)
                        for t in range(4):
                            proj_tm(wj2, 0, 2, hTo, "hTo", t * 128, Va[:, 16 + c * 4 + t, :, 0:128], "A.v", 2, 128)

                for i in range(2):
                    h = 2 * g + i
                    for part in range(2):
                        xi = state["xi"] % 2
                        state["xi"] += 1
                        tr.dma("sp", xs[xi][:, 0:576], t5own_d[h, :, part * 576:(part + 1) * 576], w=["xs%d" % xi],
                               stream="xs%d" % xi)
                        tr.op("dve", lambda xi=xi, i=i, part=part: dve.tensor_copy(
                            out=t5tab[:, i, part * 576:(part + 1) * 576], in_=xs[xi][:, 0:576]), r=["xs%d" % xi], w=["t5tab"])
                    if halo:
                        xi = state["xi"] % 2
                        state["xi"] += 1
                        tr.dma("sp", xs[xi][:, :].rearrange("p (s c) -> p s c", s=2), t5spec_d[:, h].rearrange("s p c -> p s c"),
                               w=["xs%d" % xi], stream="xs%d" % xi)
                        tr.op("dve", lambda xi=xi, i=i: dve.tensor_copy(
                            out=t5sp[:, :, i, :], in_=xs[xi][:, :].rearrange("p (s c) -> p s c", s=2)), r=["xs%d" % xi], w=["t5sp"])
                run_late(99)
                if g == 0:
                    prefetch_w([2, 3, 6, 7], [10, 11])
                else:
                    prefetch_w([16, 17, 20, 21], [24, 25])
                da_group(g, nkt, halo)

            if stage < 3:
                return
            bg.flush()
            cand = na_candidates(halo)
            tiles = na_tiles(halo)
            m2 = m2p if halo else m2s
            m2k = "m2p" if halo else "m2s"
            ntile_m2 = 20 if halo else 16
            m2v = m2[:, :].rearrange("p (t r) -> p t r", t=ntile_m2)
            for G in range(2):
                wj = load_w([16 + 2 * G, 17 + 2 * G, 20 + 2 * G, 21 + 2 * G])
                if G == 0:
                    for hh in range(4):
                        zr = slice(64, 128) if hh % 2 == 0 else slice(0, 64)
                        tr.op("dve", lambda hh=hh, zr=zr: dve.memset(QTz[zr, hh, :].bitcast(F32), 0.0), w=NQ)
                for i in range(2):
                    for c in range(4):
                        proj_fm(wj, 2 + i, hT, "hT%d" % c, c * 512, 512, KTn[:, i, c * 512:(c + 1) * 512], NKK, 1.0)
                for i in range(2):
                    for c in range(4):
                        proj_fm(wj, i, hT, "hT%d" % c, c * 512, 512,
                                (QTz[0:64, 2 * i, c * 512:(c + 1) * 512], QTz[64:128, 2 * i + 1, c * 512:(c + 1) * 512]), NQ, 0.125)
                for hh in range(4):
                    xi = state["xi"] % 2
                    state["xi"] += 1
                    tr.dma("sp", xs[xi][:], natab_d[4 * G + hh], w=["xs%d" % xi], stream="xs%d" % xi)
                    tr.op("dve", lambda xi=xi, hh=hh: dve.tensor_copy(out=natab[:, hh, :], in_=xs[xi][:]),
                          r=["xs%d" % xi], w=["natab"])
                    xi = state["xi"] % 2
                    state["xi"] += 1
                    tr.dma("sp", xs[xi][:], natabi_d[4 * G + hh], w=["xs%d" % xi], stream="xs%d" % xi)
                    tr.op("dve", lambda xi=xi, hh=hh: dve.tensor_copy(out=natabi[:, hh, :], in_=xs[xi][:]),
                          r=["xs%d" % xi], w=NIK)
                wj2 = load_w([24 + 2 * G, 25 + 2 * G])
                assert {28 + 2 * G, 29 + 2 * G} <= ready
                tr.dma("sp", wz[:, 0], ws_in[28 + 2 * G], r=["WSin%d" % (28 + 2 * G)], w=["wz"], stream="wz")
                tr.dma("sp", wz[:, 1], ws_in[29 + 2 * G], r=["WSin%d" % (29 + 2 * G)], w=["wz"], stream="wz")
                tr.op("dve", lambda: dve.memset(Vn[:, :, :, 64:65], 1.0), w=["A.v"])
                for t in range(16):
                    proj_tm(wj2, 0, 2, hT, "hT%d" % (t // 4), t * 128, Vn[:, t, :, 0:64], "A.v", 4, 64)
                if halo:
                    npipe = NormPipe()
                    for t, ot in enumerate([0, 1, 14, 15]):
                        npipe.add(oth0 + ot * 128, hTo, "hTo", t * 128)
                    npipe.run()
                    for i in range(2):
                        proj_fm(wj, 2 + i, hTo, "hTo", 0, 512, KTn[:, i, 2048:2560], NKK, 1.0)
                    for t in range(4):
                        proj_tm(wj2, 0, 2, hTo, "hTo", t * 128, Vn[:, 16 + t, :, 0:64], "A.v", 4, 64)

                first, last = {}, {}
                for (t, a) in tiles:
                    r_lo, r_hi = cand[t]
                    for qt in range(r_lo // 2, r_hi // 2 + 1):
                        first.setdefault(qt, t)
                        last[qt] = t
                nsteps = [(hh, t, a) for hh in range(4) for (t, a) in tiles]

                def na_qk(n):
                    hh, t, a = nsteps[n]
                    i, b0 = hh // 2, (hh % 2) * 64
                    r_lo, r_hi = cand[t]
                    nq = (r_hi - r_lo + 1) * 64
                    s = n % 2
                    s0 = r_lo - a + 7
                    chunks = [(c0, c1) for (c0, c1) in ((0, min(512, nq)), (512, nq)) if c1 > c0]
                    for ci, (c0, c1) in enumerate(chunks):
                        bk = [bankkeys[2 * s + ci]]
                        bnk = 2 * s + ci
                        ops = [("qk", c0, c1)]
                        ra, rb = r_lo + c0 // 64, r_lo + c1 // 64
                        if t < 16:
                            segs = [(ra, min(rb, 4), False), (max(ra, 4), min(rb, 29), True), (max(ra, 29), rb, False)]
                        else:
                            segs = [(ra, rb, False)]
                        for (x0, x1, interior) in segs:
                            if x1 <= x0:
                                continue
                            ops.append(("tabi" if interior else "tab", x0, x1))
                            if not interior:
                                ops.append(("mask", x0, x1))
                        for oi, (kind, x0, x1) in enumerate(ops):
                            lastop = (oi == len(ops) - 1)
                            sg = lastop and (ci == len(chunks) - 1)
                            if kind == "qk":
                                o = PS[:, bnk, 0:c1 - c0]
                                tr.op("pe", lambda o=o: pe.matmul(
                                    o, lhsT=KTn[:, i, t * 128:(t + 1) * 128],
                                    rhs=QTz[:, hh, r_lo * 64 + c0:r_lo * 64 + c1], start=True, stop=False),
                                    r=NKK + NQ, w=bk, sig=False)
                                continue
                            o = PS[:, bnk, (x0 - ra) * 64:(x1 - ra) * 64]
                            sa, sb_ = (x0 - a + 7) * 64, (x1 - a + 7) * 64
                            if kind == "tab":
                                tr.op("pe", lambda o=o, sa=sa, sb_=sb_, lastop=lastop: pe.matmul(
                                    o, lhsT=ident[:], rhs=natab[:, hh, sa:sb_], start=False, stop=lastop, skip_group_check=True),
                                    r=["ident", "natab"], w=bk, sig=sg)
                            elif kind == "tabi":
                                tr.op("pe", lambda o=o, sa=sa, sb_=sb_, lastop=lastop: pe.matmul(
                                    o, lhsT=ident[:], rhs=natabi[:, hh, sa:sb_], start=False, stop=lastop, skip_group_check=True),
                                    r=["ident"] + NIK, w=bk, sig=sg)
                            else:
                                tr.op("pe", lambda o=o, x0=x0, x1=x1, lastop=lastop: pe.matmul(
                                    o, lhsT=e2[:, :], rhs=m2v[:, t, x0:x1].unsqueeze(2).to_broadcast([128, x1 - x0, 64]),
                                    start=False, stop=lastop, skip_group_check=True), r=["e2", m2k], w=bk, sig=sg)

                gfirst, glast = {}, {}
                for (t, a) in tiles:
                    r_lo, r_hi = cand[t]
                    for qt in range(r_lo // 2, r_hi // 2 + 1):
                        gfirst.setdefault(qt // 4, t)
                        glast[qt // 4] = t
                gbank = {}

                def na_exp_pv(n):
                    hh, t, a = nsteps[n]
                    r_lo, r_hi = cand[t]
                    nq = (r_hi - r_lo + 1) * 64
                    s = n % 2
                    sk = [bankkeys[2 * s], bankkeys[2 * s + 1]] if nq > 512 else [bankkeys[2 * s]]
                    pi = state["pt"] % 3
                    state["pt"] += 1
                    pk = "PT%d" % pi
                    tr.op("act", lambda: act.activation(out=PT[pi][:, 0:nq], in_=psflat(2 * s, 2)[:, 0:nq], func=AF.Exp),
                          r=sk, w=[pk])
                    if n + 2 < len(nsteps):
                        na_qk(n + 2)
                    qts = list(range(r_lo // 2, r_hi // 2 + 1))
                    for qn, qt in enumerate(qts):
                        gb = qt // 4
                        fresh = False
                        if (hh, gb) not in gbank:
                            gbank[(hh, gb)] = 4 + state["sb"] % 3
                            state["sb"] += 1
                            fresh = True
                        bnk = gbank[(hh, gb)]
                        slot = qt % 4
                        dst = PS[:, bnk, slot * 65:slot * 65 + 65]
                        q0 = (qt - r_lo // 2) * 128
                        lastmm = (glast[gb] == t) and (qn == len(qts) - 1 or qts[qn + 1] // 4 != gb)
                        tr.op("pe", lambda dst=dst, q0=q0, fresh=fresh: pe.matmul(
                            dst, lhsT=PT[pi][:, q0:q0 + 128], rhs=Vn[:, t, hh, :], start=fresh, stop=(last[qt] == t),
                            skip_group_check=True), r=[pk, "A.v"], w=[bankkeys[bnk]], sig=lastmm)
                        if lastmm:
                            si = state["st"] % 4
                            state["st"] += 1
                            gv = PS[:, bnk, 0:260].rearrange("p (q e) -> p q e", q=4)
                            tr.op("dve", lambda gv=gv, si=si: dve.reciprocal(out=stat[si][:, 0:4], in_=gv[:, :, 64]),
                                  r=[bankkeys[bnk]], w=["stat%d" % si])
                            tr.op("dve", lambda gv=gv, si=si, gb=gb: dve.tensor_tensor(
                                out=untok[:, 4 * gb:4 * gb + 4, hh * 64:(hh + 1) * 64], in0=gv[:, :, 0:64],
                                in1=stat[si][:, 0:4].unsqueeze(2).to_broadcast([128, 4, 64]), op=ALU.mult),
                                r=[bankkeys[bnk], "stat%d" % si], w=["untok"])

                run_late(99)
                if G == 0:
                    prefetch_w([18, 19, 22, 23], [26, 27])
                else:
                    prefetch_w([32, 40], [33, 41])
                na_qk(0)
                if len(nsteps) > 1:
                    na_qk(1)
                for n in range(len(nsteps)):
                    na_exp_pv(n)
                for i in range(2):
                    for qc in range(4):
                        b = (7, 3)[qc % 2]
                        tb_ = (6, 2)[qc % 2]
                        for k in range(8):
                            tr.op("pe", lambda k=k: pe.matmul(PS[:, b, :], lhsT=wz[:, i, k, :], rhs=hT[:, k, qc * 512:(qc + 1) * 512],
                                                              start=(k == 0), stop=(k == 7)), r=["wz", "hT%d" % qc], w=[bankkeys[b]],
                                  sig=(k == 7))
                        silu_from_bank(b, sz[:])
                        tp = psbf(tb_)
                        for j in range(4):
                            tr.op("pe", lambda j=j: pe.transpose(out=tp[:, j * 128:(j + 1) * 128],
                                                                 in_=untok[:, qc * 4 + j, i * 128:(i + 1) * 128], identity=ident[:]),
                                  r=["untok", "ident"], w=[bankkeys[tb_]], sig=(j == 3))
                        tr.op("dve", lambda: dve.tensor_tensor(out=uT[:, 4 + 2 * G + i, qc * 512:(qc + 1) * 512], in0=tp[:, 0:512],
                                                               in1=sz[:], op=ALU.mult), r=[bankkeys[tb_], "sz"], w=["uT"])

            if stage < 4:
                return
            run_late(99)
            tr.dma("sp", Wod_sb, ws_od.rearrange("b p f c -> p b f c"), r=["WSod0", "WSod1"], w=AK, stream="A")
            tr.dma("sp", Won_sb, ws_on.rearrange("b p f c -> p b f c"), r=["WSon0", "WSon1"], w=AK, stream="A")
            for b in range(0, 8, 4):
                tr.dma("sp", Wout_sb[:, b:b + 4], ws_out[b:b + 4].rearrange("b p k c -> p b k c"),
                       r=["WSout%d" % (b // 2), "WSout%d" % (b // 2 + 1)], w=["A.out"], stream="Aout")
            mTb = [hTo, t5all[:, 0:4096].rearrange("p (d t) -> p d t", d=8)]
            mkeys = [["hTo"], NIK]
            pend_tiles = []
            pre = []
            ystore = []
            ypost = []
            if (32, 40) in wcache and (33, 41) in wcache:
                pre = [wcache.pop((32, 40)), wcache.pop((33, 41))]
            nxt = None
            if next_own0 is not None:
                nflat = natab[:].rearrange("p a b -> p (a b)").bitcast(F32)
                uflat2 = untok[:].rearrange("p a b -> p (a b)").bitcast(F32)
                ring = [(nflat[:, 0:1024], ["natabA"], "xe0"), (nflat[:, 1024:2048], ["natabB"], "xe1"),
                        (uflat2[:, 0:1024], ["untokA"], "xe2"), (uflat2[:, 1024:2048], ["untokB"], "xe3")]
                tr.op("dve", lambda: dve.memset(sm[:, 10:11], 0.0), w=["natab", "untok", "natabA", "natabB", "untokA", "untokB"])
                nxt = NormPipe(ring)
                for t in range(16):
                    nxt.add(next_own0 + t * 128, hT, "hT%d" % (t // 4), t * 128)
            for c in range(4):
                cs = slice(c * 512, (c + 1) * 512)
                mT, mk = mTb[c % 2], mkeys[c % 2]
                if nxt is not None:
                    nxt.prefetch(4 * (c + 1))
                for dc in range(8):
                    wj = pre.pop(0) if pre else load_w([32 + dc, 40 + dc])
                    if not pre:
                        if dc < 7:
                            pre.append(load_w([33 + dc, 41 + dc]))
                        elif c < 3:
                            pre.append(load_w([32, 40]))
                    ba, bn, bga, bgb = [4 * (dc % 2) + x_ for x_ in range(4)]
                    ka, kn_, kga, kgb = bankkeys[ba], bankkeys[bn], bankkeys[bga], bankkeys[bgb]
                    for n, bb in ((0, bga), (1, bgb)):
                        for k in range(8):
                            tr.op("pe", lambda k=k, n=n, bb=bb: pe.matmul(PS[:, bb, :], lhsT=wst[wj][:, n, k, :], rhs=hT[:, k, cs],
                                                                          start=(k == 0), stop=(k == 7)),
                                  r=["wst%d" % wj, "hT%d" % c], w=[bankkeys[bb]], sig=(k == 7))
                    for f in range(4):
                        tr.op("pe", lambda f=f: pe.matmul(PS[:, ba, :], lhsT=Wod_sb[:, dc, f, :], rhs=uT[:, f, cs], start=(f == 0),
                                                          stop=(f == 3)), r=AK + ["uT"], w=[ka], sig=(f == 3))
                    for f in range(4):
                        tr.op("pe", lambda f=f: pe.matmul(PS[:, bn, :], lhsT=Won_sb[:, dc, f, :], rhs=uT[:, 4 + f, cs], start=(f == 0),
                                                          stop=(f == 3)), r=AK + ["uT"], w=[kn_], sig=(f == 3))
                    ti = dc % 2
                    tr.op("act", lambda: act.activation(out=th[ti], in_=PS[:, bga, :], func=AF.Tanh, scale=0.5), r=[kga],
                          w=["th%d" % ti])
                    tr.op("act", lambda: act.activation(out=thb[ti], in_=PS[:, bgb, :], func=AF.Tanh, scale=0.5), r=[kgb],
                          w=["thb%d" % ti])
                    tr.op("dve", lambda: dve.scalar_tensor_tensor(out=t1, in0=th[ti], scalar=1.0, in1=PS[:, ba, :], op0=ALU.add,
                                                                  op1=ALU.mult), r=["th%d" % ti, ka], w=["ob"])
                    tr.op("dve", lambda: dve.scalar_tensor_tensor(out=t2, in0=thb[ti], scalar=1.0, in1=PS[:, bn, :], op0=ALU.add,
                                                                  op1=ALU.mult), r=["thb%d" % ti, kn_], w=["ob2"])
                    tr.op("dve", lambda: dve.tensor_tensor(out=mT[:, dc, :], in0=t1, in1=t2, op=ALU.add), r=["ob", "ob2"],
                          w=mk)
                    if pend_tiles and dc % 2 == 0:
                        pend_tiles.pop(0)()
                if c == 3 and next_own0 is not None:
                    prefetch_w([0, 1, 4, 5], [8, 9])
                if nxt is not None:
                    nxt.run(upto=4 * (c + 1), ahead=0)
                def out_tile(c, t, mT, mk, pb, defer):
                    row = c * 512 + t * 128
                    i = state["yo"] % 3
                    state["yo"] += 1
                    xk = "xs%d" % i
                    tr.dma("sp", xs[i][:], x_all[own0 + row:own0 + row + 128, :], w=[xk], stream=xk)
                    for half in range(2):
                        bb = pb + half
                        for dc in range(8):
                            tr.op("pe", lambda dc=dc, half=half, bb=bb: pe.matmul(
                                PS[:, bb, :], lhsT=mT[:, dc, t * 128:(t + 1) * 128], rhs=Wout_sb[:, 4 * half:4 * half + 4, dc, :],
                                start=(dc == 0), stop=(dc == 7)), r=mk + ["A.out"] + AK, w=[bankkeys[bb]], sig=(dc == 7))
                    si = state["st"] % 4
                    state["st"] += 1
                    sk = "stat%d" % si
                    pk2 = [bankkeys[pb], bankkeys[pb + 1]]
                    tr.op("act", lambda si=si, pb=pb: act.activation(out=hb[0][:], in_=psflat(pb, 2), func=AF.Square,
                                                                     accum_out=stat[si][:, 0:1]), r=pk2, w=["hb0", sk])
                    if ystore:
                        ystore.pop(0)()
                    if ypost:
                        ypost.pop(0)()
                    tr.op("dve", lambda si=si: dve.tensor_scalar(out=stat[si][:, 1:2], in0=stat[si][:, 0:1], scalar1=1.0 / 1024,
                                                                 scalar2=1e-6, op0=ALU.mult, op1=ALU.add), r=[sk], w=[sk])
                    tr.op("pool", lambda si=si: pool.tensor_tensor(out=stat[si][:, 2:3], in0=stat[si][:, 1:2], in1=mhalf[:, 0:1],
                                                                   op=ALU.pow), r=[sk, "mhalf"], w=[sk])

                    def post2(si=si, pb=pb, i=i, xk=xk, row=row, sk=sk, pk2=pk2):
                        tr.op("dve", lambda: dve.scalar_tensor_tensor(out=ytmp, in0=psflat(pb, 2), scalar=stat[si][:, 2:3],
                                                                      in1=postw[:], op0=ALU.mult, op1=ALU.mult),
                              r=pk2 + [sk, "postw"], w=["accs"])
                        tr.op("dve", lambda: dve.tensor_tensor(out=xs[i][:], in0=ytmp, in1=xs[i][:], op=ALU.add), r=["accs", xk],
                              w=[xk])
                        ystore.append(lambda: tr.dma("act", y_all[out0 + row:out0 + row + 128, :], xs[i][:],
                                                     r=[xk], stream="st%d" % i))
                    ypost.append(post2)
                    if not defer:
                        while ypost:
                            ypost.pop(0)()

                if c < 3:
                    for t in range(4):
                        pend_tiles.append(lambda c=c, t=t, mT=mT, mk=mk: out_tile(c, t, mT, mk, 4, False))
                else:
                    for t in range(4):
                        out_tile(c, t, mT, mk, 2 * (t % 4), True)
                    while ypost:
                        ypost.pop(0)()
            while ystore:
                ystore.pop(0)()
            if nxt is not None:
                nxt.run()
                tr.op("dve", lambda: dve.memset(sm[:, 11:12], 0.0), w=["natabA", "natabB", "untokA", "untokB", "natab", "untok"])

        def da_group(g, nkt, halo):
            ACC = lambda a: PS[:, 4 + a // 3, (a % 3) * 129:(a % 3) * 129 + 129]
            steps = [(i, qc, kt) for i in range(2) for qc in range(4) for kt in range(nkt)]

            def bias_of(i, qc, kt):
                h = 2 * g + i
                if kt < 16:
                    d = kt * 128 - qc * 512
                    if d < -128:
                        return None, 3 * h + 0
                    if d > 512:
                        return None, 3 * h + 1
                    off = 512 - d
                    return t5tab[:, i, off:off + 512], 12
                if kt == 16 and qc == 3:
                    return t5sp[:, 0, i, :], 12
                if kt == 31 and qc == 0:
                    return t5sp[:, 1, i, :], 12
                return None, 3 * h + 2

            def qk(n):
                i, qc, kt = steps[n]
                s = n % 2
                tab, _ = bias_of(i, qc, kt)
                for m in range(2):
                    o = PS[:, 2 * s + m, :]
                    tr.op("pe", lambda o=o, m=m: pe.matmul(o, lhsT=KTa[64 * m:64 * m + 64, i, kt * 128:(kt + 1) * 128],
                                                           rhs=QTa[64 * m:64 * m + 64, i, qc * 512:(qc + 1) * 512], start=True,
                                                           stop=(tab is None)),
                          r=["A.k%d" % i, "A.q%d" % i], w=[bankkeys[2 * s + m]], sig=(tab is None and m == 1))
                if tab is not None:
                    for m in range(2):
                        o = PS[:, 2 * s + m, :]
                        tr.op("pe", lambda o=o: pe.matmul(o, lhsT=ident[:], rhs=tab, start=False, stop=True),
                              r=["ident", "t5tab", "t5sp"], w=[bankkeys[2 * s + m]], sig=(m == 1))

            EPI_LAG, BG_EVERY = 10, 12

            def zproj_mm(i, qc, k):
                tr.op("pe", lambda: pe.matmul(PS[:, 7, :], lhsT=wz[:, i, k, :], rhs=hT[:, k, qc * 512:(qc + 1) * 512],
                                              start=(k == 0), stop=(k == 7)), r=["wz", "hT%d" % qc], w=["B7"], sig=(k == 7))
                if k == 7:
                    silu_from_bank(7, sz[:])

            zlast = min(EPI_LAG + 8, nkt - 3)
            zsteps = list(range(EPI_LAG, zlast + 1))
            zplan = {kt_: [] for kt_ in zsteps}
            for k in range(8):
                zplan[zsteps[k * len(zsteps) // 8]].append(k)

            def epilogue_dve():
                for bnk, n in ((4, 3), (5, 3), (6, 2)):
                    a0 = (bnk - 4) * 3
                    tr.op("dve", lambda bnk=bnk, n=n, a0=a0: dve.tensor_copy(
                        out=accs[:, a0:a0 + n, :], in_=PS[:, bnk, 0:n * 129].rearrange("p (a e) -> p a e", a=n)),
                        r=[bankkeys[bnk]], w=["accs"])
                tr.op("dve", lambda: dve.reciprocal(out=rs[:], in_=accs[:, :, 128]), r=["accs"], w=["rs"])
                tr.op("dve", lambda: dve.tensor_scalar(out=rs[:, 4:8], in0=rs[:, 4:8], scalar1=nlam, scalar2=None, op0=ALU.mult),
                      r=["rs", "nlam"], w=["rs"])
                tr.op("dve", lambda: dve.tensor_tensor(out=ob[:], in0=accs[:, 0:4, 0:128],
                                                       in1=rs[:, 0:4].unsqueeze(2).to_broadcast([128, 4, 128]), op=ALU.mult),
                      r=["accs", "rs"], w=["ob"])
                tr.op("dve", lambda: dve.tensor_tensor(out=ob2[:], in0=accs[:, 4:8, 0:128],
                                                       in1=rs[:, 4:8].unsqueeze(2).to_broadcast([128, 4, 128]), op=ALU.mult),
                      r=["accs", "rs"], w=["ob2"])
                tr.op("dve", lambda: dve.tensor_tensor(out=ob[:], in0=ob[:], in1=ob2[:], op=ALU.add), r=["ob", "ob2"], w=["ob"])
                tr.op("dve", lambda: dve.tensor_tensor(out=ob2[:], in0=ob[:], in1=ob[:], op=ALU.mult), r=["ob"], w=["ob2"])
                tr.op("dve", lambda: dve.tensor_reduce(out=ss4[:], in_=ob2[:], axis=AX.X, op=ALU.add), r=["ob2"], w=["ss4"])
                tr.op("dve", lambda: dve.tensor_scalar(out=ss4[:], in0=ss4[:], scalar1=1.0 / 128, scalar2=1e-5, op0=ALU.mult,
                                                       op1=ALU.add), r=["ss4"], w=["ss4"])
                tr.op("pool", lambda: pool.tensor_tensor(out=rs4[:], in0=ss4[:], in1=mhalf[:, 0:4], op=ALU.pow), r=["ss4", "mhalf"],
                      w=["rs4"])
                tr.op("dve", lambda: dve.tensor_tensor(out=onb[:], in0=ob[:], in1=rs4[:].unsqueeze(2).to_broadcast([128, 4, 128]),
                                                       op=ALU.mult), r=["ob", "rs4"], w=["onb"])

            def epilogue_pe(i, qc):
                h = 2 * g + i
                tp = psbf(7)
                for j in range(4):
                    tr.op("pe", lambda j=j: pe.transpose(out=tp[:, j * 128:(j + 1) * 128], in_=onb[:, j, :], identity=ident[:]),
                          r=["onb", "ident", "sz"], w=["B7"], sig=(j == 3))
                tr.op("dve", lambda: dve.tensor_tensor(out=uT[:, h, qc * 512:(qc + 1) * 512], in0=tp[:, 0:512], in1=sz[:],
                                                       op=ALU.mult), r=["B7", "sz"], w=["uT"])

            deferred = []
            qk(0)
            if len(steps) > 1:
                qk(1)
            for n, (i, qc, kt) in enumerate(steps):
                s = n % 2
                _, bcol = bias_of(i, qc, kt)
                pi = state["pt"] % 3
                state["pt"] += 1
                pk = "PT%d" % pi
                tr.op("act", lambda s=s, pi=pi, bcol=bcol: act.activation(out=PT[pi][:], in_=psflat(2 * s, 2), func=AF.Exp,
                                                                         bias=cst[:, bcol:bcol + 1], scale=1.0),
                      r=[bankkeys[2 * s], bankkeys[2 * s + 1], "cst"], w=[pk])
                if n + 2 < len(steps):
                    qk(n + 2)
                for m in range(2):
                    for j in range(4):
                        a = m * 4 + j
                        tr.op("pe", lambda a=a, m=m, j=j, pi=pi: pe.matmul(
                            ACC(a), lhsT=PT[pi][:, m * 512 + j * 128:m * 512 + (j + 1) * 128], rhs=Va[:, kt, i, :],
                            start=(kt == 0 and a % 3 == 0), stop=(kt == nkt - 1), skip_group_check=True),
                            r=[pk, "A.v"], w=[bankkeys[4 + a // 3]], sig=(kt == nkt - 1 and a in (2, 5, 7)))
                if deferred and deferred[0][0] <= n:
                    deferred.pop(0)[1]()
                for k in zplan.get(kt, ()):
                    zproj_mm(i, qc, k)
                if n % BG_EVERY == 3 and kt not in (nkt - 1, 0):
                    bg.tick()
                    if bg_state.get("retry"):
                        prefetch_w(*bg_state["retry"])
                if kt == nkt - 1:
                    epilogue_dve()
                    deferred.append((n + EPI_LAG, lambda i=i, qc=qc: epilogue_pe(i, qc)))
            while deferred:
                late.append(deferred.pop(0)[1])
            bg.drain()

        pre_a0 = NormPipe()
        if stage >= 1:
            for t in range(16):
                pre_a0.add(t * 128, hT, "hT%d" % (t // 4), t * 128)
        EARLY = [0, 4, 8, 12]
        for n_, b0 in enumerate(EARLY):
            cast_super(w_in3[:, :, b0 * 128:b0 * 128 + 256], 8, 2, ws_in[b0:b0 + 2], "pre",
                       keys=["WSin%d" % b0, "WSin%d" % (b0 + 1)])
            ready.update((b0, b0 + 1))
            pre_a0.run(upto=min(4 * (n_ + 1), len(pre_a0.t)))
        tr.op("dve", lambda: dve.memset(sm[:, 8:9], 0.0), w=["F0", "F1", "O0", "O1", "uT"] + AK)

        class BgCast:
            def __init__(self):
                self.pieces = []
                for b0 in [2, 6, 10, 14] + list(range(16, 48, 2)):
                    self.pieces.append((w_in3[:, :, b0 * 128:b0 * 128 + 256], 8, 2, ws_in[b0:b0 + 2], "pre",
                                        ["WSin%d" % b0, "WSin%d" % (b0 + 1)]))
                for hf in range(2):
                    self.pieces.append((w_od3[:, :, hf * 512:(hf + 1) * 512], 4, 4, ws_od[4 * hf:4 * hf + 4], "sub", ["WSod%d" % hf]))
                for hf in range(2):
                    self.pieces.append((w_on3[:, :, hf * 512:(hf + 1) * 512], 4, 4, ws_on[4 * hf:4 * hf + 4], "plain", ["WSon%d" % hf]))
                for q4 in range(4):
                    self.pieces.append((w_out3[:, :, q4 * 256:(q4 + 1) * 256], 8, 2, ws_out[2 * q4:2 * q4 + 2], "half", ["WSout%d" % q4]))
                self.F = [(hTo[:].rearrange("p a b -> p (a b)").bitcast(F32), ["hTo"]),
                          (natab[:].rearrange("p a b -> p (a b)").bitcast(F32), ["natab"])]
                uflat = untok[:].rearrange("p a b -> p (a b)")
                self.O = [(uflat[:, 0:2048], ["O0"]), (uflat[:, 2048:4096], ["O1"])]
                self.k = self.kx = self.ks = 0
                self.done = False

            def _L(self, k):
                src3, nk, nb, dst4, mode, keys = self.pieces[k]
                fb, fk = self.F[k % 2]
                n = nk * nb * 128
                tr.dma("sp", fb[:, 0:n].rearrange("p (k c) -> p k c", k=nk), src3, w=fk, stream="Fbg%d" % (k % 2))

            def _X(self, k):
                src3, nk, nb, dst4, mode, keys = self.pieces[k]
                fb, fk = self.F[k % 2]
                ob_, ok = self.O[k % 2]
                n = nk * nb * 128
                f4 = fb[:, 0:n].rearrange("p (k b c) -> p k b c", k=nk, b=nb)
                o4t = ob_[:, 0:n].rearrange("p (b k c) -> p b k c", b=nb, k=nk).rearrange("p b k c -> p k b c")
                if mode == "pre":
                    tr.op("dve", lambda: dve.tensor_tensor(out=o4t, in0=f4,
                                                           in1=prew[:, :].unsqueeze(2).unsqueeze(3).to_broadcast([128, nk, nb, 128]),
                                                           op=ALU.mult), r=fk + ["prew"], w=ok)
                elif mode == "sub":
                    tr.op("dve", lambda: dve.tensor_scalar(out=o4t, in0=f4, scalar1=subw[:, 0:1], scalar2=0.8, op0=ALU.mult,
                                                           op1=ALU.mult), r=fk + ["subw"], w=ok)
                elif mode == "half":
                    tr.op("dve", lambda: dve.tensor_scalar(out=o4t, in0=f4, scalar1=0.5, scalar2=None, op0=ALU.mult), r=fk, w=ok)
                else:
                    tr.op("dve", lambda: dve.tensor_copy(out=o4t, in_=f4), r=fk, w=ok)

            def _S(self, k):
                src3, nk, nb, dst4, mode, keys = self.pieces[k]
                ob_, ok = self.O[k % 2]
                n = nk * nb * 128
                o4 = ob_[:, 0:n].rearrange("p (b k c) -> p b k c", b=nb, k=nk)
                tr.dma("sp", dst4.rearrange("b p k c -> p b k c"), o4, r=ok, w=keys, stream="ws%d" % (k % 2))
                for kk in keys:
                    if kk.startswith("WSin"):
                        ready.add(int(kk[4:]))

            def tick(self, load=True):
                if self.done:
                    return
                n = len(self.pieces)
                if self.ks < self.kx:
                    self._S(self.ks)
                    self.ks += 1
                if self.kx < self.k:
                    self._X(self.kx)
                    self.kx += 1
                if load and self.k < n:
                    self._L(self.k)
                    self.k += 1
                if self.ks >= n:
                    self.done = True
                    bg_state["done"] = True
                    tr.op("dve", lambda: dve.memset(sm[:, 9:10], 0.0), w=["O0", "O1", "untok"])

            def drain(self):
                while not self.done and self.ks < self.k:
                    self.tick(load=False)

            def flush(self):
                while not self.done:
                    self.tick()

        bg = BgCast()

        tr.op("dve", lambda: dve.memset(cst[:, 12:13], 0.0), r=["cst"], w=["cst"])
        if stage >= 1:
            unit(0, 0, 6144, 0, pre_a=pre_a0, next_own0=(2048 if stage >= 5 else None))
        if stage >= 5:
            unit(1, 2048, None, 2048, pre_a=NormPipe(), next_own0=(4096 if stage >= 6 else None))
        if stage >= 6:
            unit(2, 4096, None, 4096, pre_a=NormPipe())
        if dbg is not None:
            dbg(nc, tr, locals())
        tr.finish([k for k in ("st0", "st1", "st2", "dbg") if k in tr.dsem])
    return nc


def _tables(t5_rel_bias, na_rpb, parity):
    tb = np.asarray(t5_rel_bias, np.float32)
    i = np.arange(128)[:, None]
    c = np.arange(1152)[None, :]
    t5own = np.ascontiguousarray(np.moveaxis(tb[t5_bucket_np(i - c + 512)], -1, 0))
    def pos_q(qp):
        return qp + 2048 * parity
    def pos_k_other(kp):
        return kp if parity == 0 else kp - 2048
    spec = []
    for (kt, qc) in ((16, 3), (31, 0)):
        kp = kt * 128 + np.arange(128)[:, None]
        qp = qc * 512 + np.arange(512)[None, :]
        rel = pos_k_other(kp) - pos_q(qp)
        spec.append(np.moveaxis(tb[t5_bucket_np(rel)], -1, 0))
    t5spec = np.ascontiguousarray(np.stack(spec, 0))
    far_neg = tb[t5_bucket_np(np.array(-1000))]
    far_pos = tb[t5_bucket_np(np.array(1000))]
    oth = far_pos if parity == 0 else far_neg
    t5c = np.zeros((1, 16), np.float32)
    for h in range(4):
        t5c[0, 3 * h + 0] = far_neg[h]
        t5c[0, 3 * h + 1] = far_pos[h]
        t5c[0, 3 * h + 2] = oth[h]
    rpb = np.asarray(na_rpb, np.float32)
    j = np.arange(2)[:, None, None, None]
    kc = np.arange(64)[None, :, None, None]
    s = np.arange(16)[None, None, :, None]
    cc = np.arange(64)[None, None, None, :]
    dr = j + 14 - s + 0 * kc + 0 * cc
    cs0 = np.clip(cc - 8, 0, 48)
    dcv = kc - cc + 15 + 0 * j + 0 * s
    valid = (dr >= 0) & (dr <= 14) & (kc >= cs0) & (kc < cs0 + 16) & (dcv >= 0) & (dcv <= 30)
    drc, dcc = np.clip(dr, 0, 14), np.clip(dcv, 0, 30)
    natab = np.empty((8, 128, 1024), np.float32)
    natabi = np.empty((8, 128, 1024), np.float32)
    valid_i = valid & (dr >= 3) & (dr <= 10)
    for h in range(8):
        g = rpb[h][drc, dcc]
        natab[h] = np.where(valid, g, np.float32(NEG)).reshape(128, 1024)
        natabi[h] = np.where(valid_i, g, np.float32(NEG)).reshape(128, 1024)
    return t5own, t5spec, t5c, natab, natabi


def _mask(halo, parity):
    ntile = 20 if halo else 16
    m = np.full((2, ntile, 32), NEG, np.float32)
    if halo:
        rows_abs, base = 64, 32 * parity
        oth_base = 32 * (1 - parity)
    else:
        rows_abs, base, oth_base = 32, 0, 0
    for t in range(ntile):
        for j in range(2):
            if t < 16:
                ka = base + 2 * t + j
            elif t < 18:
                ka = oth_base + 2 * (t - 16) + j
            else:
                ka = oth_base + 28 + 2 * (t - 18) + j
            for r in range(32):
                ra = base + r
                st = min(max(ra - 4, 0), rows_abs - 8)
                if st <= ka < st + 8:
                    m[j, t, r] = 0.0
    mp = np.zeros((128, ntile * 32), np.float32)
    mp[0:2] = m.reshape(2, ntile * 32)
    return mp


_PROGRAM = None
_HOOK = None


def kernel(x_prompt, x_sample, t5_rel_bias, pre_norm_w, post_norm_w, w_in, lambda_q1, lambda_k1, lambda_q2,
           lambda_k2, subln_w, na_rpb, w_o_diff, w_o_na, w_out):
    global _PROGRAM
    f = lambda a: np.ascontiguousarray(np.asarray(a, np.float32))
    x_prompt, x_sample = f(x_prompt), f(x_sample)
    w_in0, w_od0, w_on0, w_out0 = f(w_in)[0], f(w_o_diff)[0], f(w_o_na)[0], f(w_out)[0]
    prew = np.ascontiguousarray(f(pre_norm_w)[0].reshape(8, 128).T)
    postw = f(post_norm_w)[0].reshape(1, 1024)
    subw = f(subln_w)[0].reshape(128, 1)
    lamv = np.concatenate([f(lambda_q1)[0], f(lambda_k1)[0], f(lambda_q2)[0], f(lambda_k2)[0]]).reshape(1, 256)
    ident = np.eye(128, dtype=np.float32)
    e2 = np.zeros((128, 128), np.float32)
    e2[0, 0:64] = 1.0
    e2[1, 64:128] = 1.0
    m2s = _mask(False, 0)
    in_maps = []
    for c in range(NCORES):
        p, par = c // 2, c % 2
        own = x_prompt[p, par * 2048:(par + 1) * 2048]
        oth = x_prompt[p, (1 - par) * 2048:(2 - par) * 2048]
        x_all = np.ascontiguousarray(np.concatenate([own, x_sample[2 * c], x_sample[2 * c + 1], oth], 0))
        t5own, t5spec, t5c, natab, natabi = _tables(f(t5_rel_bias), f(na_rpb)[0], par)
        in_maps.append({
            "x_all": x_all, "w_in": w_in0, "w_od": w_od0, "w_on": w_on0, "w_out": w_out0, "prew": prew, "postw": postw,
            "subw": subw, "lamv": lamv, "ident": ident, "e2": e2, "t5own": t5own, "t5spec": t5spec, "t5c": t5c,
            "natab": natab, "natabi": natabi, "m2p": _mask(True, par), "m2s": m2s,
        })
    if _PROGRAM is None:
        _PROGRAM = build_program()
    if _HOOK is not None:
        return _HOOK(in_maps)
    res = run_bass_kernel_spmd(_PROGRAM, in_maps, core_ids=list(range(NCORES)))
    y_prompt = np.empty((4, 4096, 1024), np.float32)
    y_sample = np.empty((16, 2048, 1024), np.float32)
    for c in range(NCORES):
        y = res.results[c]["y_all"]
        p, par = c // 2, c % 2
        y_prompt[p, par * 2048:(par + 1) * 2048] = y[0:2048]
        y_sample[2 * c] = y[2048:4096]
        y_sample[2 * c + 1] = y[4096:6144]
    return (y_prompt, y_sample)
```
